# Optimizing a Trainium2 kernel written in Bass

```python
import math
import jax, jax.numpy as jnp
from jax import lax
import numpy as np

D_MODEL = 1024
BATCH = 8
SEQ = 8192
DEPTH = 2
DEC_BATCH = 8
DEC_SEQ = 4096
PAST_LEN = 128

GRID_W = 64
QBLK = 128
HEAD_DIM = 64
A_HEADS = 4
A_V_DIM = 2 * HEAD_DIM
B_Q_HEADS = 8
B_KV_HEADS = 2
B_GROUP = B_Q_HEADS // B_KV_HEADS
A_WIDTH = A_HEADS * A_V_DIM
B_WIDTH = B_Q_HEADS * HEAD_DIM
N_BRANCH = 2
D_FF = 4 * D_MODEL
N_BUCKETS = 32
MAX_DISTANCE = 128
ROPE_THETA = 10000.0
AXIS_DIM = HEAD_DIM // 2
EPS = 1e-6
SUBLN_EPS = 1e-5

QA_COLS = A_HEADS * 2 * HEAD_DIM
KA_COLS = A_HEADS * 2 * HEAD_DIM
VA_COLS = A_HEADS * A_V_DIM
QB_COLS = B_Q_HEADS * HEAD_DIM
KB_COLS = B_KV_HEADS * HEAD_DIM
VB_COLS = B_KV_HEADS * HEAD_DIM
GATE_COLS = N_BRANCH * D_MODEL
IN_COLS = QA_COLS + KA_COLS + VA_COLS + QB_COLS + KB_COLS + VB_COLS + GATE_COLS
SPLIT_1 = QA_COLS
SPLIT_2 = SPLIT_1 + KA_COLS
SPLIT_3 = SPLIT_2 + VA_COLS
SPLIT_4 = SPLIT_3 + QB_COLS
SPLIT_5 = SPLIT_4 + KB_COLS
SPLIT_6 = SPLIT_5 + VB_COLS

kernel_name = "hybrid_diffattn_gqa_axial_encoder"


def rmsnorm(x, g, eps=EPS):
    xf = x.astype(jnp.float32)
    y = xf * lax.rsqrt(jnp.mean(xf * xf, axis=-1, keepdims=True) + eps)
    return (y * g.astype(jnp.float32)).astype(x.dtype)


def t5_bucket(rel):
    nb = N_BUCKETS // 2
    max_exact = nb // 2
    ret = (rel > 0).astype(jnp.int32) * nb
    n = jnp.abs(rel)
    nf = jnp.maximum(n, 1).astype(jnp.float32)
    large = max_exact + (jnp.log(nf / max_exact) / math.log(MAX_DISTANCE / max_exact) * (nb - max_exact)).astype(jnp.int32)
    large = jnp.minimum(large, nb - 1)
    return ret + jnp.where(n < max_exact, n, large)


def axial_rope_tables(n):
    rows = n // GRID_W
    row = jnp.repeat(jnp.arange(rows, dtype=jnp.int32), GRID_W).astype(jnp.float32)
    col = jnp.tile(jnp.arange(GRID_W, dtype=jnp.int32), rows).astype(jnp.float32)
    inv = ROPE_THETA ** (-jnp.arange(0, AXIS_DIM, 2, dtype=jnp.float32) / AXIS_DIM)
    ang_r = row[:, None] * inv[None, :]
    ang_c = col[:, None] * inv[None, :]
    return (jnp.cos(ang_r), jnp.sin(ang_r), jnp.cos(ang_c), jnp.sin(ang_c))


def rope_rotate(x, cos, sin):
    half = x.shape[-1] // 2
    x1, x2 = x[..., :half], x[..., half:]
    return jnp.concatenate([x1 * cos - x2 * sin, x2 * cos + x1 * sin], axis=-1)


def axial_rope(x, tabs):
    cos_r, sin_r, cos_c, sin_c = tabs
    shp = (x.shape[1],) + (1,) * (x.ndim - 3) + (cos_r.shape[-1],)
    c = lambda t: t.reshape(shp).astype(x.dtype)
    xr = rope_rotate(x[..., :AXIS_DIM], c(cos_r), c(sin_r))
    xc = rope_rotate(x[..., AXIS_DIM:], c(cos_c), c(sin_c))
    return jnp.concatenate([xr, xc], axis=-1)


def diff_attention(qa, ka, va, t5_table, lam, subln_g, lam_init):
    B, N = qa.shape[0], qa.shape[1]
    nb = N // QBLK
    scale = HEAD_DIM ** -0.5
    qblocks = jnp.moveaxis(qa.reshape(B, nb, QBLK, A_HEADS, 2, HEAD_DIM), 1, 0)
    kpos = jnp.arange(N, dtype=jnp.int32)

    def one(args):
        q, i = args
        qpos = i * QBLK + jnp.arange(QBLK, dtype=jnp.int32)
        bias = t5_table[t5_bucket(kpos[None, :] - qpos[:, None])]
        bias = jnp.transpose(bias, (2, 0, 1)).astype(jnp.float32)
        s = jnp.einsum('bqhcd,bkhcd->bhcqk', q, ka).astype(jnp.float32) * scale + bias[None, :, None]
        p = jax.nn.softmax(s, axis=-1)
        w = p[:, :, 0] - lam * p[:, :, 1]
        return jnp.einsum('bhqk,bkhe->bqhe', w.astype(va.dtype), va)

    o = lax.map(one, (qblocks, jnp.arange(nb, dtype=jnp.int32)))
    o = jnp.moveaxis(o, 0, 1).reshape(B, N, A_HEADS, A_V_DIM)
    o = rmsnorm(o, subln_g, SUBLN_EPS) * (1.0 - lam_init)
    return o.reshape(B, N, A_WIDTH)


def gqa_attention(qb, kb, vb):
    B, N = qb.shape[0], qb.shape[1]
    nb = N // QBLK
    scale = HEAD_DIM ** -0.5
    qblocks = jnp.moveaxis(qb.reshape(B, nb, QBLK, B_KV_HEADS, B_GROUP, HEAD_DIM), 1, 0)

    def one(q):
        s = jnp.einsum('bqkgd,bnkd->bkgqn', q, kb).astype(jnp.float32) * scale
        p = jax.nn.softmax(s, axis=-1)
        return jnp.einsum('bkgqn,bnkd->bqkgd', p.astype(vb.dtype), vb)

    o = lax.map(one, qblocks)
    return jnp.moveaxis(o, 0, 1).reshape(B, N, B_WIDTH)


def encoder(x, t5_table, norm1, w_in, b_gate, lam_q1, lam_k1, lam_q2, lam_k2, subln_g,
            qk_norm_q, qk_norm_k, w_up_a, w_up_b, w_o, norm2, w_ff1, w_ff2, norm_f):
    B, N, _ = x.shape
    tabs = axial_rope_tables(N)
    for l in range(DEPTH):
        lam_init = 0.8 - 0.6 * math.exp(-0.3 * l)
        h = rmsnorm(x, norm1[l])
        z = h @ w_in[l]
        qa, ka, va, qb, kb, vb, zg = jnp.split(z, [SPLIT_1, SPLIT_2, SPLIT_3, SPLIT_4, SPLIT_5, SPLIT_6], axis=-1)
        lam = (jnp.exp(jnp.sum(lam_q1[l].astype(jnp.float32) * lam_k1[l].astype(jnp.float32)))
               - jnp.exp(jnp.sum(lam_q2[l].astype(jnp.float32) * lam_k2[l].astype(jnp.float32))) + lam_init)
        oa = diff_attention(qa.reshape(B, N, A_HEADS, 2, HEAD_DIM), ka.reshape(B, N, A_HEADS, 2, HEAD_DIM),
                            va.reshape(B, N, A_HEADS, A_V_DIM), t5_table, lam, subln_g[l], lam_init)
        qb = axial_rope(rmsnorm(qb.reshape(B, N, B_KV_HEADS, B_GROUP, HEAD_DIM), qk_norm_q[l]), tabs)
        kb = axial_rope(rmsnorm(kb.reshape(B, N, B_KV_HEADS, HEAD_DIM), qk_norm_k[l]), tabs)
        ob = gqa_attention(qb, kb, vb.reshape(B, N, B_KV_HEADS, HEAD_DIM))
        gates = jax.nn.sigmoid((zg + b_gate[l]).astype(jnp.float32)).astype(x.dtype)
        g_a, g_b = jnp.split(gates, 2, axis=-1)
        u = g_a * (oa @ w_up_a[l]) + g_b * (ob @ w_up_b[l])
        x = x + u @ w_o[l]
        h2 = rmsnorm(x, norm2[l])
        x = x + jnp.square(jax.nn.relu(h2 @ w_ff1[l])) @ w_ff2[l]
    return rmsnorm(x, norm_f)


def setup_inputs(seed: int = 0) -> dict:
    key = jax.random.key(seed)
    ks = jax.random.split(key, 24)
    f32 = jnp.float32
    nrm = lambda k, shape, s: jax.random.normal(k, shape, f32) * s
    gain = lambda k, shape: 1.0 + 0.02 * jax.random.normal(k, shape, f32)
    return {
        "x_prompt": jax.random.normal(ks[0], (BATCH, SEQ, D_MODEL), f32),
        "x_sample": jax.random.normal(ks[1], (DEC_BATCH, DEC_SEQ, D_MODEL), f32),
        "t5_table": nrm(ks[2], (N_BUCKETS, A_HEADS), 0.5),
        "norm1": gain(ks[3], (DEPTH, D_MODEL)),
        "w_in": nrm(ks[4], (DEPTH, D_MODEL, IN_COLS), D_MODEL ** -0.5),
        "b_gate": nrm(ks[5], (DEPTH, GATE_COLS), 0.02),
        "lam_q1": nrm(ks[6], (DEPTH, HEAD_DIM), 0.1),
        "lam_k1": nrm(ks[7], (DEPTH, HEAD_DIM), 0.1),
        "lam_q2": nrm(ks[8], (DEPTH, HEAD_DIM), 0.1),
        "lam_k2": nrm(ks[9], (DEPTH, HEAD_DIM), 0.1),
        "subln_g": gain(ks[10], (DEPTH, A_V_DIM)),
        "qk_norm_q": gain(ks[11], (DEPTH, HEAD_DIM)),
        "qk_norm_k": gain(ks[12], (DEPTH, HEAD_DIM)),
        "w_up_a": nrm(ks[13], (DEPTH, A_WIDTH, D_MODEL), A_WIDTH ** -0.5),
        "w_up_b": nrm(ks[14], (DEPTH, B_WIDTH, D_MODEL), B_WIDTH ** -0.5),
        "w_o": nrm(ks[15], (DEPTH, D_MODEL, D_MODEL), D_MODEL ** -0.5),
        "norm2": gain(ks[16], (DEPTH, D_MODEL)),
        "w_ff1": nrm(ks[17], (DEPTH, D_MODEL, D_FF), D_MODEL ** -0.5),
        "w_ff2": nrm(ks[18], (DEPTH, D_FF, D_MODEL), D_FF ** -0.5),
        "norm_f": gain(ks[19], (D_MODEL,)),
    }


def reference(x_prompt, x_sample, t5_table, norm1, w_in, b_gate, lam_q1, lam_k1, lam_q2, lam_k2, subln_g,
              qk_norm_q, qk_norm_k, w_up_a, w_up_b, w_o, norm2, w_ff1, w_ff2, norm_f):
    y_prompt = encoder(x_prompt, t5_table, norm1, w_in, b_gate, lam_q1, lam_k1, lam_q2, lam_k2, subln_g,
                       qk_norm_q, qk_norm_k, w_up_a, w_up_b, w_o, norm2, w_ff1, w_ff2, norm_f)
    y_sample = encoder(x_sample, t5_table, norm1, w_in, b_gate, lam_q1, lam_k1, lam_q2, lam_k2, subln_g,
                       qk_norm_q, qk_norm_k, w_up_a, w_up_b, w_o, norm2, w_ff1, w_ff2, norm_f)
    return (y_prompt, y_sample)
```

```python
import math
import numpy as np
import ml_dtypes
import concourse.bass as bass
import concourse.mybir as mybir
from concourse.bass_utils import run_bass_kernel_spmd

F32 = mybir.dt.float32
BF16 = mybir.dt.bfloat16
AF = mybir.ActivationFunctionType
ALU = mybir.AluOpType
AX = mybir.AxisListType

D = 1024
INC = 4352
DFF = 4096
EPS = 1e-6
SUBLN_EPS = 1e-5
NL = 2


class Op:
    __slots__ = ("eng", "fn", "deps", "sig", "semkey", "waits", "phase")

    def __init__(self, eng, fn, deps, semkey, phase):
        self.eng = eng
        self.fn = fn
        self.deps = deps
        self.sig = None
        self.semkey = semkey
        self.waits = None
        self.phase = phase


class Buf:
    __slots__ = ("ap", "w", "r", "key")

    def __init__(self, ap, key=None):
        self.ap = ap
        self.w = []
        self.r = []
        self.key = key


class Sched:
    ENGS = ("pe", "act", "dve", "pool", "sp")

    def __init__(self):
        self.q = {e: [] for e in self.ENGS}
        self.phase = 0
        self.last_dma = {}

    def do(self, eng, fn, reads=(), writes=(), extra=(), semkey=None):
        deps = list(extra)
        for b in reads:
            deps += b.w
        for b in writes:
            deps += b.w
            deps += b.r
        if eng == "pe":
            deps = [d for d in deps if d.eng != "pe"]
        o = Op(eng, fn, deps, semkey, self.phase)
        self.q[eng].append(o)
        for b in reads:
            b.r.append(o)
        for b in writes:
            b.w = [o]
            b.r = []
        if semkey is not None:
            self.last_dma[semkey] = o
        return o

    def barrier(self):
        deps = []
        for e in self.ENGS:
            if self.q[e]:
                for o in reversed(self.q[e]):
                    if o.fn is not None and o.semkey is None:
                        deps.append(o)
                        break
        deps += list(self.last_dma.values())
        for e in self.ENGS:
            o = Op(e, None, list(deps), None, self.phase)
            self.q[e].append(o)
        self.phase += 1

    def finalize(self):
        needed = set()
        for e in self.ENGS:
            for o in self.q[e]:
                for d in o.deps:
                    needed.add(id(d))
        counts = {}
        for e in self.ENGS:
            for o in self.q[e]:
                if o.fn is None:
                    continue
                if o.semkey is not None:
                    key = ("dma", o.semkey)
                    inc = 16
                elif id(o) in needed:
                    ph = o.phase
                    if ph == 0:
                        pg = "C-1"
                    else:
                        idx, lay = (ph - 1) % 4, (ph - 1) // 4
                        pg = f"B{lay}" if idx == 1 else (f"C{lay - 1}" if idx == 0 else f"C{lay}")
                    key = ("eng", e, pg)
                    inc = 1
                else:
                    continue
                counts[key] = counts.get(key, 0) + inc
                o.sig = (key, inc, counts[key])
        for e in self.ENGS:
            waited = {}
            for o in self.q[e]:
                w = {}
                for d in o.deps:
                    key, inc, val = d.sig
                    if waited.get(key, 0) >= val:
                        continue
                    if w.get(key, 0) < val:
                        w[key] = val
                waited.update(w)
                o.waits = w
        return list(counts.keys())

    def emit(self, name, eng, sems):
        for o in self.q[name]:
            for k, v in o.waits.items():
                eng.wait_ge(sems[k], v)
            if o.fn is None:
                continue
            ins = o.fn(eng)
            if o.sig is not None:
                ins.then_inc(sems[o.sig[0]], o.sig[1])


def build(NPR, NSA, debug=False, nlayers=NL):
    nc = bass.Bass("TRN2", target_bir_lowering=False)
    S = Sched()
    seqs = [("p", NPR), ("s", NSA)]
    NMAX = max(NPR, NSA)

    def din(name, shape, dt=F32):
        return nc.dram_tensor(name, list(shape), dt, kind="ExternalInput").ap()

    def dscr(name, shape, dt):
        kind = "ExternalOutput" if (debug and name.startswith("dbg_")) else "Internal"
        return nc.dram_tensor(name, list(shape), dt, kind=kind)

    x_in = {"p": din("xp", [NPR, D]), "s": din("xs", [NSA, D])}
    y_out = {"p": nc.dram_tensor("yp", [NPR, D], F32, kind="ExternalOutput").ap(),
             "s": nc.dram_tensor("ys", [NSA, D], F32, kind="ExternalOutput").ap()}
    w_in = din("w_in", [NL * D, INC])
    w_ua = din("w_up_a", [NL * 512, D])
    w_ub = din("w_up_b", [NL * 512, D])
    w_o = din("w_o", [NL * D, D])
    w_f1 = din("w_ff1", [NL * D, DFF])
    w_f2 = din("w_ff2", [NL * DFF, D])
    norm1 = din("norm1", [NL, D])
    norm2 = din("norm2", [NL, D])
    norm_f = din("norm_f", [D])
    b_gate = din("b_gate", [NL, 128, 16])
    lamv = din("lamv", [NL, 4, 64])
    subln = din("subln_g", [NL, 128])
    qkq = din("qk_norm_q", [NL, 64])
    qkk = din("qk_norm_k", [NL, 64])
    t5 = din("t5_table", [32, 4])
    ident_d = din("ident", [128, 128], BF16)
    jmat_d = din("jmat", [128, 128], BF16)
    rope_d = din("rope", [8192, 128])
    oh_d = din("oh", [32, 1280])

    wb_in = dscr("wb_in", [NL * D, INC], BF16).ap()
    wb_ua = dscr("wb_ua", [NL * 512, D], BF16).ap()
    wb_ub = dscr("wb_ub", [NL * 512, D], BF16).ap()
    wb_o = dscr("wb_o", [NL * D, D], BF16).ap()
    wb_f1 = dscr("wb_f1", [NL * D, DFF], BF16).ap()
    wb_f2 = dscr("wb_f2", [NL * DFF, D], BF16).ap()
    Fd_h = dscr("Fd", [4, 1280], F32)
    Xd = dscr("Xd", [2, 128, 4 * 1152], BF16).ap()
    scr = {}
    for sn, N in seqs:
        scr[sn] = dict(
            QAT=dscr(f"dbg_QAT_{sn}", [512, N], BF16).ap(),
            KAT=dscr(f"dbg_KAT_{sn}", [512, N], BF16).ap(),
            VA=dscr(f"dbg_VA_{sn}", [N, 512], BF16).ap(),
            QBT=dscr(f"dbg_QBT_{sn}", [512, N], BF16).ap(),
            KBT=dscr(f"dbg_KBT_{sn}", [128, N], BF16).ap(),
            VB=dscr(f"dbg_VB_{sn}", [N, 128], BF16).ap(),
            GT=dscr(f"dbg_GT_{sn}", [2048, N], BF16).ap(),
            OA=dscr(f"dbg_OA_{sn}", [N, 512], BF16).ap(),
            OB=dscr(f"dbg_OB_{sn}", [N, 512], BF16).ap(),
            X1=dscr(f"dbg_X1_{sn}", [N, D], F32).ap(),
            X2=dscr(f"dbg_X2_{sn}", [N, D], F32).ap(),
        )

    SB0 = (int(nc.sbuf_base) + 63) // 64 * 64
    SBTOP = int(nc.sbuf_top)

    class Alloc:
        def __init__(self, base):
            self.off = base

        def t(self, shape, dt, key=None):
            nbytes = int(np.prod(shape[1:])) * (4 if dt == F32 else 2)
            nbytes = (nbytes + 63) // 64 * 64
            nm = f"sb{Alloc.cnt}"
            Alloc.cnt += 1
            h = nc.alloc_sbuf_tensor_at(nm, list(shape), dt, offset=self.off)
            self.off += nbytes
            assert self.off <= SBTOP, ("SBUF overflow", self.off)
            return Buf(h.ap(), key or nm)
    Alloc.cnt = 0

    psum_all = nc.alloc_psum_tensor("psall", [128, 8, 512], F32).ap()
    pbank = [Buf(psum_all[:, i, :], f"bank{i}") for i in range(8)]

    def dma(out, in_, **kw):
        return lambda e: e.dma_start(out=out, in_=in_, **kw)

    PA = Alloc(SB0)
    ident = PA.t([128, 128], BF16, "ident")
    jmat = PA.t([128, 128], BF16, "jmat")
    cb = PA.t([128, 8], F32, "cb")
    bg = PA.t([128, 16], F32, "bg")
    gq8 = PA.t([128, 64], F32, "gq8")
    gqs8 = PA.t([128, 64], F32, "gqs8")
    gk = PA.t([128, 64], F32, "gk")
    gks = PA.t([128, 64], F32, "gks")
    gsub = PA.t([128, 128], F32, "gsub")
    lam4 = PA.t([128, 4, 64], F32, "lam4")
    lamt = PA.t([128, 2, 64], F32, "lamt")
    lams = PA.t([128, 2], F32, "lams")
    lame = PA.t([128, 2], F32, "lame")
    neglam = PA.t([128, 1], F32, "neglam")
    PBASE = PA.off

    S.do("sp", dma(ident.ap, ident_d), writes=[ident], semkey="c_ident")
    S.do("sp", dma(jmat.ap, jmat_d), writes=[jmat], semkey="c_jmat")
    S.do("sp", dma(cb.ap[:, 0:4], t5[15].partition_broadcast(128)), writes=[cb], semkey="c_cb")
    S.do("sp", dma(cb.ap[:, 4:8], t5[31].partition_broadcast(128)), writes=[cb], semkey="c_cb")

    A0 = Alloc(PBASE)
    CW = 4352
    stg32 = [A0.t([128, CW], F32, f"stg32_{i}") for i in range(2)]
    stg16 = [A0.t([128, CW], BF16, f"stg16_{i}") for i in range(2)]
    chunks = []
    for src, dst in ((w_in, wb_in), (w_ua, wb_ua), (w_ub, wb_ub), (w_o, wb_o), (w_f1, wb_f1), (w_f2, wb_f2)):
        R, C = src.shape
        g = max(1, CW // C)
        while (R // 128) % g != 0:
            g -= 1
        srcv = src.rearrange("(n p g) c -> n p (g c)", p=128, g=g)
        dstv = dst.rearrange("(n p g) c -> n p (g c)", p=128, g=g)
        for n in range(R // (128 * g)):
            chunks.append((srcv[n], dstv[n], g * C))

    def conv_load(ci, s32):
        a = s32[ci % 2]
        S.do("sp", dma(a.ap[:, 0:chunks[ci][2]], chunks[ci][0]), writes=[a], semkey=a.key)

    def conv_cast(ci, s32, s16, eng):
        a, b_, W = s32[ci % 2], s16[ci % 2], chunks[ci][2]
        S.do(eng, (lambda a=a, b_=b_, W=W: lambda e: e.tensor_copy(out=b_.ap[:, 0:W], in_=a.ap[:, 0:W]))(), reads=[a], writes=[b_])

    def conv_store(ci, s16):
        b_ = s16[ci % 2]
        S.do("sp", dma(chunks[ci][1], b_.ap[:, 0:chunks[ci][2]]), reads=[b_], semkey=b_.key + "s")

    NCH0 = 8
    for ci in range(NCH0):
        conv_load(ci, stg32)
        conv_cast(ci, stg32, stg16, "dve" if ci % 2 == 0 else "pool")
        conv_store(ci, stg16)

    t5sb = A0.t([32, 4], F32, "t5sb")
    ohsb = A0.t([32, 1280], F32, "ohsb")
    Fsb = A0.t([4, 1280], F32, "Fsb")
    X32 = A0.t([128, 4 * 1152], F32, "X32")
    Xhi = A0.t([128, 4 * 1152], BF16, "Xhi")
    Xr = A0.t([128, 4 * 1152], F32, "Xr")
    Xlo = A0.t([128, 4 * 1152], BF16, "Xlo")
    S.do("sp", dma(t5sb.ap, t5), writes=[t5sb], semkey="t5sb")
    S.do("sp", dma(ohsb.ap, oh_d), writes=[ohsb], semkey="ohsb")
    for c in range(3):
        w = 512 if c < 2 else 256
        S.do("pe", (lambda c=c, w=w: lambda e: e.matmul(pbank[c].ap[0:4, 0:w], lhsT=t5sb.ap[:, :], rhs=ohsb.ap[:, c * 512:c * 512 + w], start=True, stop=True))(),
             reads=[t5sb, ohsb], writes=[pbank[c]])
        S.do("dve", (lambda c=c, w=w: lambda e: e.tensor_copy(out=Fsb.ap[:, c * 512:c * 512 + w], in_=pbank[c].ap[0:4, 0:w]))(), reads=[pbank[c]], writes=[Fsb])
    S.do("sp", dma(Fd_h.ap(), Fsb.ap), reads=[Fsb], semkey="Fd")
    fdst = S.last_dma["Fd"]
    for h in range(4):
        S.do("sp", dma(X32.ap[:, h * 1152:(h + 1) * 1152], bass.AP(tensor=Fd_h, offset=h * 1280, ap=[[1, 128], [1, 1152]])),
             writes=[X32], extra=[fdst], semkey="X32")
    S.do("dve", lambda e: e.tensor_copy(out=Xhi.ap, in_=X32.ap), reads=[X32], writes=[Xhi])
    S.do("dve", lambda e: e.tensor_tensor(out=Xr.ap, in0=X32.ap, in1=Xhi.ap, op=ALU.subtract), reads=[X32, Xhi], writes=[Xr])
    S.do("dve", lambda e: e.tensor_copy(out=Xlo.ap, in_=Xr.ap), reads=[Xr], writes=[Xlo])
    S.do("sp", dma(Xd[0], Xhi.ap), reads=[Xhi], semkey="Xd")
    S.do("sp", dma(Xd[1], Xlo.ap), reads=[Xlo], semkey="Xd")
    S.barrier()

    def rsqrt_inplace(buf, apf):
        S.do("act", lambda e: e.activation(out=apf(), in_=apf(), func=AF.Sqrt), reads=[buf], writes=[buf])
        S.do("dve", lambda e: e.reciprocal(out=apf(), in_=apf()), reads=[buf], writes=[buf])

    for l in range(nlayers):
        lam_init = 0.8 - 0.6 * math.exp(-0.3 * l)
        S.do("sp", dma(bg.ap, b_gate[l]), writes=[bg], semkey="c_bg")
        S.do("sp", dma(gq8.ap, qkq[l].partition_broadcast(128)), writes=[gq8], semkey="c_gq")
        S.do("sp", dma(gk.ap, qkk[l].partition_broadcast(128)), writes=[gk], semkey="c_gk")
        S.do("sp", dma(gsub.ap, subln[l].partition_broadcast(128)), writes=[gsub], semkey="c_gsub")
        S.do("sp", dma(lam4.ap.rearrange("p a b -> p (a b)"), lamv[l].rearrange("a b -> (a b)").partition_broadcast(128)), writes=[lam4], semkey="c_lam")
        S.do("pool", lambda e: e.tensor_scalar(out=gq8.ap, in0=gq8.ap, scalar1=0.125, scalar2=None, op0=ALU.mult), reads=[gq8], writes=[gq8])
        S.do("pool", (lambda sc_=float(1.0 - lam_init): lambda e: e.tensor_scalar(out=gsub.ap, in0=gsub.ap, scalar1=sc_, scalar2=None, op0=ALU.mult))(), reads=[gsub], writes=[gsub])
        for (src, dst) in ((gq8, gqs8), (gk, gks)):
            for blk in range(4):
                sb = blk ^ 1
                S.do("pool", (lambda src=src, dst=dst, blk=blk, sb=sb: lambda e: e.tensor_copy(out=dst.ap[:, blk * 16:(blk + 1) * 16], in_=src.ap[:, sb * 16:(sb + 1) * 16]))(),
                     reads=[src], writes=[dst])
        S.do("dve", lambda e: e.tensor_tensor(out=lamt.ap[:, 0, :], in0=lam4.ap[:, 0, :], in1=lam4.ap[:, 1, :], op=ALU.mult), reads=[lam4], writes=[lamt])
        S.do("dve", lambda e: e.tensor_tensor(out=lamt.ap[:, 1, :], in0=lam4.ap[:, 2, :], in1=lam4.ap[:, 3, :], op=ALU.mult), reads=[lam4, lamt], writes=[lamt])
        S.do("dve", lambda e: e.tensor_reduce(out=lams.ap, in_=lamt.ap, axis=AX.X, op=ALU.add), reads=[lamt], writes=[lams])
        S.do("act", lambda e: e.activation(out=lame.ap, in_=lams.ap, func=AF.Exp), reads=[lams], writes=[lame])
        S.do("dve", lambda e: e.tensor_tensor(out=neglam.ap, in0=lame.ap[:, 1:2], in1=lame.ap[:, 0:1], op=ALU.subtract), reads=[lame], writes=[neglam])
        S.do("dve", (lambda li=lam_init: lambda e: e.tensor_scalar(out=neglam.ap, in0=neglam.ap, scalar1=float(-li), scalar2=None, op0=ALU.add))(), reads=[neglam], writes=[neglam])

        AA = Alloc(PBASE)
        wA = AA.t([128, 8, INC], BF16, "wA")
        g1b = AA.t([128, D], F32, "g1b")
        xin = [AA.t([128, 4, D], F32, f"xin{i}") for i in range(2)]
        junk = AA.t([128, 4, D], F32, "junk")
        ss4 = AA.t([128, 4], F32, "ss4")
        rstd4 = AA.t([128, 4], F32, "rstd4")
        hbf = AA.t([128, 4, D], BF16, "hbf")
        hT = [AA.t([128, 8, 512], BF16, f"hT{i}") for i in range(2)]
        fmst = [AA.t([128, 4, 512], BF16, f"fmst{i}") for i in range(4)]
        va_st = AA.t([128, 4, 512], BF16, "va_st")
        vb_st = AA.t([128, 4, 128], BF16, "vb_st")
        qbT_st = AA.t([128, 4, 512], BF16, "qbT_st")
        kbT_st = AA.t([128, 512], BF16, "kbT_st")
        ropet = [AA.t([128, 4, 128], F32, f"ropet{i}") for i in range(2)]
        tq = [AA.t([128, 512], F32, f"tq{i}") for i in range(3)]
        tk = [AA.t([128, 128], F32, f"tk{i}") for i in range(3)]
        qb_bf = [AA.t([128, 512], BF16, f"qb_bf{i}") for i in range(4)]
        kb_bf = [AA.t([128, 128], BF16, f"kb_bf{i}") for i in range(4)]
        CqSq = AA.t([128, 128], F32, "CqSq")
        CkSk = AA.t([128, 128], F32, "CkSk")
        ss8 = AA.t([128, 8], F32, "ss8")

        S.do("sp", dma(wA.ap, wb_in[l * D:(l + 1) * D, :].rearrange("(c p) n -> p c n", p=128)), writes=[wA], semkey="wA")
        S.do("sp", dma(g1b.ap, norm1[l].partition_broadcast(128)), writes=[g1b], semkey="g1b")

        tilesA = []
        for sn, N in seqs:
            for t in range(N // 512):
                tilesA.append((sn, t))
        fmi_box = [0]

        def a_load(i):
            sn, t = tilesA[i]
            sc = scr[sn]
            xsrc = x_in[sn] if l == 0 else sc["X2"]
            xb = xin[i % 2]
            rp = ropet[i % 2]
            S.do("sp", dma(xb.ap, xsrc[t * 512:(t + 1) * 512, :].rearrange("(j p) d -> p j d", p=128)), writes=[xb], semkey=xb.key)
            S.do("sp", dma(rp.ap, rope_d[t * 512:(t + 1) * 512, :].rearrange("(j p) d -> p j d", p=128)), writes=[rp], semkey=rp.key)

        t0q = [tq[0]] + [AA.t([128, 512], F32, f"t0q{j}") for j in range(1, 4)]
        t0k = [tk[0]] + [AA.t([128, 128], F32, f"t0k{j}") for j in range(1, 4)]
        ssA = AA.t([128, 4, 10], F32, "ssA")

        def a_norm_a(i):
            xb = xin[i % 2]
            S.do("pool", (lambda xb=xb: lambda e: e.tensor_tensor(out=junk.ap, in0=xb.ap, in1=xb.ap, op=ALU.mult))(), reads=[xb], writes=[junk])
            S.do("dve", lambda e: e.tensor_reduce(out=ss4.ap, in_=junk.ap, axis=AX.X, op=ALU.add), reads=[junk], writes=[ss4])
            S.do("dve", lambda e: e.tensor_scalar(out=rstd4.ap, in0=ss4.ap, scalar1=1.0 / D, scalar2=EPS, op0=ALU.mult, op1=ALU.add), reads=[ss4], writes=[rstd4])

        def a_norm_b(i):
            xb = xin[i % 2]
            rsqrt_inplace(rstd4, lambda: rstd4.ap)
            ws = []
            prev = list(hbf.r) + list(hbf.w)
            for j in range(4):
                ws.append(S.do("dve", (lambda j=j, xb=xb: lambda e: e.scalar_tensor_tensor(out=hbf.ap[:, j, :], in0=xb.ap[:, j, :], scalar=rstd4.ap[:, j:j + 1], in1=g1b.ap, op0=ALU.mult, op1=ALU.mult))(),
                               reads=[xb, rstd4, g1b], extra=prev))
            hbf.w = ws
            hbf.r = []

        def a_tr(i):
            hTb = hT[i % 2]
            ws = []
            prev = list(hTb.r) + list(hTb.w)
            for c in range(8):
                pb = pbank[c % 2]
                pv = pb.ap.bitcast(BF16)

                def f(e, c=c, pv=pv):
                    last = None
                    for j in range(4):
                        last = e.transpose(pv[:, j * 128:(j + 1) * 128], hbf.ap[:, j, c * 128:(c + 1) * 128], ident.ap)
                    return last
                S.do("pe", f, reads=[hbf, ident], writes=[pb])
                if c % 2 == 0:
                    o = S.do("act", (lambda c=c, pv=pv, hTb=hTb: lambda e: e.copy(out=hTb.ap[:, c, :], in_=pv[:, 0:512]))(), reads=[pb], extra=prev)
                else:
                    o = S.do("dve", (lambda c=c, pv=pv, hTb=hTb: lambda e: e.tensor_copy(out=hTb.ap[:, c, :], in_=pv[:, 0:512]))(), reads=[pb], extra=prev)
                ws.append(o)
            hTb.w = ws
            hTb.r = []

        def a_tm1(i):
            hTb = hT[i % 2]
            vaw, vbw = [], []
            prev_va = list(va_st.r) + list(va_st.w)
            prev_vb = list(vb_st.r) + list(vb_st.w)
            ssw = []
            prev_ss = list(ssA.r) + list(ssA.w)
            for j in range(4):
                def mmtm(pb, c0, w, j=j, hTb=hTb):
                    def f(e):
                        last = None
                        for c in range(8):
                            last = e.matmul(pb.ap[:, 0:w], lhsT=hTb.ap[:, c, j * 128:(j + 1) * 128], rhs=wA.ap[:, c, c0:c0 + w], start=(c == 0), stop=(c == 7))
                        return last
                    return f
                pb = pbank[4]
                S.do("pe", mmtm(pb, 1024, 512), reads=[wA, hTb], writes=[pb])
                vaw.append(S.do("act", (lambda j=j, pb=pb: lambda e: e.copy(out=va_st.ap[:, j, :], in_=pb.ap))(), reads=[pb], extra=prev_va))
                for (nh, pbi, c0, w, t0, t1, h0) in ((8, 5, 1536, 512, t0q[j], tq[1], 0), (2, 6, 2048, 256, t0k[j], tk[1], 8)):
                    pb = pbank[pbi]
                    W = nh * 64
                    S.do("pe", mmtm(pb, c0, w), reads=[wA, hTb], writes=[pb])
                    if nh == 2:
                        vbw.append(S.do("act", (lambda j=j, pb=pb: lambda e: e.copy(out=vb_st.ap[:, j, :], in_=pb.ap[:, 128:256]))(), reads=[pb], extra=prev_vb))
                    S.do("act", (lambda pb=pb, t0=t0, W=W: lambda e: e.copy(out=t0.ap[:, 0:W], in_=pb.ap[:, 0:W]))(), reads=[pb], writes=[t0])
                    S.do("pool", (lambda t0=t0, t1=t1, W=W: lambda e: e.tensor_tensor(out=t1.ap[:, 0:W], in0=t0.ap[:, 0:W], in1=t0.ap[:, 0:W], op=ALU.mult))(), reads=[t0], writes=[t1])
                    ssw.append(S.do("dve", (lambda t1=t1, nh=nh, W=W, j=j, h0=h0: lambda e: e.tensor_reduce(out=ssA.ap[:, j, h0:h0 + nh], in_=t1.ap[:, 0:W].rearrange("p (h d) -> p h d", d=64), axis=AX.X, op=ALU.add))(),
                                    reads=[t1], extra=prev_ss))
            va_st.w = vaw
            va_st.r = []
            vb_st.w = vbw
            vb_st.r = []
            ssA.w = ssw
            ssA.r = []

        def a_qkrs(i):
            S.do("dve", lambda e: e.tensor_scalar(out=ssA.ap, in0=ssA.ap, scalar1=1.0 / 64, scalar2=EPS, op0=ALU.mult, op1=ALU.add), reads=[ssA], writes=[ssA])
            rsqrt_inplace(ssA, lambda: ssA.ap)

        def a_st3(i):
            rp = ropet[i % 2]
            for j in range(4):
                S.do("pool", (lambda j=j, rp=rp: lambda e: e.tensor_tensor(out=CqSq.ap[:, 0:64], in0=rp.ap[:, j, 0:64], in1=gq8.ap, op=ALU.mult))(), reads=[rp, gq8], writes=[CqSq])
                S.do("pool", (lambda j=j, rp=rp: lambda e: e.tensor_tensor(out=CqSq.ap[:, 64:128], in0=rp.ap[:, j, 64:128], in1=gqs8.ap, op=ALU.mult))(), reads=[rp, gqs8, CqSq], writes=[CqSq])
                S.do("pool", (lambda j=j, rp=rp: lambda e: e.tensor_tensor(out=CkSk.ap[:, 0:64], in0=rp.ap[:, j, 0:64], in1=gk.ap, op=ALU.mult))(), reads=[rp, gk], writes=[CkSk])
                S.do("pool", (lambda j=j, rp=rp: lambda e: e.tensor_tensor(out=CkSk.ap[:, 64:128], in0=rp.ap[:, j, 64:128], in1=gks.ap, op=ALU.mult))(), reads=[rp, gks, CkSk], writes=[CkSk])
                for (nh, t0, t1, t2, obf, tab, h0) in ((8, t0q[j], tq[1], tq[2], qb_bf[j], CqSq, 0), (2, t0k[j], tk[1], tk[2], kb_bf[j], CkSk, 8)):
                    W = nh * 64
                    S.do("dve", (lambda t0=t0, t1=t1, nh=nh, W=W, j=j, h0=h0: lambda e: e.tensor_tensor(out=t1.ap[:, 0:W].rearrange("p (h d) -> p h d", d=64), in0=t0.ap[:, 0:W].rearrange("p (h d) -> p h d", d=64),
                                                                                                        in1=ssA.ap[:, j, h0:h0 + nh].unsqueeze(2).to_broadcast([128, nh, 64]), op=ALU.mult))(), reads=[t0, ssA], writes=[t1])
                    S.do("pool", (lambda t1=t1, t2=t2, nh=nh, W=W, tab=tab: lambda e: e.tensor_tensor(out=t2.ap[:, 0:W].rearrange("p (h d) -> p h d", d=64), in0=t1.ap[:, 0:W].rearrange("p (h d) -> p h d", d=64),
                                                                                                        in1=tab.ap[:, 0:64].unsqueeze(1).to_broadcast([128, nh, 64]), op=ALU.mult))(), reads=[t1, tab], writes=[t2])
                    ops = []
                    prev0 = list(t0.r) + list(t0.w)
                    for xh in range(2):
                        eng = "dve" if xh == 0 else "pool"

                        def f(e, xh=xh, t0=t0, t1=t1, nh=nh, W=W, tab=tab):
                            o_ = t0.ap[:, 0:W].rearrange("p (h a x d) -> p h a x d", a=2, x=2, d=16)[:, :, :, xh, :]
                            i_ = t1.ap[:, 0:W].rearrange("p (h a x d) -> p h a x d", a=2, x=2, d=16)[:, :, :, 1 - xh, :]
                            s_ = tab.ap[:, 64:128].rearrange("p (a x d) -> p a x d", a=2, x=2, d=16)[:, :, xh, :].unsqueeze(1).to_broadcast([128, nh, 2, 16])
                            return e.tensor_tensor(out=o_, in0=i_, in1=s_, op=ALU.mult)
                        ops.append(S.do(eng, f, reads=[t1, tab], extra=prev0))
                    t0.w = ops
                    t0.r = []
                    S.do("dve", (lambda t0=t0, t2=t2, obf=obf, W=W: lambda e: e.tensor_tensor(out=obf.ap[:, 0:W], in0=t0.ap[:, 0:W], in1=t2.ap[:, 0:W], op=ALU.add))(), reads=[t0, t2], writes=[obf])

        def a_fm(i, glist):
            sn, t = tilesA[i]
            sc = scr[sn]
            hTb = hT[i % 2]
            groups = [("QA", 0, sc["QAT"], 0), ("KA", 512, sc["KAT"], 0)] + [("G", 2304 + 512 * k, sc["GT"], 512 * k) for k in range(4)]
            for gidx in glist:
                kind, col0, dst, drow0 = groups[gidx]
                st = fmst[fmi_box[0] % 4]
                fmi_box[0] += 1
                ws = []
                prev = list(st.r) + list(st.w)
                for b in range(4):
                    pb = pbank[2 + (b % 2)]
                    cc = col0 + b * 128

                    def f(e, cc=cc, pb=pb, hTb=hTb):
                        last = None
                        for c in range(8):
                            last = e.matmul(pb.ap, lhsT=wA.ap[:, c, cc:cc + 128], rhs=hTb.ap[:, c, :], start=(c == 0), stop=(c == 7))
                        return last
                    S.do("pe", f, reads=[wA, hTb], writes=[pb])
                    if kind == "QA":
                        o = S.do("act", (lambda st=st, b=b, pb=pb: lambda e: e.mul(out=st.ap[:, b, :], in_=pb.ap, mul=0.125))(), reads=[pb], extra=prev)
                    elif kind == "KA":
                        o = S.do("act", (lambda st=st, b=b, pb=pb: lambda e: e.copy(out=st.ap[:, b, :], in_=pb.ap))(), reads=[pb], extra=prev)
                    else:
                        gbi = (col0 - 2304) // 128 + b
                        o = S.do("act", (lambda st=st, b=b, pb=pb, gbi=gbi: lambda e: e.activation(out=st.ap[:, b, :], in_=pb.ap, func=AF.Sigmoid, bias=bg.ap[:, gbi:gbi + 1], scale=1.0))(), reads=[pb, bg], extra=prev)
                    ws.append(o)
                st.w = ws
                st.r = []
                S.do("sp", dma(dst[drow0:drow0 + 512, t * 512:(t + 1) * 512].rearrange("(b p) n -> p b n", p=128), st.ap), reads=[st], semkey=st.key)

        def a_qktr(i):
            sn, t = tilesA[i]
            sc = scr[sn]
            qtw, ktw = [], []
            prev_qt = list(qbT_st.r) + list(qbT_st.w)
            prev_kt = list(kbT_st.r) + list(kbT_st.w)
            for j in range(4):
                for nh, obf in ((8, qb_bf[j]), (2, kb_bf[j])):
                    ncb = nh * 64 // 128
                    tpb = pbank[7]
                    pv = tpb.ap.bitcast(BF16)

                    def ft(e, obf=obf, ncb=ncb, pv=pv):
                        last = None
                        for cbk in range(ncb):
                            last = e.transpose(pv[:, cbk * 128:(cbk + 1) * 128], obf.ap[:, cbk * 128:(cbk + 1) * 128], ident.ap)
                        return last
                    S.do("pe", ft, reads=[obf, ident], writes=[tpb])
                    if nh == 8:
                        qtw.append(S.do("dve", (lambda j=j, pv=pv: lambda e: e.tensor_copy(out=qbT_st.ap[:, :, j * 128:(j + 1) * 128], in_=pv[:, 0:512].rearrange("p (c n) -> p c n", n=128)))(), reads=[tpb], extra=prev_qt))
                    else:
                        ktw.append(S.do("dve", (lambda j=j, pv=pv: lambda e: e.tensor_copy(out=kbT_st.ap[:, j * 128:(j + 1) * 128], in_=pv[:, 0:128]))(), reads=[tpb], extra=prev_kt))
            qbT_st.w = qtw
            qbT_st.r = []
            kbT_st.w = ktw
            kbT_st.r = []
            S.do("sp", dma(sc["VA"][t * 512:(t + 1) * 512, :].rearrange("(j p) d -> p j d", p=128), va_st.ap), reads=[va_st], semkey="va_st")
            S.do("sp", dma(sc["VB"][t * 512:(t + 1) * 512, :].rearrange("(j p) d -> p j d", p=128), vb_st.ap), reads=[vb_st], semkey="vb_st")
            S.do("sp", dma(sc["QBT"][:, t * 512:(t + 1) * 512].rearrange("(c p) n -> p c n", p=128), qbT_st.ap), reads=[qbT_st], semkey="qbT_st")
            S.do("sp", dma(sc["KBT"][:, t * 512:(t + 1) * 512], kbT_st.ap), reads=[kbT_st], semkey="kbT_st")

        nTA = len(tilesA)
        a_load(0)
        a_norm_a(0)
        a_norm_b(0)
        a_tr(0)
        for i in range(nTA):
            nxt = i + 1 < nTA
            if nxt:
                a_load(i + 1)
            a_tm1(i)
            if nxt:
                a_norm_a(i + 1)
            a_fm(i, [0, 1])
            a_qkrs(i)
            a_fm(i, [2, 3])
            if nxt:
                a_norm_b(i + 1)
            a_st3(i)
            a_fm(i, [4, 5])
            a_qktr(i)
            if nxt:
                a_tr(i + 1)
        S.barrier()

        AB = Alloc(PBASE)
        NKBM = NMAX // 128
        KT = [AB.t([128, NMAX], BF16, f"KT{i}") for i in range(2)]
        VV = [AB.t([128, NKBM, 129], BF16, f"VV{i}") for i in range(2)]
        QT = [AB.t([128, 512], BF16, f"QT{i}") for i in range(3)]
        PP = [AB.t([128, 2, 512], BF16, f"PP{i}") for i in range(4)]
        XH = AB.t([128, 4 * 1152], BF16, "XH")
        XL = AB.t([128, 4 * 1152], BF16, "XL")
        accS = AB.t([128, 4, 258], F32, "accS")
        rl = AB.t([128, 8], F32, "rl")
        e0 = AB.t([128, 4, 128], F32, "e0")
        e1 = AB.t([128, 4, 128], F32, "e1")
        e2 = AB.t([128, 4, 128], F32, "e2")
        ssb = AB.t([128, 4], F32, "ssb")
        ost = [AB.t([128, 4, 128], BF16, f"ost{i}") for i in range(2)]
        bstg32 = [AB.t([128, CW], F32, f"bstg32_{i}") for i in range(2)]
        bstg16 = [AB.t([128, CW], BF16, f"bstg16_{i}") for i in range(2)]
        bg_state = {"k": 0}

        def bg_step():
            if l != 0:
                return False
            k = bg_state["k"]
            nrem = len(chunks) - NCH0
            if k >= nrem + 2:
                return False
            if k < nrem:
                conv_load(NCH0 + k, bstg32)
            if 1 <= k <= nrem:
                conv_cast(NCH0 + k - 1, bstg32, bstg16, "pool")
            if 2 <= k <= nrem + 1:
                conv_store(NCH0 + k - 2, bstg16)
            bg_state["k"] = k + 1
            return True
        S.do("sp", dma(XH.ap, Xd[0]), writes=[XH], semkey="XH")
        S.do("sp", dma(XL.ap, Xd[1]), writes=[XL], semkey="XL")
        Spair = [(pbank[0], pbank[1]), (pbank[2], pbank[3])]
        def pair_ap(i):
            return psum_all[:, 2 * i:2 * i + 2, :]
        acc_b = pbank[4:8]
        gi = 0
        ui = 0
        ji = 0
        qi = 0
        pendq = []
        for sn, N in seqs:
            sc = scr[sn]
            NKB = N // 128
            NQC = N // 512
            for grp in range(8):
                diff = grp < 4
                E = 128 if diff else 64
                kt, vv = KT[gi % 2], VV[gi % 2]
                gi += 1
                if diff:
                    h = grp
                    S.do("sp", dma(kt.ap[:, 0:N], sc["KAT"][h * 128:(h + 1) * 128, :]), writes=[kt], semkey=kt.key)
                    S.do("sp", dma(vv.ap[:, 0:NKB, 0:128], sc["VA"][:, h * 128:(h + 1) * 128].rearrange("(k p) e -> p k e", p=128)), writes=[vv], semkey=vv.key)
                    qsrc = sc["QAT"][h * 128:(h + 1) * 128, :]
                else:
                    pr = grp - 4
                    g = pr // 2
                    o1 = S.do("sp", dma(kt.ap[0:64, 0:N], sc["KBT"][g * 64:(g + 1) * 64, :]), writes=[kt], semkey=kt.key)
                    o2 = S.do("sp", dma(kt.ap[64:128, 0:N], sc["KBT"][g * 64:(g + 1) * 64, :]), extra=list(o1.deps), semkey=kt.key)
                    kt.w = [o1, o2]
                    S.do("sp", dma(vv.ap[:, 0:NKB, 0:64], sc["VB"][:, g * 64:(g + 1) * 64].rearrange("(k p) e -> p k e", p=128)), writes=[vv], semkey=vv.key)
                    qsrc = sc["QBT"][pr * 128:(pr + 1) * 128, :]
                lw = list(vv.w)
                om = S.do("pool", (lambda vv=vv, E=E, NKB=NKB: lambda e: e.memset(vv.ap[:, 0:NKB, E:E + 1], 1.0))(), extra=lw + [x for x in lw[0].deps])
                vv.w = lw + [om]
                for qc in range(NQC):
                    qt = QT[qi % 3]
                    qi += 1
                    S.do("sp", dma(qt.ap, qsrc[:, qc * 512:(qc + 1) * 512]), writes=[qt], semkey=qt.key)
                    for kb in range(NKB):
                        sp0, sp1 = Spair[ui % 2]
                        pp = PP[ui % 4]
                        near = diff and (4 * qc - 1 <= kb <= 4 * qc + 4)
                        d_off = kb * 128 - qc * 512
                        wcol = 512 - d_off

                        def fqk(e, kt=kt, qt=qt, kb=kb, sp0=sp0, sp1=sp1, near=near, wcol=wcol, grp=grp):
                            last = None
                            for c, spx in ((0, sp0), (1, sp1)):
                                last = e.matmul(spx.ap, lhsT=kt.ap[c * 64:(c + 1) * 64, kb * 128:(kb + 1) * 128], rhs=qt.ap[c * 64:(c + 1) * 64, :], start=True, stop=not near)
                            if near:
                                for c, spx in ((0, sp0), (1, sp1)):
                                    e.matmul(spx.ap, lhsT=jmat.ap, rhs=XH.ap[:, grp * 1152 + wcol:grp * 1152 + wcol + 512], start=False, stop=False)
                                    last = e.matmul(spx.ap, lhsT=jmat.ap, rhs=XL.ap[:, grp * 1152 + wcol:grp * 1152 + wcol + 512], start=False, stop=True)
                            return last
                        S.do("pe", fqk, reads=[kt, qt] + ([XH, XL, jmat] if near else []), writes=[sp0, sp1])
                        if len(pendq) >= 2:
                            pendq.pop(0)()
                        if diff and not near:
                            bcol = grp if kb < 4 * qc else 4 + grp
                            bias_ap = cb.ap[:, bcol:bcol + 1]
                        else:
                            bias_ap = None
                        pa = pair_ap(ui % 2)

                        def fex(e, pa=pa, pp=pp, bias_ap=bias_ap):
                            if bias_ap is None:
                                return e.activation(out=pp.ap, in_=pa, func=AF.Exp)
                            return e.activation(out=pp.ap, in_=pa, func=AF.Exp, bias=bias_ap, scale=1.0)
                        S.do("act", fex, reads=[sp0, sp1] + ([cb] if bias_ap is not None else []), writes=[pp])

                        def mk_pv(pp=pp, vv=vv, kb=kb, NKB=NKB, E=E, diff=diff, qc=qc, grp=grp, sc=sc, ji=ji, kt=kt, qt=qt):
                            def fpv(e):
                                last = None
                                for c in range(2):
                                    for qb in range(4):
                                        if diff:
                                            bk = acc_b[c * 2 + qb // 2]
                                            col = (qb % 2) * 129
                                        else:
                                            bk = acc_b[c]
                                            col = qb * 65
                                        first_in_bank = (qb % 2 == 0) if diff else (qb == 0)
                                        last = e.matmul(bk.ap[:, col:col + E + 1], lhsT=pp.ap[:, c, qb * 128:(qb + 1) * 128], rhs=vv.ap[:, kb, 0:E + 1],
                                                        start=(kb == 0 and first_in_bank), stop=(kb == NKB - 1), skip_group_check=True)
                                return last
                            S.do("pe", fpv, reads=[pp, vv], writes=(acc_b if diff else acc_b[0:2]) if kb == 0 else [], extra=[])
                            if kb == NKB - 1:
                                ob = ost[ji % 2]
                                if diff:
                                    def fcp(e):
                                        return e.tensor_copy(out=accS.ap, in_=psum_all[:, 4:8, 0:258])
                                    o_cp = S.do("dve", fcp, reads=[], writes=[accS], extra=[S.q["pe"][-1]])
                                    for bkx in acc_b:
                                        bkx.r.append(o_cp)
                                    av = accS.ap.rearrange("p b (i e) -> p (b i) e", e=129)
                                    S.do("dve", lambda e: e.reciprocal(out=rl.ap, in_=av[:, :, 128]), reads=[accS], writes=[rl])
                                    S.do("dve", lambda e: e.tensor_scalar(out=rl.ap[:, 4:8], in0=rl.ap[:, 4:8], scalar1=neglam.ap[:, 0:1], scalar2=None, op0=ALU.mult), reads=[rl, neglam], writes=[rl])
                                    S.do("pool", lambda e: e.tensor_tensor(out=e0.ap, in0=av[:, 0:4, 0:128], in1=rl.ap[:, 0:4].unsqueeze(2).to_broadcast([128, 4, 128]), op=ALU.mult), reads=[accS, rl], writes=[e0])
                                    S.do("dve", lambda e: e.tensor_tensor(out=e1.ap, in0=av[:, 4:8, 0:128], in1=rl.ap[:, 4:8].unsqueeze(2).to_broadcast([128, 4, 128]), op=ALU.mult), reads=[accS, rl], writes=[e1])
                                    S.do("pool", lambda e: e.tensor_tensor(out=e0.ap, in0=e0.ap, in1=e1.ap, op=ALU.add), reads=[e0, e1], writes=[e0])
                                    S.do("pool", lambda e: e.tensor_tensor(out=e2.ap, in0=e0.ap, in1=e0.ap, op=ALU.mult), reads=[e0], writes=[e2])
                                    S.do("dve", lambda e: e.tensor_reduce(out=ssb.ap, in_=e2.ap, axis=AX.X, op=ALU.add), reads=[e2], writes=[ssb])
                                    S.do("dve", lambda e: e.tensor_scalar(out=ssb.ap, in0=ssb.ap, scalar1=1.0 / 128, scalar2=SUBLN_EPS, op0=ALU.mult, op1=ALU.add), reads=[ssb], writes=[ssb])
                                    rsqrt_inplace(ssb, lambda: ssb.ap)
                                    S.do("dve", lambda e: e.tensor_tensor(out=e1.ap, in0=e0.ap, in1=ssb.ap.unsqueeze(2).to_broadcast([128, 4, 128]), op=ALU.mult), reads=[e0, ssb], writes=[e1])
                                    S.do("pool", lambda e: e.tensor_tensor(out=ob.ap, in0=e1.ap, in1=gsub.ap.unsqueeze(1).to_broadcast([128, 4, 128]), op=ALU.mult), reads=[e1, gsub], writes=[ob])
                                    S.do("sp", dma(sc["OA"][qc * 512:(qc + 1) * 512, grp * 128:(grp + 1) * 128].rearrange("(j p) e -> p j e", p=128), ob.ap), reads=[ob], semkey=ob.key)
                                else:
                                    pr = grp - 4

                                    def fcp(e):
                                        return e.tensor_copy(out=accS.ap.rearrange("p b w -> p (b w)")[:, 0:520].rearrange("p (b w) -> p b w", w=260), in_=psum_all[:, 4:6, 0:260])
                                    o_cp = S.do("dve", fcp, reads=[], writes=[accS], extra=[S.q["pe"][-1]])
                                    for bkx in acc_b[0:2]:
                                        bkx.r.append(o_cp)
                                    av = accS.ap.rearrange("p b w -> p (b w)")[:, 0:520].rearrange("p (i e) -> p i e", e=65)
                                    S.do("dve", lambda e: e.reciprocal(out=rl.ap, in_=av[:, :, 64]), reads=[accS], writes=[rl])
                                    def fo(e):
                                        o_ = ob.ap.rearrange("p j (hd d) -> p hd j d", d=64)
                                        i_ = av[:, :, 0:64].rearrange("p (hd j) d -> p hd j d", j=4)
                                        r_ = rl.ap.rearrange("p (hd j) -> p hd j", j=4).unsqueeze(3).to_broadcast([128, 2, 4, 64])
                                        return e.tensor_tensor(out=o_, in0=i_, in1=r_, op=ALU.mult)
                                    S.do("dve", fo, reads=[accS, rl], writes=[ob])
                                    S.do("sp", dma(sc["OB"][qc * 512:(qc + 1) * 512, pr * 128:(pr + 1) * 128].rearrange("(j p) e -> p j e", p=128), ob.ap), reads=[ob], semkey=ob.key)
                        pendq.append(mk_pv)
                        if kb == NKB - 1:
                            bg_step()
                        if kb == NKB - 1:
                            ji += 1
                        ui += 1
        while pendq:
            pendq.pop(0)()
        while bg_step():
            pass
        S.barrier()

        AC = Alloc(PBASE)
        wua = AC.t([128, 4, D], BF16, "wua")
        wub = AC.t([128, 4, D], BF16, "wub")
        wo = AC.t([128, 8, D], BF16, "wo")
        oat = [AC.t([128, 4, 512], BF16, f"oat{i}") for i in range(2)]
        obt = [AC.t([128, 4, 512], BF16, f"obt{i}") for i in range(2)]
        gtt = [AC.t([128, 16, 512], BF16, f"gtt{i}") for i in range(2)]
        xc = [AC.t([128, 4, D], F32, f"xc{i}") for i in range(2)]
        oT = AC.t([128, 8, 512], BF16, "oT")
        uT = AC.t([128, 8, 512], BF16, "uT")
        u1 = [AC.t([128, 512], F32, f"u1_{i}") for i in range(2)]
        u2 = [AC.t([128, 512], F32, f"u2_{i}") for i in range(2)]
        S.do("sp", dma(wua.ap, wb_ua[l * 512:(l + 1) * 512, :].rearrange("(c p) n -> p c n", p=128)), writes=[wua], semkey="wua")
        S.do("sp", dma(wub.ap, wb_ub[l * 512:(l + 1) * 512, :].rearrange("(c p) n -> p c n", p=128)), writes=[wub], semkey="wub")
        S.do("sp", dma(wo.ap, wb_o[l * D:(l + 1) * D, :].rearrange("(c p) n -> p c n", p=128)), writes=[wo], semkey="wo")
        gt = 0
        for sn, N in seqs:
            sc = scr[sn]
            xsrc = x_in[sn] if l == 0 else sc["X2"]
            for t in range(N // 512):
                oa_, ob_, g_, x_ = oat[gt % 2], obt[gt % 2], gtt[gt % 2], xc[gt % 2]
                gt += 1
                S.do("sp", dma(oa_.ap, sc["OA"][t * 512:(t + 1) * 512, :].rearrange("(j p) e -> p j e", p=128)), writes=[oa_], semkey=oa_.key)
                S.do("sp", dma(ob_.ap, sc["OB"][t * 512:(t + 1) * 512, :].rearrange("(j p) e -> p j e", p=128)), writes=[ob_], semkey=ob_.key)
                S.do("sp", dma(g_.ap, sc["GT"][:, t * 512:(t + 1) * 512].rearrange("(b p) n -> p b n", p=128)), writes=[g_], semkey=g_.key)
                S.do("sp", dma(x_.ap, xsrc[t * 512:(t + 1) * 512, :].rearrange("(j p) d -> p j d", p=128)), writes=[x_], semkey=x_.key)
                ws = []
                prev = list(oT.r) + list(oT.w)
                for cb8 in range(8):
                    srcb = oa_ if cb8 < 4 else ob_
                    cbk = cb8 % 4
                    pb = pbank[cb8 % 2]
                    pv = pb.ap.bitcast(BF16)

                    def f(e, srcb=srcb, cbk=cbk, pv=pv):
                        last = None
                        for j in range(4):
                            last = e.transpose(pv[:, j * 128:(j + 1) * 128], srcb.ap[:, j, cbk * 128:(cbk + 1) * 128], ident.ap)
                        return last
                    S.do("pe", f, reads=[srcb, ident], writes=[pb])
                    if cb8 % 2 == 0:
                        ws.append(S.do("act", (lambda cb8=cb8, pv=pv: lambda e: e.copy(out=oT.ap[:, cb8, :], in_=pv[:, 0:512]))(), reads=[pb], extra=prev))
                    else:
                        ws.append(S.do("dve", (lambda cb8=cb8, pv=pv: lambda e: e.tensor_copy(out=oT.ap[:, cb8, :], in_=pv[:, 0:512]))(), reads=[pb], extra=prev))
                oT.w = ws
                oT.r = []
                ws = []
                prev = list(uT.r) + list(uT.w)
                for fb in range(8):
                    pa_, pb_ = pbank[2 + 2 * (fb % 2)], pbank[3 + 2 * (fb % 2)]
                    t1_, t2_ = u1[fb % 2], u2[fb % 2]

                    def f(e, fb=fb, pa_=pa_, pb_=pb_):
                        last = None
                        for c in range(4):
                            last = e.matmul(pa_.ap, lhsT=wua.ap[:, c, fb * 128:(fb + 1) * 128], rhs=oT.ap[:, c, :], start=(c == 0), stop=(c == 3))
                        for c in range(4):
                            last = e.matmul(pb_.ap, lhsT=wub.ap[:, c, fb * 128:(fb + 1) * 128], rhs=oT.ap[:, 4 + c, :], start=(c == 0), stop=(c == 3))
                        return last
                    S.do("pe", f, reads=[wua, wub, oT], writes=[pa_, pb_])
                    S.do("dve", (lambda fb=fb, pa_=pa_, t1_=t1_, g_=g_: lambda e: e.tensor_tensor(out=t1_.ap, in0=pa_.ap, in1=g_.ap[:, fb, :], op=ALU.mult))(), reads=[pa_, g_], writes=[t1_])
                    S.do("dve", (lambda fb=fb, pb_=pb_, t2_=t2_, g_=g_: lambda e: e.tensor_tensor(out=t2_.ap, in0=pb_.ap, in1=g_.ap[:, 8 + fb, :], op=ALU.mult))(), reads=[pb_, g_], writes=[t2_])
                    ws.append(S.do("pool", (lambda fb=fb, t1_=t1_, t2_=t2_: lambda e: e.tensor_tensor(out=uT.ap[:, fb, :], in0=t1_.ap, in1=t2_.ap, op=ALU.add))(), reads=[t1_, t2_], extra=prev))
                uT.w = ws
                uT.r = []
                ws = []
                for j in range(4):
                    for hf in range(2):
                        pb = pbank[6 + hf]

                        def f(e, j=j, hf=hf, pb=pb):
                            last = None
                            for c in range(8):
                                last = e.matmul(pb.ap, lhsT=uT.ap[:, c, j * 128:(j + 1) * 128], rhs=wo.ap[:, c, hf * 512:(hf + 1) * 512], start=(c == 0), stop=(c == 7))
                            return last
                        S.do("pe", f, reads=[uT, wo], writes=[pb])
                        ws.append(S.do("dve", (lambda j=j, hf=hf, pb=pb, x_=x_: lambda e: e.tensor_tensor(out=x_.ap[:, j, hf * 512:(hf + 1) * 512], in0=pb.ap, in1=x_.ap[:, j, hf * 512:(hf + 1) * 512], op=ALU.add))(), reads=[pb, x_]))
                x_.w = x_.w + ws
                S.do("sp", dma(sc["X1"][t * 512:(t + 1) * 512, :].rearrange("(j p) d -> p j d", p=128), x_.ap), reads=[x_], semkey=x_.key + "s")
        S.barrier()

        AD = Alloc(PBASE)
        w1 = AD.t([128, 8, DFF], BF16, "w1")
        w2 = AD.t([128, 32, D], BF16, "w2")
        g2b = AD.t([128, D], F32, "g2b")
        gfb = AD.t([128, D], F32, "gfb")
        xd = [AD.t([128, 2, D], F32, f"xd{i}") for i in range(3)]
        junk2 = AD.t([128, 2, D], F32, "junk2")
        h2 = AD.t([128, 2, D], BF16, "h2")
        h2T = AD.t([128, 8, 256], BF16, "h2T")
        aT = AD.t([128, 32, 256], BF16, "aT")
        rr = [AD.t([128, 256], F32, f"rr{i}") for i in range(2)]
        ssd = AD.t([128, 2], F32, "ssd")
        rsd = AD.t([128, 2], F32, "rsd")
        ssf = AD.t([128, 2], F32, "ssf")
        rsf = [AD.t([128, 2], F32, f"rsf{i}") for i in range(2)]
        S.do("sp", dma(w1.ap, wb_f1[l * D:(l + 1) * D, :].rearrange("(c p) n -> p c n", p=128)), writes=[w1], semkey="w1")
        for q4 in range(4):
            o = S.do("sp", dma(w2.ap[:, q4 * 8:(q4 + 1) * 8, :], wb_f2[l * DFF + q4 * 1024:l * DFF + (q4 + 1) * 1024, :].rearrange("(c p) n -> p c n", p=128)), semkey="w2")
        w2.w = [o]
        S.do("sp", dma(g2b.ap, norm2[l].partition_broadcast(128)), writes=[g2b], semkey="g2b")
        S.do("sp", dma(gfb.ap, norm_f.partition_broadcast(128)), writes=[gfb], semkey="gfb")
        tilesD = []
        for sn, N in seqs:
            for t in range(N // 256):
                tilesD.append((sn, t))
        last_layer = (l == nlayers - 1)

        def d_load(i):
            sn, t = tilesD[i]
            x_ = xd[i % 3]
            S.do("sp", dma(x_.ap, scr[sn]["X1"][t * 256:(t + 1) * 256, :].rearrange("(j p) d -> p j d", p=128)), writes=[x_], semkey=x_.key)

        def d_norm_a(i):
            x_ = xd[i % 3]
            S.do("pool", (lambda x_=x_: lambda e: e.tensor_tensor(out=junk2.ap, in0=x_.ap, in1=x_.ap, op=ALU.mult))(), reads=[x_], writes=[junk2])
            S.do("dve", lambda e: e.tensor_reduce(out=ssd.ap, in_=junk2.ap, axis=AX.X, op=ALU.add), reads=[junk2], writes=[ssd])
            S.do("dve", lambda e: e.tensor_scalar(out=rsd.ap, in0=ssd.ap, scalar1=1.0 / D, scalar2=EPS, op0=ALU.mult, op1=ALU.add), reads=[ssd], writes=[rsd])

        def d_norm_b(i):
            x_ = xd[i % 3]
            rsqrt_inplace(rsd, lambda: rsd.ap)
            ws = []
            prev = list(h2.r) + list(h2.w)
            for j in range(2):
                ws.append(S.do("dve", (lambda j=j, x_=x_: lambda e: e.scalar_tensor_tensor(out=h2.ap[:, j, :], in0=x_.ap[:, j, :], scalar=rsd.ap[:, j:j + 1], in1=g2b.ap, op0=ALU.mult, op1=ALU.mult))(),
                               reads=[x_, rsd, g2b], extra=prev))
            h2.w = ws
            h2.r = []

        def d_tr(i):
            ws = []
            prev = list(h2T.r) + list(h2T.w)
            for c in range(8):
                pb = pbank[c % 2]
                pv = pb.ap.bitcast(BF16)

                def f(e, c=c, pv=pv):
                    last = None
                    for j in range(2):
                        last = e.transpose(pv[:, j * 128:(j + 1) * 128], h2.ap[:, j, c * 128:(c + 1) * 128], ident.ap)
                    return last
                S.do("pe", f, reads=[h2, ident], writes=[pb])
                if c % 2 == 0:
                    ws.append(S.do("act", (lambda c=c, pv=pv: lambda e: e.copy(out=h2T.ap[:, c, :], in_=pv[:, 0:256]))(), reads=[pb], extra=prev))
                else:
                    ws.append(S.do("dve", (lambda c=c, pv=pv: lambda e: e.tensor_copy(out=h2T.ap[:, c, :], in_=pv[:, 0:256]))(), reads=[pb], extra=prev))
            h2T.w = ws
            h2T.r = []

        aT_state = {"ws": [], "prev": []}

        def d_ffn1(i, fbs):
            if fbs[0] == 0:
                aT_state["ws"] = []
                aT_state["prev"] = list(aT.r) + list(aT.w)
            for fb in fbs:
                pb = pbank[2 + fb % 4]
                r_ = rr[fb % 2]

                def f(e, fb=fb, pb=pb):
                    last = None
                    for c in range(8):
                        last = e.matmul(pb.ap[:, 0:256], lhsT=w1.ap[:, c, fb * 128:(fb + 1) * 128], rhs=h2T.ap[:, c, :], start=(c == 0), stop=(c == 7))
                    return last
                S.do("pe", f, reads=[w1, h2T], writes=[pb])
                S.do("act", (lambda pb=pb, r_=r_: lambda e: e.activation(out=r_.ap, in_=pb.ap[:, 0:256], func=AF.Relu))(), reads=[pb], writes=[r_])
                aT_state["ws"].append(S.do("pool" if fb % 2 == 0 else "dve", (lambda fb=fb, r_=r_: lambda e: e.tensor_tensor(out=aT.ap[:, fb, :], in0=r_.ap, in1=r_.ap, op=ALU.mult))(), reads=[r_], extra=aT_state["prev"]))
            if fbs[-1] == 31:
                aT.w = aT_state["ws"]
                aT.r = []

        def d_ffn2(i):
            sn, t = tilesD[i]
            x_ = xd[i % 3]
            ws = []
            for j in range(2):
                for hf in range(2):
                    pb = pbank[6 + hf]

                    def f(e, j=j, hf=hf, pb=pb):
                        last = None
                        for c in range(32):
                            last = e.matmul(pb.ap, lhsT=aT.ap[:, c, j * 128:(j + 1) * 128], rhs=w2.ap[:, c, hf * 512:(hf + 1) * 512], start=(c == 0), stop=(c == 31))
                        return last
                    S.do("pe", f, reads=[aT, w2], writes=[pb])
                    ws.append(S.do("dve", (lambda j=j, hf=hf, pb=pb, x_=x_: lambda e: e.tensor_tensor(out=x_.ap[:, j, hf * 512:(hf + 1) * 512], in0=pb.ap, in1=x_.ap[:, j, hf * 512:(hf + 1) * 512], op=ALU.add))(), reads=[pb, x_]))
            x_.w = x_.w + ws
            if (not last_layer) or debug:
                S.do("sp", dma(scr[sn]["X2"][t * 256:(t + 1) * 256, :].rearrange("(j p) d -> p j d", p=128), x_.ap), reads=[x_], semkey=x_.key + "s")
            if last_layer:
                rs_ = rsf[i % 2]
                S.do("pool", (lambda x_=x_: lambda e: e.tensor_tensor(out=junk2.ap, in0=x_.ap, in1=x_.ap, op=ALU.mult))(), reads=[x_], writes=[junk2])
                S.do("dve", lambda e: e.tensor_reduce(out=ssf.ap, in_=junk2.ap, axis=AX.X, op=ALU.add), reads=[junk2], writes=[ssf])
                S.do("dve", (lambda rs_=rs_: lambda e: e.tensor_scalar(out=rs_.ap, in0=ssf.ap, scalar1=1.0 / D, scalar2=EPS, op0=ALU.mult, op1=ALU.add))(), reads=[ssf], writes=[rs_])

        def d_final_b(i):
            sn, t = tilesD[i]
            x_ = xd[i % 3]
            rs_ = rsf[i % 2]
            rsqrt_inplace(rs_, (lambda rs_=rs_: lambda: rs_.ap)())
            ws = []
            for j in range(2):
                ws.append(S.do("dve", (lambda j=j, x_=x_, rs_=rs_: lambda e: e.scalar_tensor_tensor(out=x_.ap[:, j, :], in0=x_.ap[:, j, :], scalar=rs_.ap[:, j:j + 1], in1=gfb.ap, op0=ALU.mult, op1=ALU.mult))(),
                               reads=[x_, rs_, gfb]))
            x_.w = x_.w + ws
            S.do("sp", dma(y_out[sn][t * 256:(t + 1) * 256, :].rearrange("(j p) d -> p j d", p=128), x_.ap), reads=[x_], semkey="yo")

        nTD = len(tilesD)
        d_load(0)
        d_norm_a(0)
        d_norm_b(0)
        d_tr(0)
        for i in range(nTD):
            nxt = i + 1 < nTD
            if nxt:
                d_load(i + 1)
            d_ffn1(i, list(range(0, 16)))
            if nxt:
                d_norm_a(i + 1)
            d_ffn1(i, list(range(16, 32)))
            if nxt:
                d_norm_b(i + 1)
            if last_layer and i > 0:
                d_final_b(i - 1)
            d_ffn2(i)
            if nxt:
                d_tr(i + 1)
        if last_layer:
            d_final_b(nTD - 1)
        S.barrier()

    keys = S.finalize()
    assert len(keys) <= 100, len(keys)
    sems = {k: nc.alloc_semaphore(name="s_" + "_".join(map(str, k))) for k in keys}
    with nc.Block() as block:
        @block.tensor
        def _(e):
            S.emit("pe", e, sems)

        @block.scalar
        def _(e):
            S.emit("act", e, sems)

        @block.vector
        def _(e):
            S.emit("dve", e, sems)

        @block.gpsimd
        def _(e):
            S.emit("pool", e, sems)

        @block.sync
        def _(e):
            S.emit("sp", e, sems)
    return nc, {e: len(S.q[e]) for e in S.ENGS}, len(keys)


def _consts():
    import jax
    import jax.numpy as jnp
    bf = ml_dtypes.bfloat16
    ident = np.eye(128, dtype=np.float32).astype(bf)
    jmat = np.eye(128, dtype=np.float32)[::-1].copy().astype(bf)
    with jax.default_device(jax.devices("cpu")[0]):
        n = 8192
        row = jnp.repeat(jnp.arange(n // 64, dtype=jnp.int32), 64).astype(jnp.float32)
        col = jnp.tile(jnp.arange(64, dtype=jnp.int32), n // 64).astype(jnp.float32)
        inv = 10000.0 ** (-jnp.arange(0, 32, 2, dtype=jnp.float32) / 32)
        ar = row[:, None] * inv[None, :]
        ac = col[:, None] * inv[None, :]
        cr, sr, cc, sc_ = (np.asarray(jnp.cos(ar)), np.asarray(jnp.sin(ar)), np.asarray(jnp.cos(ac)), np.asarray(jnp.sin(ac)))
        C = np.concatenate([cr, cr, cc, cc], axis=1)
        Sg = np.concatenate([-sr, sr, -sc_, sc_], axis=1)
        rope = np.ascontiguousarray(np.concatenate([C, Sg], axis=1).astype(np.float32))
        rel = 639 - jnp.arange(1280, dtype=jnp.int32)
        nb = 16
        max_exact = 8
        ret = (rel > 0).astype(jnp.int32) * nb
        na = jnp.abs(rel)
        nf = jnp.maximum(na, 1).astype(jnp.float32)
        large = max_exact + (jnp.log(nf / max_exact) / math.log(128 / max_exact) * (nb - max_exact)).astype(jnp.int32)
        large = jnp.minimum(large, nb - 1)
        bucket = np.asarray(ret + jnp.where(na < max_exact, na, large))
    oh = np.zeros((32, 1280), np.float32)
    oh[bucket, np.arange(1280)] = 1.0
    return ident, jmat, rope, oh


_CACHE = {}


def _run(inputs, NPR, NSA, debug=False, nlayers=NL):
    key = (NPR, NSA, debug, nlayers)
    if key not in _CACHE:
        _CACHE[key] = build(NPR, NSA, debug, nlayers)
    nc, nops, nsem = _CACHE[key]
    ident, jmat, rope, oh = _consts()
    f = lambda a: np.ascontiguousarray(np.asarray(a, dtype=np.float32))
    shared = {
        "w_in": f(inputs["w_in"]).reshape(NL * D, INC),
        "w_up_a": f(inputs["w_up_a"]).reshape(NL * 512, D),
        "w_up_b": f(inputs["w_up_b"]).reshape(NL * 512, D),
        "w_o": f(inputs["w_o"]).reshape(NL * D, D),
        "w_ff1": f(inputs["w_ff1"]).reshape(NL * D, DFF),
        "w_ff2": f(inputs["w_ff2"]).reshape(NL * DFF, D),
        "norm1": f(inputs["norm1"]), "norm2": f(inputs["norm2"]), "norm_f": f(inputs["norm_f"]),
        "b_gate": np.ascontiguousarray(f(inputs["b_gate"]).reshape(NL, 16, 128).transpose(0, 2, 1)),
        "lamv": np.ascontiguousarray(np.stack([f(inputs["lam_q1"]), f(inputs["lam_k1"]), f(inputs["lam_q2"]), f(inputs["lam_k2"])], axis=1)),
        "subln_g": f(inputs["subln_g"]), "qk_norm_q": f(inputs["qk_norm_q"]), "qk_norm_k": f(inputs["qk_norm_k"]),
        "t5_table": f(inputs["t5_table"]),
        "ident": ident, "jmat": jmat, "rope": rope, "oh": oh,
    }
    xp = f(inputs["x_prompt"])
    xs = f(inputs["x_sample"])
    in_maps = []
    for c in range(8):
        m = dict(shared)
        m["xp"] = xp[c]
        m["xs"] = xs[c]
        in_maps.append(m)
    res = run_bass_kernel_spmd(nc, in_maps, core_ids=list(range(8)))
    return res


def kernel(**inputs):
    xp = np.asarray(inputs["x_prompt"])
    xs = np.asarray(inputs["x_sample"])
    res = _run(inputs, xp.shape[1], xs.shape[1])
    yp = np.stack([np.asarray(r["yp"], dtype=np.float32) for r in res.results], axis=0)
    ys = np.stack([np.asarray(r["ys"], dtype=np.float32) for r in res.results], axis=0)
    return (yp, ys)
```

```python
import math
import numpy as np
import ml_dtypes
import concourse.bass as bass
import concourse.mybir as mybir
from concourse.bass_utils import run_bass_kernel_spmd

F32 = mybir.dt.float32
BF16 = mybir.dt.bfloat16
AF = mybir.ActivationFunctionType
ALU = mybir.AluOpType
AX = mybir.AxisListType

D = 1024
INC = 4352
DFF = 4096
EPS = 1e-6
SUBLN_EPS = 1e-5
NL = 2


class Op:
    __slots__ = ("eng", "fn", "deps", "sig", "semkey", "waits", "phase")

    def __init__(self, eng, fn, deps, semkey, phase):
        self.eng = eng
        self.fn = fn
        self.deps = deps
        self.sig = None
        self.semkey = semkey
        self.waits = None
        self.phase = phase


class Buf:
    __slots__ = ("ap", "w", "r", "key")

    def __init__(self, ap, key=None):
        self.ap = ap
        self.w = []
        self.r = []
        self.key = key


class Sched:
    ENGS = ("pe", "act", "dve", "pool", "sp")

    def __init__(self):
        self.q = {e: [] for e in self.ENGS}
        self.phase = 0
        self.last_dma = {}

    def do(self, eng, fn, reads=(), writes=(), extra=(), semkey=None):
        deps = list(extra)
        for b in reads:
            deps += b.w
        for b in writes:
            deps += b.w
            deps += b.r
        if eng == "pe":
            deps = [d for d in deps if d.eng != "pe"]
        o = Op(eng, fn, deps, semkey, self.phase)
        self.q[eng].append(o)
        for b in reads:
            b.r.append(o)
        for b in writes:
            b.w = [o]
            b.r = []
        if semkey is not None:
            self.last_dma[semkey] = o
        return o

    def barrier(self):
        deps = []
        for e in self.ENGS:
            if self.q[e]:
                for o in reversed(self.q[e]):
                    if o.fn is not None and o.semkey is None:
                        deps.append(o)
                        break
        deps += list(self.last_dma.values())
        for e in self.ENGS:
            o = Op(e, None, list(deps), None, self.phase)
            self.q[e].append(o)
        self.phase += 1

    def finalize(self):
        needed = set()
        for e in self.ENGS:
            for o in self.q[e]:
                for d in o.deps:
                    needed.add(id(d))
        counts = {}
        for e in self.ENGS:
            for o in self.q[e]:
                if o.fn is None:
                    continue
                if o.semkey is not None:
                    key = ("dma", o.semkey)
                    inc = 16
                elif id(o) in needed:
                    ph = o.phase
                    if ph == 0:
                        pg = "C-1"
                    else:
                        idx, lay = (ph - 1) % 4, (ph - 1) // 4
                        pg = f"B{lay}" if idx == 1 else (f"C{lay - 1}" if idx == 0 else f"C{lay}")
                    key = ("eng", e, pg)
                    inc = 1
                else:
                    continue
                counts[key] = counts.get(key, 0) + inc
                o.sig = (key, inc, counts[key])
        for e in self.ENGS:
            waited = {}
            for o in self.q[e]:
                w = {}
                for d in o.deps:
                    key, inc, val = d.sig
                    if waited.get(key, 0) >= val:
                        continue
                    if w.get(key, 0) < val:
                        w[key] = val
                waited.update(w)
                o.waits = w
        return list(counts.keys())

    def emit(self, name, eng, sems):
        for o in self.q[name]:
            for k, v in o.waits.items():
                eng.wait_ge(sems[k], v)
            if o.fn is None:
                continue
            ins = o.fn(eng)
            if o.sig is not None:
                ins.then_inc(sems[o.sig[0]], o.sig[1])


def build(NPR, NSA, debug=False, nlayers=NL):
    nc = bass.Bass("TRN2", target_bir_lowering=False)
    S = Sched()
    seqs = [("p", NPR), ("s", NSA)]
    NMAX = max(NPR, NSA)

    def din(name, shape, dt=F32):
        return nc.dram_tensor(name, list(shape), dt, kind="ExternalInput").ap()

    def dscr(name, shape, dt):
        kind = "ExternalOutput" if (debug and name.startswith("dbg_")) else "Internal"
        return nc.dram_tensor(name, list(shape), dt, kind=kind)

    x_in = {"p": din("xp", [NPR, D]), "s": din("xs", [NSA, D])}
    y_out = {"p": nc.dram_tensor("yp", [NPR, D], F32, kind="ExternalOutput").ap(),
             "s": nc.dram_tensor("ys", [NSA, D], F32, kind="ExternalOutput").ap()}
    w_in = din("w_in", [NL * D, INC])
    w_ua = din("w_up_a", [NL * 512, D])
    w_ub = din("w_up_b", [NL * 512, D])
    w_o = din("w_o", [NL * D, D])
    w_f1 = din("w_ff1", [NL * D, DFF])
    w_f2 = din("w_ff2", [NL * DFF, D])
    norm1 = din("norm1", [NL, D])
    norm2 = din("norm2", [NL, D])
    norm_f = din("norm_f", [D])
    b_gate = din("b_gate", [NL, 128, 16])
    lamv = din("lamv", [NL, 4, 64])
    subln = din("subln_g", [NL, 128])
    qkq = din("qk_norm_q", [NL, 64])
    qkk = din("qk_norm_k", [NL, 64])
    t5 = din("t5_table", [32, 4])
    ident_d = din("ident", [128, 128], BF16)
    jmat_d = din("jmat", [128, 128], BF16)
    rope_d = din("rope", [8192, 128])
    oh_d = din("oh", [32, 1280])

    wb_in = dscr("wb_in", [NL * D, INC], BF16).ap()
    wb_ua = dscr("wb_ua", [NL * 512, D], BF16).ap()
    wb_ub = dscr("wb_ub", [NL * 512, D], BF16).ap()
    wb_o = dscr("wb_o", [NL * D, D], BF16).ap()
    wb_f1 = dscr("wb_f1", [NL * D, DFF], BF16).ap()
    wb_f2 = dscr("wb_f2", [NL * DFF, D], BF16).ap()
    Fd_h = dscr("Fd", [4, 1280], F32)
    Xd = dscr("Xd", [2, 128, 4 * 1152], BF16).ap()
    scr = {}
    for sn, N in seqs:
        scr[sn] = dict(
            QAT=dscr(f"dbg_QAT_{sn}", [512, N], BF16).ap(),
            KAT=dscr(f"dbg_KAT_{sn}", [512, N], BF16).ap(),
            VA=dscr(f"dbg_VA_{sn}", [N, 512], BF16).ap(),
            QBT=dscr(f"dbg_QBT_{sn}", [512, N], BF16).ap(),
            KBT=dscr(f"dbg_KBT_{sn}", [128, N], BF16).ap(),
            VB=dscr(f"dbg_VB_{sn}", [N, 128], BF16).ap(),
            GT=dscr(f"dbg_GT_{sn}", [2048, N], BF16).ap(),
            OA=dscr(f"dbg_OA_{sn}", [N, 512], BF16).ap(),
            OB=dscr(f"dbg_OB_{sn}", [N, 512], BF16).ap(),
            X1=dscr(f"dbg_X1_{sn}", [N, D], F32).ap(),
            X2=dscr(f"dbg_X2_{sn}", [N, D], F32).ap(),
        )

    SB0 = (int(nc.sbuf_base) + 63) // 64 * 64
    SBTOP = int(nc.sbuf_top)

    class Alloc:
        def __init__(self, base):
            self.off = base

        def t(self, shape, dt, key=None):
            nbytes = int(np.prod(shape[1:])) * (4 if dt == F32 else 2)
            nbytes = (nbytes + 63) // 64 * 64
            nm = f"sb{Alloc.cnt}"
            Alloc.cnt += 1
            h = nc.alloc_sbuf_tensor_at(nm, list(shape), dt, offset=self.off)
            self.off += nbytes
            assert self.off <= SBTOP, ("SBUF overflow", self.off)
            return Buf(h.ap(), key or nm)
    Alloc.cnt = 0

    psum_all = nc.alloc_psum_tensor("psall", [128, 8, 512], F32).ap()
    pbank = [Buf(psum_all[:, i, :], f"bank{i}") for i in range(8)]

    def dma(out, in_, **kw):
        return lambda e: e.dma_start(out=out, in_=in_, **kw)

    PA = Alloc(SB0)
    ident = PA.t([128, 128], BF16, "ident")
    jmat = PA.t([128, 128], BF16, "jmat")
    cb = PA.t([128, 8], F32, "cb")
    bg = PA.t([128, 16], F32, "bg")
    gq8 = PA.t([128, 64], F32, "gq8")
    gqs8 = PA.t([128, 64], F32, "gqs8")
    gk = PA.t([128, 64], F32, "gk")
    gks = PA.t([128, 64], F32, "gks")
    gsub = PA.t([128, 128], F32, "gsub")
    lam4 = PA.t([128, 4, 64], F32, "lam4")
    lamt = PA.t([128, 2, 64], F32, "lamt")
    lams = PA.t([128, 2], F32, "lams")
    lame = PA.t([128, 2], F32, "lame")
    neglam = PA.t([128, 1], F32, "neglam")
    PBASE = PA.off

    S.do("sp", dma(ident.ap, ident_d), writes=[ident], semkey="c_ident")
    S.do("sp", dma(jmat.ap, jmat_d), writes=[jmat], semkey="c_jmat")
    S.do("sp", dma(cb.ap[:, 0:4], t5[15].partition_broadcast(128)), writes=[cb], semkey="c_cb")
    S.do("sp", dma(cb.ap[:, 4:8], t5[31].partition_broadcast(128)), writes=[cb], semkey="c_cb")

    A0 = Alloc(PBASE)
    CW = 4352
    stg32 = [A0.t([128, CW], F32, f"stg32_{i}") for i in range(2)]
    stg16 = [A0.t([128, CW], BF16, f"stg16_{i}") for i in range(2)]
    chunks = []
    for src, dst in ((w_in, wb_in), (w_ua, wb_ua), (w_ub, wb_ub), (w_o, wb_o), (w_f1, wb_f1), (w_f2, wb_f2)):
        R, C = src.shape
        g = max(1, CW // C)
        while (R // 128) % g != 0:
            g -= 1
        srcv = src.rearrange("(n p g) c -> n p (g c)", p=128, g=g)
        dstv = dst.rearrange("(n p g) c -> n p (g c)", p=128, g=g)
        for n in range(R // (128 * g)):
            chunks.append((srcv[n], dstv[n], g * C))

    def conv_load(ci, s32, q="sp"):
        a = s32[ci % 2]
        S.do(q, dma(a.ap[:, 0:chunks[ci][2]], chunks[ci][0]), writes=[a], semkey=a.key)

    def conv_cast(ci, s32, s16, eng):
        a, b_, W = s32[ci % 2], s16[ci % 2], chunks[ci][2]
        S.do(eng, (lambda a=a, b_=b_, W=W: lambda e: e.tensor_copy(out=b_.ap[:, 0:W], in_=a.ap[:, 0:W]))(), reads=[a], writes=[b_])

    def conv_store(ci, s16, q="sp"):
        b_ = s16[ci % 2]
        S.do(q, dma(chunks[ci][1], b_.ap[:, 0:chunks[ci][2]]), reads=[b_], semkey=b_.key + "s")

    NCH0 = 8
    for ci in range(NCH0):
        conv_load(ci, stg32)
        conv_cast(ci, stg32, stg16, "dve" if ci % 2 == 0 else "pool")
        conv_store(ci, stg16)

    t5sb = A0.t([32, 4], F32, "t5sb")
    ohsb = A0.t([32, 1280], F32, "ohsb")
    Fsb = A0.t([4, 1280], F32, "Fsb")
    X32 = A0.t([128, 4 * 1152], F32, "X32")
    Xhi = A0.t([128, 4 * 1152], BF16, "Xhi")
    Xr = A0.t([128, 4 * 1152], F32, "Xr")
    Xlo = A0.t([128, 4 * 1152], BF16, "Xlo")
    S.do("sp", dma(t5sb.ap, t5), writes=[t5sb], semkey="t5sb")
    S.do("sp", dma(ohsb.ap, oh_d), writes=[ohsb], semkey="ohsb")
    for c in range(3):
        w = 512 if c < 2 else 256
        S.do("pe", (lambda c=c, w=w: lambda e: e.matmul(pbank[c].ap[0:4, 0:w], lhsT=t5sb.ap[:, :], rhs=ohsb.ap[:, c * 512:c * 512 + w], start=True, stop=True))(),
             reads=[t5sb, ohsb], writes=[pbank[c]])
        S.do("dve", (lambda c=c, w=w: lambda e: e.tensor_copy(out=Fsb.ap[:, c * 512:c * 512 + w], in_=pbank[c].ap[0:4, 0:w]))(), reads=[pbank[c]], writes=[Fsb])
    S.do("sp", dma(Fd_h.ap(), Fsb.ap), reads=[Fsb], semkey="Fd")
    fdst = S.last_dma["Fd"]
    for h in range(4):
        S.do("sp", dma(X32.ap[:, h * 1152:(h + 1) * 1152], bass.AP(tensor=Fd_h, offset=h * 1280, ap=[[1, 128], [1, 1152]])),
             writes=[X32], extra=[fdst], semkey="X32")
    S.do("dve", lambda e: e.tensor_copy(out=Xhi.ap, in_=X32.ap), reads=[X32], writes=[Xhi])
    S.do("dve", lambda e: e.tensor_tensor(out=Xr.ap, in0=X32.ap, in1=Xhi.ap, op=ALU.subtract), reads=[X32, Xhi], writes=[Xr])
    S.do("dve", lambda e: e.tensor_copy(out=Xlo.ap, in_=Xr.ap), reads=[Xr], writes=[Xlo])
    S.do("sp", dma(Xd[0], Xhi.ap), reads=[Xhi], semkey="Xd")
    S.do("sp", dma(Xd[1], Xlo.ap), reads=[Xlo], semkey="Xd")
    S.barrier()

    def rsqrt_inplace(buf, apf):
        S.do("act", lambda e: e.activation(out=apf(), in_=apf(), func=AF.Sqrt), reads=[buf], writes=[buf])
        S.do("dve", lambda e: e.reciprocal(out=apf(), in_=apf()), reads=[buf], writes=[buf])

    for l in range(nlayers):
        lam_init = 0.8 - 0.6 * math.exp(-0.3 * l)
        S.do("sp", dma(bg.ap, b_gate[l]), writes=[bg], semkey="c_bg")
        S.do("sp", dma(gq8.ap, qkq[l].partition_broadcast(128)), writes=[gq8], semkey="c_gq")
        S.do("sp", dma(gk.ap, qkk[l].partition_broadcast(128)), writes=[gk], semkey="c_gk")
        S.do("sp", dma(gsub.ap, subln[l].partition_broadcast(128)), writes=[gsub], semkey="c_gsub")
        S.do("sp", dma(lam4.ap.rearrange("p a b -> p (a b)"), lamv[l].rearrange("a b -> (a b)").partition_broadcast(128)), writes=[lam4], semkey="c_lam")
        S.do("pool", lambda e: e.tensor_scalar(out=gq8.ap, in0=gq8.ap, scalar1=0.125, scalar2=None, op0=ALU.mult), reads=[gq8], writes=[gq8])
        S.do("pool", (lambda sc_=float(1.0 - lam_init): lambda e: e.tensor_scalar(out=gsub.ap, in0=gsub.ap, scalar1=sc_, scalar2=None, op0=ALU.mult))(), reads=[gsub], writes=[gsub])
        for (src, dst) in ((gq8, gqs8), (gk, gks)):
            for blk in range(4):
                sb = blk ^ 1
                S.do("pool", (lambda src=src, dst=dst, blk=blk, sb=sb: lambda e: e.tensor_copy(out=dst.ap[:, blk * 16:(blk + 1) * 16], in_=src.ap[:, sb * 16:(sb + 1) * 16]))(),
                     reads=[src], writes=[dst])
        S.do("dve", lambda e: e.tensor_tensor(out=lamt.ap[:, 0, :], in0=lam4.ap[:, 0, :], in1=lam4.ap[:, 1, :], op=ALU.mult), reads=[lam4], writes=[lamt])
        S.do("dve", lambda e: e.tensor_tensor(out=lamt.ap[:, 1, :], in0=lam4.ap[:, 2, :], in1=lam4.ap[:, 3, :], op=ALU.mult), reads=[lam4, lamt], writes=[lamt])
        S.do("dve", lambda e: e.tensor_reduce(out=lams.ap, in_=lamt.ap, axis=AX.X, op=ALU.add), reads=[lamt], writes=[lams])
        S.do("act", lambda e: e.activation(out=lame.ap, in_=lams.ap, func=AF.Exp), reads=[lams], writes=[lame])
        S.do("dve", lambda e: e.tensor_tensor(out=neglam.ap, in0=lame.ap[:, 1:2], in1=lame.ap[:, 0:1], op=ALU.subtract), reads=[lame], writes=[neglam])
        S.do("dve", (lambda li=lam_init: lambda e: e.tensor_scalar(out=neglam.ap, in0=neglam.ap, scalar1=float(-li), scalar2=None, op0=ALU.add))(), reads=[neglam], writes=[neglam])

        AA = Alloc(PBASE)
        wA = AA.t([128, 8, INC], BF16, "wA")
        g1b = AA.t([128, D], F32, "g1b")
        xin = [AA.t([128, 4, D], F32, f"xin{i}") for i in range(2)]
        junk = AA.t([128, 4, D], F32, "junk")
        ss4 = AA.t([128, 4], F32, "ss4")
        rstd4 = AA.t([128, 4], F32, "rstd4")
        hbf = AA.t([128, 4, D], BF16, "hbf")
        hT = [AA.t([128, 8, 512], BF16, f"hT{i}") for i in range(2)]
        fmst = [AA.t([128, 4, 512], BF16, f"fmst{i}") for i in range(4)]
        va_st = AA.t([128, 4, 512], BF16, "va_st")
        vb_st = AA.t([128, 4, 128], BF16, "vb_st")
        qbT_st = AA.t([128, 4, 512], BF16, "qbT_st")
        kbT_st = AA.t([128, 512], BF16, "kbT_st")
        ropet = [AA.t([128, 4, 128], F32, f"ropet{i}") for i in range(2)]
        tq = [AA.t([128, 512], F32, f"tq{i}") for i in range(3)]
        tk = [AA.t([128, 128], F32, f"tk{i}") for i in range(3)]
        qb_bf = [AA.t([128, 512], BF16, f"qb_bf{i}") for i in range(4)]
        kb_bf = [AA.t([128, 128], BF16, f"kb_bf{i}") for i in range(4)]
        CqSq = AA.t([128, 128], F32, "CqSq")
        CkSk = AA.t([128, 128], F32, "CkSk")
        ss8 = AA.t([128, 8], F32, "ss8")

        S.do("sp", dma(wA.ap, wb_in[l * D:(l + 1) * D, :].rearrange("(c p) n -> p c n", p=128)), writes=[wA], semkey="wA")
        S.do("sp", dma(g1b.ap, norm1[l].partition_broadcast(128)), writes=[g1b], semkey="g1b")

        tilesA = []
        for sn, N in seqs:
            for t in range(N // 512):
                tilesA.append((sn, t))
        fmi_box = [0]

        def a_load(i):
            sn, t = tilesA[i]
            sc = scr[sn]
            xsrc = x_in[sn] if l == 0 else sc["X2"]
            xb = xin[i % 2]
            rp = ropet[i % 2]
            S.do("sp", dma(xb.ap, xsrc[t * 512:(t + 1) * 512, :].rearrange("(j p) d -> p j d", p=128)), writes=[xb], semkey=xb.key)
            S.do("sp", dma(rp.ap, rope_d[t * 512:(t + 1) * 512, :].rearrange("(j p) d -> p j d", p=128)), writes=[rp], semkey=rp.key)

        t0q = [tq[0]] + [AA.t([128, 512], F32, f"t0q{j}") for j in range(1, 4)]
        t0k = [tk[0]] + [AA.t([128, 128], F32, f"t0k{j}") for j in range(1, 4)]
        ssA = AA.t([128, 4, 10], F32, "ssA")

        def a_norm_a(i):
            xb = xin[i % 2]
            S.do("pool", (lambda xb=xb: lambda e: e.tensor_tensor(out=junk.ap, in0=xb.ap, in1=xb.ap, op=ALU.mult))(), reads=[xb], writes=[junk])
            S.do("dve", lambda e: e.tensor_reduce(out=ss4.ap, in_=junk.ap, axis=AX.X, op=ALU.add), reads=[junk], writes=[ss4])
            S.do("dve", lambda e: e.tensor_scalar(out=rstd4.ap, in0=ss4.ap, scalar1=1.0 / D, scalar2=EPS, op0=ALU.mult, op1=ALU.add), reads=[ss4], writes=[rstd4])

        def a_norm_b(i):
            xb = xin[i % 2]
            rsqrt_inplace(rstd4, lambda: rstd4.ap)
            ws = []
            prev = list(hbf.r) + list(hbf.w)
            for j in range(4):
                ws.append(S.do("dve", (lambda j=j, xb=xb: lambda e: e.scalar_tensor_tensor(out=hbf.ap[:, j, :], in0=xb.ap[:, j, :], scalar=rstd4.ap[:, j:j + 1], in1=g1b.ap, op0=ALU.mult, op1=ALU.mult))(),
                               reads=[xb, rstd4, g1b], extra=prev))
            hbf.w = ws
            hbf.r = []

        def a_tr(i):
            hTb = hT[i % 2]
            ws = []
            prev = list(hTb.r) + list(hTb.w)
            for c in range(8):
                pb = pbank[c % 2]
                pv = pb.ap.bitcast(BF16)

                def f(e, c=c, pv=pv):
                    last = None
                    for j in range(4):
                        last = e.transpose(pv[:, j * 128:(j + 1) * 128], hbf.ap[:, j, c * 128:(c + 1) * 128], ident.ap)
                    return last
                S.do("pe", f, reads=[hbf, ident], writes=[pb])
                if c % 2 == 0:
                    o = S.do("act", (lambda c=c, pv=pv, hTb=hTb: lambda e: e.copy(out=hTb.ap[:, c, :], in_=pv[:, 0:512]))(), reads=[pb], extra=prev)
                else:
                    o = S.do("dve", (lambda c=c, pv=pv, hTb=hTb: lambda e: e.tensor_copy(out=hTb.ap[:, c, :], in_=pv[:, 0:512]))(), reads=[pb], extra=prev)
                ws.append(o)
            hTb.w = ws
            hTb.r = []

        def a_tm1(i):
            hTb = hT[i % 2]
            vaw, vbw = [], []
            prev_va = list(va_st.r) + list(va_st.w)
            prev_vb = list(vb_st.r) + list(vb_st.w)
            ssw = []
            prev_ss = list(ssA.r) + list(ssA.w)
            for j in range(4):
                def mmtm(pb, c0, w, j=j, hTb=hTb):
                    def f(e):
                        last = None
                        for c in range(8):
                            last = e.matmul(pb.ap[:, 0:w], lhsT=hTb.ap[:, c, j * 128:(j + 1) * 128], rhs=wA.ap[:, c, c0:c0 + w], start=(c == 0), stop=(c == 7))
                        return last
                    return f
                pb = pbank[4]
                S.do("pe", mmtm(pb, 1024, 512), reads=[wA, hTb], writes=[pb])
                vaw.append(S.do("act", (lambda j=j, pb=pb: lambda e: e.copy(out=va_st.ap[:, j, :], in_=pb.ap))(), reads=[pb], extra=prev_va))
                for (nh, pbi, c0, w, t0, t1, h0) in ((8, 5, 1536, 512, t0q[j], tq[1], 0), (2, 6, 2048, 256, t0k[j], tk[1], 8)):
                    pb = pbank[pbi]
                    W = nh * 64
                    S.do("pe", mmtm(pb, c0, w), reads=[wA, hTb], writes=[pb])
                    if nh == 2:
                        vbw.append(S.do("act", (lambda j=j, pb=pb: lambda e: e.copy(out=vb_st.ap[:, j, :], in_=pb.ap[:, 128:256]))(), reads=[pb], extra=prev_vb))
                    S.do("act", (lambda pb=pb, t0=t0, W=W: lambda e: e.copy(out=t0.ap[:, 0:W], in_=pb.ap[:, 0:W]))(), reads=[pb], writes=[t0])
                    S.do("pool", (lambda t0=t0, t1=t1, W=W: lambda e: e.tensor_tensor(out=t1.ap[:, 0:W], in0=t0.ap[:, 0:W], in1=t0.ap[:, 0:W], op=ALU.mult))(), reads=[t0], writes=[t1])
                    ssw.append(S.do("dve", (lambda t1=t1, nh=nh, W=W, j=j, h0=h0: lambda e: e.tensor_reduce(out=ssA.ap[:, j, h0:h0 + nh], in_=t1.ap[:, 0:W].rearrange("p (h d) -> p h d", d=64), axis=AX.X, op=ALU.add))(),
                                    reads=[t1], extra=prev_ss))
            va_st.w = vaw
            va_st.r = []
            vb_st.w = vbw
            vb_st.r = []
            ssA.w = ssw
            ssA.r = []

        def a_qkrs(i):
            S.do("dve", lambda e: e.tensor_scalar(out=ssA.ap, in0=ssA.ap, scalar1=1.0 / 64, scalar2=EPS, op0=ALU.mult, op1=ALU.add), reads=[ssA], writes=[ssA])
            rsqrt_inplace(ssA, lambda: ssA.ap)

        def a_st3(i):
            rp = ropet[i % 2]
            for j in range(4):
                S.do("pool", (lambda j=j, rp=rp: lambda e: e.tensor_tensor(out=CqSq.ap[:, 0:64], in0=rp.ap[:, j, 0:64], in1=gq8.ap, op=ALU.mult))(), reads=[rp, gq8], writes=[CqSq])
                S.do("pool", (lambda j=j, rp=rp: lambda e: e.tensor_tensor(out=CqSq.ap[:, 64:128], in0=rp.ap[:, j, 64:128], in1=gqs8.ap, op=ALU.mult))(), reads=[rp, gqs8, CqSq], writes=[CqSq])
                S.do("pool", (lambda j=j, rp=rp: lambda e: e.tensor_tensor(out=CkSk.ap[:, 0:64], in0=rp.ap[:, j, 0:64], in1=gk.ap, op=ALU.mult))(), reads=[rp, gk], writes=[CkSk])
                S.do("pool", (lambda j=j, rp=rp: lambda e: e.tensor_tensor(out=CkSk.ap[:, 64:128], in0=rp.ap[:, j, 64:128], in1=gks.ap, op=ALU.mult))(), reads=[rp, gks, CkSk], writes=[CkSk])
                for (nh, t0, t1, t2, obf, tab, h0) in ((8, t0q[j], tq[1], tq[2], qb_bf[j], CqSq, 0), (2, t0k[j], tk[1], tk[2], kb_bf[j], CkSk, 8)):
                    W = nh * 64
                    S.do("dve", (lambda t0=t0, t1=t1, nh=nh, W=W, j=j, h0=h0: lambda e: e.tensor_tensor(out=t1.ap[:, 0:W].rearrange("p (h d) -> p h d", d=64), in0=t0.ap[:, 0:W].rearrange("p (h d) -> p h d", d=64),
                                                                                                        in1=ssA.ap[:, j, h0:h0 + nh].unsqueeze(2).to_broadcast([128, nh, 64]), op=ALU.mult))(), reads=[t0, ssA], writes=[t1])
                    S.do("pool", (lambda t1=t1, t2=t2, nh=nh, W=W, tab=tab: lambda e: e.tensor_tensor(out=t2.ap[:, 0:W].rearrange("p (h d) -> p h d", d=64), in0=t1.ap[:, 0:W].rearrange("p (h d) -> p h d", d=64),
                                                                                                        in1=tab.ap[:, 0:64].unsqueeze(1).to_broadcast([128, nh, 64]), op=ALU.mult))(), reads=[t1, tab], writes=[t2])
                    ops = []
                    prev0 = list(t0.r) + list(t0.w)
                    for xh in range(2):
                        eng = "dve" if xh == 0 else "pool"

                        def f(e, xh=xh, t0=t0, t1=t1, nh=nh, W=W, tab=tab):
                            o_ = t0.ap[:, 0:W].rearrange("p (h a x d) -> p h a x d", a=2, x=2, d=16)[:, :, :, xh, :]
                            i_ = t1.ap[:, 0:W].rearrange("p (h a x d) -> p h a x d", a=2, x=2, d=16)[:, :, :, 1 - xh, :]
                            s_ = tab.ap[:, 64:128].rearrange("p (a x d) -> p a x d", a=2, x=2, d=16)[:, :, xh, :].unsqueeze(1).to_broadcast([128, nh, 2, 16])
                            return e.tensor_tensor(out=o_, in0=i_, in1=s_, op=ALU.mult)
                        ops.append(S.do(eng, f, reads=[t1, tab], extra=prev0))
                    t0.w = ops
                    t0.r = []
                    S.do("dve", (lambda t0=t0, t2=t2, obf=obf, W=W: lambda e: e.tensor_tensor(out=obf.ap[:, 0:W], in0=t0.ap[:, 0:W], in1=t2.ap[:, 0:W], op=ALU.add))(), reads=[t0, t2], writes=[obf])

        def a_fm(i, glist):
            sn, t = tilesA[i]
            sc = scr[sn]
            hTb = hT[i % 2]
            groups = [("QA", 0, sc["QAT"], 0), ("KA", 512, sc["KAT"], 0)] + [("G", 2304 + 512 * k, sc["GT"], 512 * k) for k in range(4)]
            for gidx in glist:
                kind, col0, dst, drow0 = groups[gidx]
                st = fmst[fmi_box[0] % 4]
                fmi_box[0] += 1
                ws = []
                prev = list(st.r) + list(st.w)
                for b in range(4):
                    pb = pbank[2 + (b % 2)]
                    cc = col0 + b * 128

                    def f(e, cc=cc, pb=pb, hTb=hTb):
                        last = None
                        for c in range(8):
                            last = e.matmul(pb.ap, lhsT=wA.ap[:, c, cc:cc + 128], rhs=hTb.ap[:, c, :], start=(c == 0), stop=(c == 7))
                        return last
                    S.do("pe", f, reads=[wA, hTb], writes=[pb])
                    if kind == "QA":
                        o = S.do("act", (lambda st=st, b=b, pb=pb: lambda e: e.mul(out=st.ap[:, b, :], in_=pb.ap, mul=0.125))(), reads=[pb], extra=prev)
                    elif kind == "KA":
                        o = S.do("act", (lambda st=st, b=b, pb=pb: lambda e: e.copy(out=st.ap[:, b, :], in_=pb.ap))(), reads=[pb], extra=prev)
                    else:
                        gbi = (col0 - 2304) // 128 + b
                        o = S.do("act", (lambda st=st, b=b, pb=pb, gbi=gbi: lambda e: e.activation(out=st.ap[:, b, :], in_=pb.ap, func=AF.Sigmoid, bias=bg.ap[:, gbi:gbi + 1], scale=1.0))(), reads=[pb, bg], extra=prev)
                    ws.append(o)
                st.w = ws
                st.r = []
                S.do("sp", dma(dst[drow0:drow0 + 512, t * 512:(t + 1) * 512].rearrange("(b p) n -> p b n", p=128), st.ap), reads=[st], semkey=st.key)

        def a_qktr(i):
            sn, t = tilesA[i]
            sc = scr[sn]
            qtw, ktw = [], []
            prev_qt = list(qbT_st.r) + list(qbT_st.w)
            prev_kt = list(kbT_st.r) + list(kbT_st.w)
            for j in range(4):
                for nh, obf in ((8, qb_bf[j]), (2, kb_bf[j])):
                    ncb = nh * 64 // 128
                    tpb = pbank[7]
                    pv = tpb.ap.bitcast(BF16)

                    def ft(e, obf=obf, ncb=ncb, pv=pv):
                        last = None
                        for cbk in range(ncb):
                            last = e.transpose(pv[:, cbk * 128:(cbk + 1) * 128], obf.ap[:, cbk * 128:(cbk + 1) * 128], ident.ap)
                        return last
                    S.do("pe", ft, reads=[obf, ident], writes=[tpb])
                    if nh == 8:
                        qtw.append(S.do("dve", (lambda j=j, pv=pv: lambda e: e.tensor_copy(out=qbT_st.ap[:, :, j * 128:(j + 1) * 128], in_=pv[:, 0:512].rearrange("p (c n) -> p c n", n=128)))(), reads=[tpb], extra=prev_qt))
                    else:
                        ktw.append(S.do("dve", (lambda j=j, pv=pv: lambda e: e.tensor_copy(out=kbT_st.ap[:, j * 128:(j + 1) * 128], in_=pv[:, 0:128]))(), reads=[tpb], extra=prev_kt))
            qbT_st.w = qtw
            qbT_st.r = []
            kbT_st.w = ktw
            kbT_st.r = []
            S.do("sp", dma(sc["VA"][t * 512:(t + 1) * 512, :].rearrange("(j p) d -> p j d", p=128), va_st.ap), reads=[va_st], semkey="va_st")
            S.do("sp", dma(sc["VB"][t * 512:(t + 1) * 512, :].rearrange("(j p) d -> p j d", p=128), vb_st.ap), reads=[vb_st], semkey="vb_st")
            S.do("sp", dma(sc["QBT"][:, t * 512:(t + 1) * 512].rearrange("(c p) n -> p c n", p=128), qbT_st.ap), reads=[qbT_st], semkey="qbT_st")
            S.do("sp", dma(sc["KBT"][:, t * 512:(t + 1) * 512], kbT_st.ap), reads=[kbT_st], semkey="kbT_st")

        nTA = len(tilesA)
        a_load(0)
        a_norm_a(0)
        a_norm_b(0)
        a_tr(0)
        for i in range(nTA):
            nxt = i + 1 < nTA
            if nxt:
                a_load(i + 1)
            a_tm1(i)
            if nxt:
                a_norm_a(i + 1)
            a_fm(i, [0, 1])
            a_qkrs(i)
            a_fm(i, [2, 3])
            if nxt:
                a_norm_b(i + 1)
            a_st3(i)
            a_fm(i, [4, 5])
            a_qktr(i)
            if nxt:
                a_tr(i + 1)
        S.barrier()

        AB = Alloc(PBASE)
        NKBM = NMAX // 128
        KT = [AB.t([128, NMAX], BF16, f"KT{i}") for i in range(2)]
        VV = [AB.t([128, NKBM, 129], BF16, f"VV{i}") for i in range(2)]
        QT = [AB.t([128, 512], BF16, f"QT{i}") for i in range(3)]
        PP = [AB.t([128, 2, 512], BF16, f"PP{i}") for i in range(4)]
        XH = AB.t([128, 4 * 1152], BF16, "XH")
        XL = AB.t([128, 4 * 1152], BF16, "XL")
        accS = AB.t([128, 4, 258], F32, "accS")
        rl = AB.t([128, 8], F32, "rl")
        e0 = AB.t([128, 4, 128], F32, "e0")
        e1 = AB.t([128, 4, 128], F32, "e1")
        e2 = AB.t([128, 4, 128], F32, "e2")
        ssb = AB.t([128, 4], F32, "ssb")
        ost = [AB.t([128, 4, 128], BF16, f"ost{i}") for i in range(2)]
        bstg32 = [AB.t([128, CW], F32, f"bstg32_{i}") for i in range(2)]
        bstg16 = [AB.t([128, CW], BF16, f"bstg16_{i}") for i in range(2)]
        bg_state = {"k": 0}

        def bg_step():
            if l != 0:
                return False
            k = bg_state["k"]
            nrem = len(chunks) - NCH0
            if k >= nrem + 2:
                return False
            if k < nrem:
                conv_load(NCH0 + k, bstg32, "pool")
            if 1 <= k <= nrem:
                conv_cast(NCH0 + k - 1, bstg32, bstg16, "pool")
            if 2 <= k <= nrem + 1:
                conv_store(NCH0 + k - 2, bstg16, "pool")
            bg_state["k"] = k + 1
            return True
        S.do("sp", dma(XH.ap, Xd[0]), writes=[XH], semkey="XH")
        S.do("sp", dma(XL.ap, Xd[1]), writes=[XL], semkey="XL")
        Spair = [(pbank[0], pbank[1]), (pbank[2], pbank[3])]
        def pair_ap(i):
            return psum_all[:, 2 * i:2 * i + 2, :]
        acc_b = pbank[4:8]
        gi = 0
        ui = 0
        ji = 0
        qi = 0
        pendq = []
        for sn, N in seqs:
            sc = scr[sn]
            NKB = N // 128
            NQC = N // 512
            for grp in range(8):
                diff = grp < 4
                E = 128 if diff else 64
                kt, vv = KT[gi % 2], VV[gi % 2]
                gi += 1
                if diff:
                    h = grp
                    S.do("sp", dma(kt.ap[:, 0:N], sc["KAT"][h * 128:(h + 1) * 128, :]), writes=[kt], semkey=kt.key)
                    S.do("sp", dma(vv.ap[:, 0:NKB, 0:128], sc["VA"][:, h * 128:(h + 1) * 128].rearrange("(k p) e -> p k e", p=128)), writes=[vv], semkey=vv.key)
                    qsrc = sc["QAT"][h * 128:(h + 1) * 128, :]
                else:
                    pr = grp - 4
                    g = pr // 2
                    o1 = S.do("sp", dma(kt.ap[0:64, 0:N], sc["KBT"][g * 64:(g + 1) * 64, :]), writes=[kt], semkey=kt.key)
                    o2 = S.do("sp", dma(kt.ap[64:128, 0:N], sc["KBT"][g * 64:(g + 1) * 64, :]), extra=list(o1.deps), semkey=kt.key)
                    kt.w = [o1, o2]
                    S.do("sp", dma(vv.ap[:, 0:NKB, 0:64], sc["VB"][:, g * 64:(g + 1) * 64].rearrange("(k p) e -> p k e", p=128)), writes=[vv], semkey=vv.key)
                    qsrc = sc["QBT"][pr * 128:(pr + 1) * 128, :]
                lw = list(vv.w)
                om = S.do("pool", (lambda vv=vv, E=E, NKB=NKB: lambda e: e.memset(vv.ap[:, 0:NKB, E:E + 1], 1.0))(), extra=lw + [x for x in lw[0].deps])
                vv.w = lw + [om]
                for qc in range(NQC):
                    qt = QT[qi % 3]
                    qi += 1
                    S.do("sp", dma(qt.ap, qsrc[:, qc * 512:(qc + 1) * 512]), writes=[qt], semkey=qt.key)
                    for kb in range(NKB):
                        sp0, sp1 = Spair[ui % 2]
                        pp = PP[ui % 4]
                        near = diff and (4 * qc - 1 <= kb <= 4 * qc + 4)
                        d_off = kb * 128 - qc * 512
                        wcol = 512 - d_off

                        def fqk(e, kt=kt, qt=qt, kb=kb, sp0=sp0, sp1=sp1, near=near, wcol=wcol, grp=grp):
                            last = None
                            for c, spx in ((0, sp0), (1, sp1)):
                                last = e.matmul(spx.ap, lhsT=kt.ap[c * 64:(c + 1) * 64, kb * 128:(kb + 1) * 128], rhs=qt.ap[c * 64:(c + 1) * 64, :], start=True, stop=not near)
                            if near:
                                for c, spx in ((0, sp0), (1, sp1)):
                                    e.matmul(spx.ap, lhsT=jmat.ap, rhs=XH.ap[:, grp * 1152 + wcol:grp * 1152 + wcol + 512], start=False, stop=False)
                                    last = e.matmul(spx.ap, lhsT=jmat.ap, rhs=XL.ap[:, grp * 1152 + wcol:grp * 1152 + wcol + 512], start=False, stop=True)
                            return last
                        S.do("pe", fqk, reads=[kt, qt] + ([XH, XL, jmat] if near else []), writes=[sp0, sp1])
                        if len(pendq) >= 2:
                            pendq.pop(0)()
                        if diff and not near:
                            bcol = grp if kb < 4 * qc else 4 + grp
                            bias_ap = cb.ap[:, bcol:bcol + 1]
                        else:
                            bias_ap = None
                        pa = pair_ap(ui % 2)

                        def fex(e, pa=pa, pp=pp, bias_ap=bias_ap):
                            if bias_ap is None:
                                return e.activation(out=pp.ap, in_=pa, func=AF.Exp)
                            return e.activation(out=pp.ap, in_=pa, func=AF.Exp, bias=bias_ap, scale=1.0)
                        S.do("act", fex, reads=[sp0, sp1] + ([cb] if bias_ap is not None else []), writes=[pp])

                        def mk_pv(pp=pp, vv=vv, kb=kb, NKB=NKB, E=E, diff=diff, qc=qc, grp=grp, sc=sc, ji=ji, kt=kt, qt=qt):
                            def fpv(e):
                                last = None
                                for c in range(2):
                                    for qb in range(4):
                                        if diff:
                                            bk = acc_b[c * 2 + qb // 2]
                                            col = (qb % 2) * 129
                                        else:
                                            bk = acc_b[c]
                                            col = qb * 65
                                        first_in_bank = (qb % 2 == 0) if diff else (qb == 0)
                                        last = e.matmul(bk.ap[:, col:col + E + 1], lhsT=pp.ap[:, c, qb * 128:(qb + 1) * 128], rhs=vv.ap[:, kb, 0:E + 1],
                                                        start=(kb == 0 and first_in_bank), stop=(kb == NKB - 1), skip_group_check=True)
                                return last
                            S.do("pe", fpv, reads=[pp, vv], writes=(acc_b if diff else acc_b[0:2]) if kb == 0 else [], extra=[])
                            if kb == NKB - 1:
                                ob = ost[ji % 2]
                                if diff:
                                    def fcp(e):
                                        return e.tensor_copy(out=accS.ap, in_=psum_all[:, 4:8, 0:258])
                                    o_cp = S.do("dve", fcp, reads=[], writes=[accS], extra=[S.q["pe"][-1]])
                                    for bkx in acc_b:
                                        bkx.r.append(o_cp)
                                    av = accS.ap.rearrange("p b (i e) -> p (b i) e", e=129)
                                    S.do("dve", lambda e: e.reciprocal(out=rl.ap, in_=av[:, :, 128]), reads=[accS], writes=[rl])
                                    S.do("dve", lambda e: e.tensor_scalar(out=rl.ap[:, 4:8], in0=rl.ap[:, 4:8], scalar1=neglam.ap[:, 0:1], scalar2=None, op0=ALU.mult), reads=[rl, neglam], writes=[rl])
                                    S.do("pool", lambda e: e.tensor_tensor(out=e0.ap, in0=av[:, 0:4, 0:128], in1=rl.ap[:, 0:4].unsqueeze(2).to_broadcast([128, 4, 128]), op=ALU.mult), reads=[accS, rl], writes=[e0])
                                    S.do("dve", lambda e: e.tensor_tensor(out=e1.ap, in0=av[:, 4:8, 0:128], in1=rl.ap[:, 4:8].unsqueeze(2).to_broadcast([128, 4, 128]), op=ALU.mult), reads=[accS, rl], writes=[e1])
                                    S.do("pool", lambda e: e.tensor_tensor(out=e0.ap, in0=e0.ap, in1=e1.ap, op=ALU.add), reads=[e0, e1], writes=[e0])
                                    S.do("pool", lambda e: e.tensor_tensor(out=e2.ap, in0=e0.ap, in1=e0.ap, op=ALU.mult), reads=[e0], writes=[e2])
                                    S.do("dve", lambda e: e.tensor_reduce(out=ssb.ap, in_=e2.ap, axis=AX.X, op=ALU.add), reads=[e2], writes=[ssb])
                                    S.do("dve", lambda e: e.tensor_scalar(out=ssb.ap, in0=ssb.ap, scalar1=1.0 / 128, scalar2=SUBLN_EPS, op0=ALU.mult, op1=ALU.add), reads=[ssb], writes=[ssb])
                                    rsqrt_inplace(ssb, lambda: ssb.ap)
                                    S.do("dve", lambda e: e.tensor_tensor(out=e1.ap, in0=e0.ap, in1=ssb.ap.unsqueeze(2).to_broadcast([128, 4, 128]), op=ALU.mult), reads=[e0, ssb], writes=[e1])
                                    S.do("pool", lambda e: e.tensor_tensor(out=ob.ap, in0=e1.ap, in1=gsub.ap.unsqueeze(1).to_broadcast([128, 4, 128]), op=ALU.mult), reads=[e1, gsub], writes=[ob])
                                    S.do("sp", dma(sc["OA"][qc * 512:(qc + 1) * 512, grp * 128:(grp + 1) * 128].rearrange("(j p) e -> p j e", p=128), ob.ap), reads=[ob], semkey=ob.key)
                                else:
                                    pr = grp - 4

                                    def fcp(e):
                                        return e.tensor_copy(out=accS.ap.rearrange("p b w -> p (b w)")[:, 0:520].rearrange("p (b w) -> p b w", w=260), in_=psum_all[:, 4:6, 0:260])
                                    o_cp = S.do("dve", fcp, reads=[], writes=[accS], extra=[S.q["pe"][-1]])
                                    for bkx in acc_b[0:2]:
                                        bkx.r.append(o_cp)
                                    av = accS.ap.rearrange("p b w -> p (b w)")[:, 0:520].rearrange("p (i e) -> p i e", e=65)
                                    S.do("dve", lambda e: e.reciprocal(out=rl.ap, in_=av[:, :, 64]), reads=[accS], writes=[rl])
                                    def fo(e):
                                        o_ = ob.ap.rearrange("p j (hd d) -> p hd j d", d=64)
                                        i_ = av[:, :, 0:64].rearrange("p (hd j) d -> p hd j d", j=4)
                                        r_ = rl.ap.rearrange("p (hd j) -> p hd j", j=4).unsqueeze(3).to_broadcast([128, 2, 4, 64])
                                        return e.tensor_tensor(out=o_, in0=i_, in1=r_, op=ALU.mult)
                                    S.do("dve", fo, reads=[accS, rl], writes=[ob])
                                    S.do("sp", dma(sc["OB"][qc * 512:(qc + 1) * 512, pr * 128:(pr + 1) * 128].rearrange("(j p) e -> p j e", p=128), ob.ap), reads=[ob], semkey=ob.key)
                        pendq.append(mk_pv)
                        if kb == NKB - 1:
                            bg_step()
                        if kb == NKB - 1:
                            ji += 1
                        ui += 1
        while pendq:
            pendq.pop(0)()
        while bg_step():
            pass
        S.barrier()

        AC = Alloc(PBASE)
        wua = AC.t([128, 4, D], BF16, "wua")
        wub = AC.t([128, 4, D], BF16, "wub")
        wo = AC.t([128, 8, D], BF16, "wo")
        oat = [AC.t([128, 4, 512], BF16, f"oat{i}") for i in range(2)]
        obt = [AC.t([128, 4, 512], BF16, f"obt{i}") for i in range(2)]
        gtt = [AC.t([128, 16, 512], BF16, f"gtt{i}") for i in range(2)]
        xc = [AC.t([128, 4, D], F32, f"xc{i}") for i in range(2)]
        oT = AC.t([128, 8, 512], BF16, "oT")
        uT = AC.t([128, 8, 512], BF16, "uT")
        u1 = [AC.t([128, 512], F32, f"u1_{i}") for i in range(2)]
        u2 = [AC.t([128, 512], F32, f"u2_{i}") for i in range(2)]
        S.do("sp", dma(wua.ap, wb_ua[l * 512:(l + 1) * 512, :].rearrange("(c p) n -> p c n", p=128)), writes=[wua], semkey="wua")
        S.do("sp", dma(wub.ap, wb_ub[l * 512:(l + 1) * 512, :].rearrange("(c p) n -> p c n", p=128)), writes=[wub], semkey="wub")
        S.do("sp", dma(wo.ap, wb_o[l * D:(l + 1) * D, :].rearrange("(c p) n -> p c n", p=128)), writes=[wo], semkey="wo")
        gt = 0
        for sn, N in seqs:
            sc = scr[sn]
            xsrc = x_in[sn] if l == 0 else sc["X2"]
            for t in range(N // 512):
                oa_, ob_, g_, x_ = oat[gt % 2], obt[gt % 2], gtt[gt % 2], xc[gt % 2]
                gt += 1
                S.do("sp", dma(oa_.ap, sc["OA"][t * 512:(t + 1) * 512, :].rearrange("(j p) e -> p j e", p=128)), writes=[oa_], semkey=oa_.key)
                S.do("sp", dma(ob_.ap, sc["OB"][t * 512:(t + 1) * 512, :].rearrange("(j p) e -> p j e", p=128)), writes=[ob_], semkey=ob_.key)
                S.do("sp", dma(g_.ap, sc["GT"][:, t * 512:(t + 1) * 512].rearrange("(b p) n -> p b n", p=128)), writes=[g_], semkey=g_.key)
                S.do("sp", dma(x_.ap, xsrc[t * 512:(t + 1) * 512, :].rearrange("(j p) d -> p j d", p=128)), writes=[x_], semkey=x_.key)
                ws = []
                prev = list(oT.r) + list(oT.w)
                for cb8 in range(8):
                    srcb = oa_ if cb8 < 4 else ob_
                    cbk = cb8 % 4
                    pb = pbank[cb8 % 2]
                    pv = pb.ap.bitcast(BF16)

                    def f(e, srcb=srcb, cbk=cbk, pv=pv):
                        last = None
                        for j in range(4):
                            last = e.transpose(pv[:, j * 128:(j + 1) * 128], srcb.ap[:, j, cbk * 128:(cbk + 1) * 128], ident.ap)
                        return last
                    S.do("pe", f, reads=[srcb, ident], writes=[pb])
                    if cb8 % 2 == 0:
                        ws.append(S.do("act", (lambda cb8=cb8, pv=pv: lambda e: e.copy(out=oT.ap[:, cb8, :], in_=pv[:, 0:512]))(), reads=[pb], extra=prev))
                    else:
                        ws.append(S.do("dve", (lambda cb8=cb8, pv=pv: lambda e: e.tensor_copy(out=oT.ap[:, cb8, :], in_=pv[:, 0:512]))(), reads=[pb], extra=prev))
                oT.w = ws
                oT.r = []
                ws = []
                prev = list(uT.r) + list(uT.w)
                for fb in range(8):
                    pa_, pb_ = pbank[2 + 2 * (fb % 2)], pbank[3 + 2 * (fb % 2)]
                    t1_, t2_ = u1[fb % 2], u2[fb % 2]

                    def f(e, fb=fb, pa_=pa_, pb_=pb_):
                        last = None
                        for c in range(4):
                            last = e.matmul(pa_.ap, lhsT=wua.ap[:, c, fb * 128:(fb + 1) * 128], rhs=oT.ap[:, c, :], start=(c == 0), stop=(c == 3))
                        for c in range(4):
                            last = e.matmul(pb_.ap, lhsT=wub.ap[:, c, fb * 128:(fb + 1) * 128], rhs=oT.ap[:, 4 + c, :], start=(c == 0), stop=(c == 3))
                        return last
                    S.do("pe", f, reads=[wua, wub, oT], writes=[pa_, pb_])
                    S.do("dve", (lambda fb=fb, pa_=pa_, t1_=t1_, g_=g_: lambda e: e.tensor_tensor(out=t1_.ap, in0=pa_.ap, in1=g_.ap[:, fb, :], op=ALU.mult))(), reads=[pa_, g_], writes=[t1_])
                    S.do("dve", (lambda fb=fb, pb_=pb_, t2_=t2_, g_=g_: lambda e: e.tensor_tensor(out=t2_.ap, in0=pb_.ap, in1=g_.ap[:, 8 + fb, :], op=ALU.mult))(), reads=[pb_, g_], writes=[t2_])
                    ws.append(S.do("pool", (lambda fb=fb, t1_=t1_, t2_=t2_: lambda e: e.tensor_tensor(out=uT.ap[:, fb, :], in0=t1_.ap, in1=t2_.ap, op=ALU.add))(), reads=[t1_, t2_], extra=prev))
                uT.w = ws
                uT.r = []
                ws = []
                for j in range(4):
                    for hf in range(2):
                        pb = pbank[6 + hf]

                        def f(e, j=j, hf=hf, pb=pb):
                            last = None
                            for c in range(8):
                                last = e.matmul(pb.ap, lhsT=uT.ap[:, c, j * 128:(j + 1) * 128], rhs=wo.ap[:, c, hf * 512:(hf + 1) * 512], start=(c == 0), stop=(c == 7))
                            return last
                        S.do("pe", f, reads=[uT, wo], writes=[pb])
                        ws.append(S.do("dve", (lambda j=j, hf=hf, pb=pb, x_=x_: lambda e: e.tensor_tensor(out=x_.ap[:, j, hf * 512:(hf + 1) * 512], in0=pb.ap, in1=x_.ap[:, j, hf * 512:(hf + 1) * 512], op=ALU.add))(), reads=[pb, x_]))
                x_.w = x_.w + ws
                S.do("sp", dma(sc["X1"][t * 512:(t + 1) * 512, :].rearrange("(j p) d -> p j d", p=128), x_.ap), reads=[x_], semkey=x_.key + "s")
        S.barrier()

        AD = Alloc(PBASE)
        w1 = AD.t([128, 8, DFF], BF16, "w1")
        w2 = AD.t([128, 32, D], BF16, "w2")
        g2b = AD.t([128, D], F32, "g2b")
        gfb = AD.t([128, D], F32, "gfb")
        xd = [AD.t([128, 2, D], F32, f"xd{i}") for i in range(3)]
        junk2 = AD.t([128, 2, D], F32, "junk2")
        h2 = AD.t([128, 2, D], BF16, "h2")
        h2T = AD.t([128, 8, 256], BF16, "h2T")
        aT = AD.t([128, 32, 256], BF16, "aT")
        rr = [AD.t([128, 256], F32, f"rr{i}") for i in range(2)]
        ssd = AD.t([128, 2], F32, "ssd")
        rsd = AD.t([128, 2], F32, "rsd")
        ssf = AD.t([128, 2], F32, "ssf")
        rsf = [AD.t([128, 2], F32, f"rsf{i}") for i in range(2)]
        S.do("sp", dma(w1.ap, wb_f1[l * D:(l + 1) * D, :].rearrange("(c p) n -> p c n", p=128)), writes=[w1], semkey="w1")
        for q4 in range(4):
            o = S.do("sp", dma(w2.ap[:, q4 * 8:(q4 + 1) * 8, :], wb_f2[l * DFF + q4 * 1024:l * DFF + (q4 + 1) * 1024, :].rearrange("(c p) n -> p c n", p=128)), semkey="w2")
        w2.w = [o]
        S.do("sp", dma(g2b.ap, norm2[l].partition_broadcast(128)), writes=[g2b], semkey="g2b")
        S.do("sp", dma(gfb.ap, norm_f.partition_broadcast(128)), writes=[gfb], semkey="gfb")
        tilesD = []
        for sn, N in seqs:
            for t in range(N // 256):
                tilesD.append((sn, t))
        last_layer = (l == nlayers - 1)

        def d_load(i):
            sn, t = tilesD[i]
            x_ = xd[i % 3]
            S.do("sp", dma(x_.ap, scr[sn]["X1"][t * 256:(t + 1) * 256, :].rearrange("(j p) d -> p j d", p=128)), writes=[x_], semkey=x_.key)

        def d_norm_a(i):
            x_ = xd[i % 3]
            S.do("pool", (lambda x_=x_: lambda e: e.tensor_tensor(out=junk2.ap, in0=x_.ap, in1=x_.ap, op=ALU.mult))(), reads=[x_], writes=[junk2])
            S.do("dve", lambda e: e.tensor_reduce(out=ssd.ap, in_=junk2.ap, axis=AX.X, op=ALU.add), reads=[junk2], writes=[ssd])
            S.do("dve", lambda e: e.tensor_scalar(out=rsd.ap, in0=ssd.ap, scalar1=1.0 / D, scalar2=EPS, op0=ALU.mult, op1=ALU.add), reads=[ssd], writes=[rsd])

        def d_norm_b(i):
            x_ = xd[i % 3]
            rsqrt_inplace(rsd, lambda: rsd.ap)
            ws = []
            prev = list(h2.r) + list(h2.w)
            for j in range(2):
                ws.append(S.do("dve", (lambda j=j, x_=x_: lambda e: e.scalar_tensor_tensor(out=h2.ap[:, j, :], in0=x_.ap[:, j, :], scalar=rsd.ap[:, j:j + 1], in1=g2b.ap, op0=ALU.mult, op1=ALU.mult))(),
                               reads=[x_, rsd, g2b], extra=prev))
            h2.w = ws
            h2.r = []

        def d_tr(i):
            ws = []
            prev = list(h2T.r) + list(h2T.w)
            for c in range(8):
                pb = pbank[c % 2]
                pv = pb.ap.bitcast(BF16)

                def f(e, c=c, pv=pv):
                    last = None
                    for j in range(2):
                        last = e.transpose(pv[:, j * 128:(j + 1) * 128], h2.ap[:, j, c * 128:(c + 1) * 128], ident.ap)
                    return last
                S.do("pe", f, reads=[h2, ident], writes=[pb])
                if c % 2 == 0:
                    ws.append(S.do("act", (lambda c=c, pv=pv: lambda e: e.copy(out=h2T.ap[:, c, :], in_=pv[:, 0:256]))(), reads=[pb], extra=prev))
                else:
                    ws.append(S.do("dve", (lambda c=c, pv=pv: lambda e: e.tensor_copy(out=h2T.ap[:, c, :], in_=pv[:, 0:256]))(), reads=[pb], extra=prev))
            h2T.w = ws
            h2T.r = []

        aT_state = {"ws": [], "prev": []}

        def d_ffn1(i, fbs):
            if fbs[0] == 0:
                aT_state["ws"] = []
                aT_state["prev"] = list(aT.r) + list(aT.w)
            for fb in fbs:
                pb = pbank[2 + fb % 4]
                r_ = rr[fb % 2]

                def f(e, fb=fb, pb=pb):
                    last = None
                    for c in range(8):
                        last = e.matmul(pb.ap[:, 0:256], lhsT=w1.ap[:, c, fb * 128:(fb + 1) * 128], rhs=h2T.ap[:, c, :], start=(c == 0), stop=(c == 7))
                    return last
                S.do("pe", f, reads=[w1, h2T], writes=[pb])
                S.do("act", (lambda pb=pb, r_=r_: lambda e: e.activation(out=r_.ap, in_=pb.ap[:, 0:256], func=AF.Relu))(), reads=[pb], writes=[r_])
                aT_state["ws"].append(S.do("pool" if fb % 2 == 0 else "dve", (lambda fb=fb, r_=r_: lambda e: e.tensor_tensor(out=aT.ap[:, fb, :], in0=r_.ap, in1=r_.ap, op=ALU.mult))(), reads=[r_], extra=aT_state["prev"]))
            if fbs[-1] == 31:
                aT.w = aT_state["ws"]
                aT.r = []

        def d_ffn2(i):
            sn, t = tilesD[i]
            x_ = xd[i % 3]
            ws = []
            for j in range(2):
                for hf in range(2):
                    pb = pbank[6 + hf]

                    def f(e, j=j, hf=hf, pb=pb):
                        last = None
                        for c in range(32):
                            last = e.matmul(pb.ap, lhsT=aT.ap[:, c, j * 128:(j + 1) * 128], rhs=w2.ap[:, c, hf * 512:(hf + 1) * 512], start=(c == 0), stop=(c == 31))
                        return last
                    S.do("pe", f, reads=[aT, w2], writes=[pb])
                    ws.append(S.do("dve", (lambda j=j, hf=hf, pb=pb, x_=x_: lambda e: e.tensor_tensor(out=x_.ap[:, j, hf * 512:(hf + 1) * 512], in0=pb.ap, in1=x_.ap[:, j, hf * 512:(hf + 1) * 512], op=ALU.add))(), reads=[pb, x_]))
            x_.w = x_.w + ws
            if (not last_layer) or debug:
                S.do("sp", dma(scr[sn]["X2"][t * 256:(t + 1) * 256, :].rearrange("(j p) d -> p j d", p=128), x_.ap), reads=[x_], semkey=x_.key + "s")
            if last_layer:
                rs_ = rsf[i % 2]
                S.do("pool", (lambda x_=x_: lambda e: e.tensor_tensor(out=junk2.ap, in0=x_.ap, in1=x_.ap, op=ALU.mult))(), reads=[x_], writes=[junk2])
                S.do("dve", lambda e: e.tensor_reduce(out=ssf.ap, in_=junk2.ap, axis=AX.X, op=ALU.add), reads=[junk2], writes=[ssf])
                S.do("dve", (lambda rs_=rs_: lambda e: e.tensor_scalar(out=rs_.ap, in0=ssf.ap, scalar1=1.0 / D, scalar2=EPS, op0=ALU.mult, op1=ALU.add))(), reads=[ssf], writes=[rs_])

        def d_final_b(i):
            sn, t = tilesD[i]
            x_ = xd[i % 3]
            rs_ = rsf[i % 2]
            rsqrt_inplace(rs_, (lambda rs_=rs_: lambda: rs_.ap)())
            ws = []
            for j in range(2):
                ws.append(S.do("dve", (lambda j=j, x_=x_, rs_=rs_: lambda e: e.scalar_tensor_tensor(out=x_.ap[:, j, :], in0=x_.ap[:, j, :], scalar=rs_.ap[:, j:j + 1], in1=gfb.ap, op0=ALU.mult, op1=ALU.mult))(),
                               reads=[x_, rs_, gfb]))
            x_.w = x_.w + ws
            S.do("sp", dma(y_out[sn][t * 256:(t + 1) * 256, :].rearrange("(j p) d -> p j d", p=128), x_.ap), reads=[x_], semkey="yo")

        nTD = len(tilesD)
        d_load(0)
        d_norm_a(0)
        d_norm_b(0)
        d_tr(0)
        for i in range(nTD):
            nxt = i + 1 < nTD
            if nxt:
                d_load(i + 1)
            d_ffn1(i, list(range(0, 16)))
            if nxt:
                d_norm_a(i + 1)
            d_ffn1(i, list(range(16, 32)))
            if nxt:
                d_norm_b(i + 1)
            if last_layer and i > 0:
                d_final_b(i - 1)
            d_ffn2(i)
            if nxt:
                d_tr(i + 1)
        if last_layer:
            d_final_b(nTD - 1)
        S.barrier()

    keys = S.finalize()
    assert len(keys) <= 100, len(keys)
    sems = {k: nc.alloc_semaphore(name="s_" + "_".join(map(str, k))) for k in keys}
    with nc.Block() as block:
        @block.tensor
        def _(e):
            S.emit("pe", e, sems)

        @block.scalar
        def _(e):
            S.emit("act", e, sems)

        @block.vector
        def _(e):
            S.emit("dve", e, sems)

        @block.gpsimd
        def _(e):
            S.emit("pool", e, sems)

        @block.sync
        def _(e):
            S.emit("sp", e, sems)
    return nc, {e: len(S.q[e]) for e in S.ENGS}, len(keys)


def _consts():
    import jax
    import jax.numpy as jnp
    bf = ml_dtypes.bfloat16
    ident = np.eye(128, dtype=np.float32).astype(bf)
    jmat = np.eye(128, dtype=np.float32)[::-1].copy().astype(bf)
    with jax.default_device(jax.devices("cpu")[0]):
        n = 8192
        row = jnp.repeat(jnp.arange(n // 64, dtype=jnp.int32), 64).astype(jnp.float32)
        col = jnp.tile(jnp.arange(64, dtype=jnp.int32), n // 64).astype(jnp.float32)
        inv = 10000.0 ** (-jnp.arange(0, 32, 2, dtype=jnp.float32) / 32)
        ar = row[:, None] * inv[None, :]
        ac = col[:, None] * inv[None, :]
        cr, sr, cc, sc_ = (np.asarray(jnp.cos(ar)), np.asarray(jnp.sin(ar)), np.asarray(jnp.cos(ac)), np.asarray(jnp.sin(ac)))
        C = np.concatenate([cr, cr, cc, cc], axis=1)
        Sg = np.concatenate([-sr, sr, -sc_, sc_], axis=1)
        rope = np.ascontiguousarray(np.concatenate([C, Sg], axis=1).astype(np.float32))
        rel = 639 - jnp.arange(1280, dtype=jnp.int32)
        nb = 16
        max_exact = 8
        ret = (rel > 0).astype(jnp.int32) * nb
        na = jnp.abs(rel)
        nf = jnp.maximum(na, 1).astype(jnp.float32)
        large = max_exact + (jnp.log(nf / max_exact) / math.log(128 / max_exact) * (nb - max_exact)).astype(jnp.int32)
        large = jnp.minimum(large, nb - 1)
        bucket = np.asarray(ret + jnp.where(na < max_exact, na, large))
    oh = np.zeros((32, 1280), np.float32)
    oh[bucket, np.arange(1280)] = 1.0
    return ident, jmat, rope, oh


_CACHE = {}


def _run(inputs, NPR, NSA, debug=False, nlayers=NL):
    key = (NPR, NSA, debug, nlayers)
    if key not in _CACHE:
        _CACHE[key] = build(NPR, NSA, debug, nlayers)
    nc, nops, nsem = _CACHE[key]
    ident, jmat, rope, oh = _consts()
    f = lambda a: np.ascontiguousarray(np.asarray(a, dtype=np.float32))
    shared = {
        "w_in": f(inputs["w_in"]).reshape(NL * D, INC),
        "w_up_a": f(inputs["w_up_a"]).reshape(NL * 512, D),
        "w_up_b": f(inputs["w_up_b"]).reshape(NL * 512, D),
        "w_o": f(inputs["w_o"]).reshape(NL * D, D),
        "w_ff1": f(inputs["w_ff1"]).reshape(NL * D, DFF),
        "w_ff2": f(inputs["w_ff2"]).reshape(NL * DFF, D),
        "norm1": f(inputs["norm1"]), "norm2": f(inputs["norm2"]), "norm_f": f(inputs["norm_f"]),
        "b_gate": np.ascontiguousarray(f(inputs["b_gate"]).reshape(NL, 16, 128).transpose(0, 2, 1)),
        "lamv": np.ascontiguousarray(np.stack([f(inputs["lam_q1"]), f(inputs["lam_k1"]), f(inputs["lam_q2"]), f(inputs["lam_k2"])], axis=1)),
        "subln_g": f(inputs["subln_g"]), "qk_norm_q": f(inputs["qk_norm_q"]), "qk_norm_k": f(inputs["qk_norm_k"]),
        "t5_table": f(inputs["t5_table"]),
        "ident": ident, "jmat": jmat, "rope": rope, "oh": oh,
    }
    xp = f(inputs["x_prompt"])
    xs = f(inputs["x_sample"])
    in_maps = []
    for c in range(8):
        m = dict(shared)
        m["xp"] = xp[c]
        m["xs"] = xs[c]
        in_maps.append(m)
    res = run_bass_kernel_spmd(nc, in_maps, core_ids=list(range(8)))
    return res


def kernel(**inputs):
    xp = np.asarray(inputs["x_prompt"])
    xs = np.asarray(inputs["x_sample"])
    res = _run(inputs, xp.shape[1], xs.shape[1])
    yp = np.stack([np.asarray(r["yp"], dtype=np.float32) for r in res.results], axis=0)
    ys = np.stack([np.asarray(r["ys"], dtype=np.float32) for r in res.results], axis=0)
    return (yp, ys)
```

```python
import math
import numpy as np
import ml_dtypes
import concourse.bass as bass
import concourse.mybir as mybir
from concourse.bass_utils import run_bass_kernel_spmd

F32 = mybir.dt.float32
BF16 = mybir.dt.bfloat16
AF = mybir.ActivationFunctionType
ALU = mybir.AluOpType
AX = mybir.AxisListType

D = 1024
INC = 4352
DFF = 4096
EPS = 1e-6
SUBLN_EPS = 1e-5
NL = 2


class Op:
    __slots__ = ("eng", "fn", "deps", "sig", "semkey", "waits", "phase")

    def __init__(self, eng, fn, deps, semkey, phase):
        self.eng = eng
        self.fn = fn
        self.deps = deps
        self.sig = None
        self.semkey = semkey
        self.waits = None
        self.phase = phase


class Buf:
    __slots__ = ("ap", "w", "r", "key")

    def __init__(self, ap, key=None):
        self.ap = ap
        self.w = []
        self.r = []
        self.key = key


class Sched:
    ENGS = ("pe", "act", "dve", "pool", "sp")

    def __init__(self):
        self.q = {e: [] for e in self.ENGS}
        self.phase = 0
        self.last_dma = {}

    def do(self, eng, fn, reads=(), writes=(), extra=(), semkey=None):
        deps = list(extra)
        for b in reads:
            deps += b.w
        for b in writes:
            deps += b.w
            deps += b.r
        if eng == "pe":
            deps = [d for d in deps if d.eng != "pe"]
        o = Op(eng, fn, deps, semkey, self.phase)
        self.q[eng].append(o)
        for b in reads:
            b.r.append(o)
        for b in writes:
            b.w = [o]
            b.r = []
        if semkey is not None:
            self.last_dma[semkey] = o
        return o

    def barrier(self):
        deps = []
        for e in self.ENGS:
            if self.q[e]:
                for o in reversed(self.q[e]):
                    if o.fn is not None and o.semkey is None:
                        deps.append(o)
                        break
        deps += list(self.last_dma.values())
        for e in self.ENGS:
            o = Op(e, None, list(deps), None, self.phase)
            self.q[e].append(o)
        self.phase += 1

    def finalize(self):
        needed = set()
        for e in self.ENGS:
            for o in self.q[e]:
                for d in o.deps:
                    needed.add(id(d))
        counts = {}
        for e in self.ENGS:
            for o in self.q[e]:
                if o.fn is None:
                    continue
                if o.semkey is not None:
                    key = ("dma", o.semkey)
                    inc = 16
                elif id(o) in needed:
                    ph = o.phase
                    if ph == 0:
                        pg = "C-1"
                    else:
                        idx, lay = (ph - 1) % 4, (ph - 1) // 4
                        pg = f"B{lay}" if idx == 1 else (f"C{lay - 1}" if idx == 0 else f"C{lay}")
                    key = ("eng", e, pg)
                    inc = 1
                else:
                    continue
                counts[key] = counts.get(key, 0) + inc
                o.sig = (key, inc, counts[key])
        for e in self.ENGS:
            waited = {}
            for o in self.q[e]:
                w = {}
                for d in o.deps:
                    key, inc, val = d.sig
                    if waited.get(key, 0) >= val:
                        continue
                    if w.get(key, 0) < val:
                        w[key] = val
                waited.update(w)
                o.waits = w
        return list(counts.keys())

    def emit(self, name, eng, sems):
        for o in self.q[name]:
            for k, v in o.waits.items():
                eng.wait_ge(sems[k], v)
            if o.fn is None:
                continue
            ins = o.fn(eng)
            if o.sig is not None:
                ins.then_inc(sems[o.sig[0]], o.sig[1])


def build(NPR, NSA, debug=False, nlayers=NL):
    nc = bass.Bass("TRN2", target_bir_lowering=False)
    S = Sched()
    seqs = [("p", NPR), ("s", NSA)]
    NMAX = max(NPR, NSA)

    def din(name, shape, dt=F32):
        return nc.dram_tensor(name, list(shape), dt, kind="ExternalInput").ap()

    def dscr(name, shape, dt):
        kind = "ExternalOutput" if (debug and name.startswith("dbg_")) else "Internal"
        return nc.dram_tensor(name, list(shape), dt, kind=kind)

    x_in = {"p": din("xp", [NPR, D]), "s": din("xs", [NSA, D])}
    y_out = {"p": nc.dram_tensor("yp", [NPR, D], F32, kind="ExternalOutput").ap(),
             "s": nc.dram_tensor("ys", [NSA, D], F32, kind="ExternalOutput").ap()}
    w_in = din("w_in", [NL * D, INC])
    w_ua = din("w_up_a", [NL * 512, D])
    w_ub = din("w_up_b", [NL * 512, D])
    w_o = din("w_o", [NL * D, D])
    w_f1 = din("w_ff1", [NL * D, DFF])
    w_f2 = din("w_ff2", [NL * DFF, D])
    norm1 = din("norm1", [NL, D])
    norm2 = din("norm2", [NL, D])
    norm_f = din("norm_f", [D])
    b_gate = din("b_gate", [NL, 128, 16])
    lamv = din("lamv", [NL, 4, 64])
    subln = din("subln_g", [NL, 128])
    qkq = din("qk_norm_q", [NL, 64])
    qkk = din("qk_norm_k", [NL, 64])
    t5 = din("t5_table", [32, 4])
    ident_d = din("ident", [128, 128], BF16)
    jmat_d = din("jmat", [128, 128], BF16)
    rope_d = din("rope", [8192, 128])
    oh_d = din("oh", [32, 1280])

    wb_in = dscr("wb_in", [NL * D, INC], BF16).ap()
    wb_ua = dscr("wb_ua", [NL * 512, D], BF16).ap()
    wb_ub = dscr("wb_ub", [NL * 512, D], BF16).ap()
    wb_o = dscr("wb_o", [NL * D, D], BF16).ap()
    wb_f1 = dscr("wb_f1", [NL * D, DFF], BF16).ap()
    wb_f2 = dscr("wb_f2", [NL * DFF, D], BF16).ap()
    Fd_h = dscr("Fd", [4, 1280], F32)
    Xd = dscr("Xd", [2, 128, 4 * 1152], BF16).ap()
    scr = {}
    for sn, N in seqs:
        scr[sn] = dict(
            QAT=dscr(f"dbg_QAT_{sn}", [512, N], BF16).ap(),
            KAT=dscr(f"dbg_KAT_{sn}", [512, N], BF16).ap(),
            VA=dscr(f"dbg_VA_{sn}", [N, 512], BF16).ap(),
            QBT=dscr(f"dbg_QBT_{sn}", [512, N], BF16).ap(),
            KBT=dscr(f"dbg_KBT_{sn}", [128, N], BF16).ap(),
            VB=dscr(f"dbg_VB_{sn}", [N, 128], BF16).ap(),
            GT=dscr(f"dbg_GT_{sn}", [2048, N], BF16).ap(),
            OA=dscr(f"dbg_OA_{sn}", [N, 512], BF16).ap(),
            OB=dscr(f"dbg_OB_{sn}", [N, 512], BF16).ap(),
            X1=dscr(f"dbg_X1_{sn}", [N, D], F32).ap(),
            X2=dscr(f"dbg_X2_{sn}", [N, D], F32).ap(),
        )

    SB0 = (int(nc.sbuf_base) + 63) // 64 * 64
    SBTOP = int(nc.sbuf_top)

    class Alloc:
        def __init__(self, base):
            self.off = base

        def t(self, shape, dt, key=None):
            nbytes = int(np.prod(shape[1:])) * (4 if dt == F32 else 2)
            nbytes = (nbytes + 63) // 64 * 64
            nm = f"sb{Alloc.cnt}"
            Alloc.cnt += 1
            h = nc.alloc_sbuf_tensor_at(nm, list(shape), dt, offset=self.off)
            self.off += nbytes
            assert self.off <= SBTOP, ("SBUF overflow", self.off)
            return Buf(h.ap(), key or nm)
    Alloc.cnt = 0

    psum_all = nc.alloc_psum_tensor("psall", [128, 8, 512], F32).ap()
    pbank = [Buf(psum_all[:, i, :], f"bank{i}") for i in range(8)]

    def dma(out, in_, **kw):
        return lambda e: e.dma_start(out=out, in_=in_, **kw)

    PA = Alloc(SB0)
    ident = PA.t([128, 128], BF16, "ident")
    jmat = PA.t([128, 128], BF16, "jmat")
    cb = PA.t([128, 8], F32, "cb")
    bg = PA.t([128, 16], F32, "bg")
    gq8 = PA.t([128, 64], F32, "gq8")
    gqs8 = PA.t([128, 64], F32, "gqs8")
    gk = PA.t([128, 64], F32, "gk")
    gks = PA.t([128, 64], F32, "gks")
    gsub = PA.t([128, 128], F32, "gsub")
    lam4 = PA.t([128, 4, 64], F32, "lam4")
    lamt = PA.t([128, 2, 64], F32, "lamt")
    lams = PA.t([128, 2], F32, "lams")
    lame = PA.t([128, 2], F32, "lame")
    neglam = PA.t([128, 1], F32, "neglam")
    PBASE = PA.off

    S.do("sp", dma(ident.ap, ident_d), writes=[ident], semkey="c_ident")
    S.do("sp", dma(jmat.ap, jmat_d), writes=[jmat], semkey="c_jmat")
    S.do("sp", dma(cb.ap[:, 0:4], t5[15].partition_broadcast(128)), writes=[cb], semkey="c_cb")
    S.do("sp", dma(cb.ap[:, 4:8], t5[31].partition_broadcast(128)), writes=[cb], semkey="c_cb")

    A0 = Alloc(PBASE)
    CW = 4352
    stg32 = [A0.t([128, CW], F32, f"stg32_{i}") for i in range(2)]
    stg16 = [A0.t([128, CW], BF16, f"stg16_{i}") for i in range(2)]
    chunks = []
    for src, dst in ((w_in, wb_in), (w_ua, wb_ua), (w_ub, wb_ub), (w_o, wb_o), (w_f1, wb_f1), (w_f2, wb_f2)):
        R, C = src.shape
        g = max(1, CW // C)
        while (R // 128) % g != 0:
            g -= 1
        srcv = src.rearrange("(n p g) c -> n p (g c)", p=128, g=g)
        dstv = dst.rearrange("(n p g) c -> n p (g c)", p=128, g=g)
        for n in range(R // (128 * g)):
            chunks.append((srcv[n], dstv[n], g * C))

    def conv_load(ci, s32, q="sp"):
        a = s32[ci % 2]
        S.do(q, dma(a.ap[:, 0:chunks[ci][2]], chunks[ci][0]), writes=[a], semkey=a.key)

    def conv_cast(ci, s32, s16, eng):
        a, b_, W = s32[ci % 2], s16[ci % 2], chunks[ci][2]
        S.do(eng, (lambda a=a, b_=b_, W=W: lambda e: e.tensor_copy(out=b_.ap[:, 0:W], in_=a.ap[:, 0:W]))(), reads=[a], writes=[b_])

    def conv_store(ci, s16, q="sp"):
        b_ = s16[ci % 2]
        S.do(q, dma(chunks[ci][1], b_.ap[:, 0:chunks[ci][2]]), reads=[b_], semkey=b_.key + "s")

    NCH0 = 8
    for ci in range(NCH0):
        conv_load(ci, stg32)
        conv_cast(ci, stg32, stg16, "dve" if ci % 2 == 0 else "pool")
        conv_store(ci, stg16)

    t5sb = A0.t([32, 4], F32, "t5sb")
    ohsb = A0.t([32, 1280], F32, "ohsb")
    Fsb = A0.t([4, 1280], F32, "Fsb")
    X32 = A0.t([128, 4 * 1152], F32, "X32")
    Xhi = A0.t([128, 4 * 1152], BF16, "Xhi")
    Xr = A0.t([128, 4 * 1152], F32, "Xr")
    Xlo = A0.t([128, 4 * 1152], BF16, "Xlo")
    S.do("sp", dma(t5sb.ap, t5), writes=[t5sb], semkey="t5sb")
    S.do("sp", dma(ohsb.ap, oh_d), writes=[ohsb], semkey="ohsb")
    for c in range(3):
        w = 512 if c < 2 else 256
        S.do("pe", (lambda c=c, w=w: lambda e: e.matmul(pbank[c].ap[0:4, 0:w], lhsT=t5sb.ap[:, :], rhs=ohsb.ap[:, c * 512:c * 512 + w], start=True, stop=True))(),
             reads=[t5sb, ohsb], writes=[pbank[c]])
        S.do("dve", (lambda c=c, w=w: lambda e: e.tensor_copy(out=Fsb.ap[:, c * 512:c * 512 + w], in_=pbank[c].ap[0:4, 0:w]))(), reads=[pbank[c]], writes=[Fsb])
    S.do("sp", dma(Fd_h.ap(), Fsb.ap), reads=[Fsb], semkey="Fd")
    fdst = S.last_dma["Fd"]
    for h in range(4):
        S.do("sp", dma(X32.ap[:, h * 1152:(h + 1) * 1152], bass.AP(tensor=Fd_h, offset=h * 1280, ap=[[1, 128], [1, 1152]])),
             writes=[X32], extra=[fdst], semkey="X32")
    S.do("dve", lambda e: e.tensor_copy(out=Xhi.ap, in_=X32.ap), reads=[X32], writes=[Xhi])
    S.do("dve", lambda e: e.tensor_tensor(out=Xr.ap, in0=X32.ap, in1=Xhi.ap, op=ALU.subtract), reads=[X32, Xhi], writes=[Xr])
    S.do("dve", lambda e: e.tensor_copy(out=Xlo.ap, in_=Xr.ap), reads=[Xr], writes=[Xlo])
    S.do("sp", dma(Xd[0], Xhi.ap), reads=[Xhi], semkey="Xd")
    S.do("sp", dma(Xd[1], Xlo.ap), reads=[Xlo], semkey="Xd")
    S.barrier()

    def rsqrt_inplace(buf, apf):
        S.do("act", lambda e: e.activation(out=apf(), in_=apf(), func=AF.Sqrt), reads=[buf], writes=[buf])
        S.do("dve", lambda e: e.reciprocal(out=apf(), in_=apf()), reads=[buf], writes=[buf])

    for l in range(nlayers):
        lam_init = 0.8 - 0.6 * math.exp(-0.3 * l)
        S.do("sp", dma(bg.ap, b_gate[l]), writes=[bg], semkey="c_bg")
        S.do("sp", dma(gq8.ap, qkq[l].partition_broadcast(128)), writes=[gq8], semkey="c_gq")
        S.do("sp", dma(gk.ap, qkk[l].partition_broadcast(128)), writes=[gk], semkey="c_gk")
        S.do("sp", dma(gsub.ap, subln[l].partition_broadcast(128)), writes=[gsub], semkey="c_gsub")
        S.do("sp", dma(lam4.ap.rearrange("p a b -> p (a b)"), lamv[l].rearrange("a b -> (a b)").partition_broadcast(128)), writes=[lam4], semkey="c_lam")
        S.do("pool", lambda e: e.tensor_scalar(out=gq8.ap, in0=gq8.ap, scalar1=0.125, scalar2=None, op0=ALU.mult), reads=[gq8], writes=[gq8])
        S.do("pool", (lambda sc_=float(1.0 - lam_init): lambda e: e.tensor_scalar(out=gsub.ap, in0=gsub.ap, scalar1=sc_, scalar2=None, op0=ALU.mult))(), reads=[gsub], writes=[gsub])
        for (src, dst) in ((gq8, gqs8), (gk, gks)):
            for blk in range(4):
                sb = blk ^ 1
                S.do("pool", (lambda src=src, dst=dst, blk=blk, sb=sb: lambda e: e.tensor_copy(out=dst.ap[:, blk * 16:(blk + 1) * 16], in_=src.ap[:, sb * 16:(sb + 1) * 16]))(),
                     reads=[src], writes=[dst])
        S.do("dve", lambda e: e.tensor_tensor(out=lamt.ap[:, 0, :], in0=lam4.ap[:, 0, :], in1=lam4.ap[:, 1, :], op=ALU.mult), reads=[lam4], writes=[lamt])
        S.do("dve", lambda e: e.tensor_tensor(out=lamt.ap[:, 1, :], in0=lam4.ap[:, 2, :], in1=lam4.ap[:, 3, :], op=ALU.mult), reads=[lam4, lamt], writes=[lamt])
        S.do("dve", lambda e: e.tensor_reduce(out=lams.ap, in_=lamt.ap, axis=AX.X, op=ALU.add), reads=[lamt], writes=[lams])
        S.do("act", lambda e: e.activation(out=lame.ap, in_=lams.ap, func=AF.Exp), reads=[lams], writes=[lame])
        S.do("dve", lambda e: e.tensor_tensor(out=neglam.ap, in0=lame.ap[:, 1:2], in1=lame.ap[:, 0:1], op=ALU.subtract), reads=[lame], writes=[neglam])
        S.do("dve", (lambda li=lam_init: lambda e: e.tensor_scalar(out=neglam.ap, in0=neglam.ap, scalar1=float(-li), scalar2=None, op0=ALU.add))(), reads=[neglam], writes=[neglam])

        AA = Alloc(PBASE)
        wA = AA.t([128, 8, INC], BF16, "wA")
        g1b = AA.t([128, D], F32, "g1b")
        xin = [AA.t([128, 4, D], F32, f"xin{i}") for i in range(2)]
        junk = AA.t([128, 4, D], F32, "junk")
        ss4 = AA.t([128, 4], F32, "ss4")
        rstd4 = AA.t([128, 4], F32, "rstd4")
        hbf = AA.t([128, 4, D], BF16, "hbf")
        hT = [AA.t([128, 8, 512], BF16, f"hT{i}") for i in range(2)]
        fmst = [AA.t([128, 4, 512], BF16, f"fmst{i}") for i in range(4)]
        va_st = AA.t([128, 4, 512], BF16, "va_st")
        vb_st = AA.t([128, 4, 128], BF16, "vb_st")
        qbT_st = AA.t([128, 4, 512], BF16, "qbT_st")
        kbT_st = AA.t([128, 512], BF16, "kbT_st")
        ropet = [AA.t([128, 4, 128], F32, f"ropet{i}") for i in range(2)]
        tq = [AA.t([128, 512], F32, f"tq{i}") for i in range(3)]
        tk = [AA.t([128, 128], F32, f"tk{i}") for i in range(3)]
        qb_bf = [AA.t([128, 512], BF16, f"qb_bf{i}") for i in range(4)]
        kb_bf = [AA.t([128, 128], BF16, f"kb_bf{i}") for i in range(4)]
        CqSq = AA.t([128, 128], F32, "CqSq")
        CkSk = AA.t([128, 128], F32, "CkSk")
        ss8 = AA.t([128, 8], F32, "ss8")

        S.do("sp", dma(wA.ap, wb_in[l * D:(l + 1) * D, :].rearrange("(c p) n -> p c n", p=128)), writes=[wA], semkey="wA")
        S.do("sp", dma(g1b.ap, norm1[l].partition_broadcast(128)), writes=[g1b], semkey="g1b")

        tilesA = []
        for sn, N in seqs:
            for t in range(N // 512):
                tilesA.append((sn, t))
        fmi_box = [0]

        def a_load(i):
            sn, t = tilesA[i]
            sc = scr[sn]
            xsrc = x_in[sn] if l == 0 else sc["X2"]
            xb = xin[i % 2]
            rp = ropet[i % 2]
            S.do("sp", dma(xb.ap, xsrc[t * 512:(t + 1) * 512, :].rearrange("(j p) d -> p j d", p=128)), writes=[xb], semkey=xb.key)
            S.do("sp", dma(rp.ap, rope_d[t * 512:(t + 1) * 512, :].rearrange("(j p) d -> p j d", p=128)), writes=[rp], semkey=rp.key)

        t0q = [tq[0]] + [AA.t([128, 512], F32, f"t0q{j}") for j in range(1, 4)]
        t0k = [tk[0]] + [AA.t([128, 128], F32, f"t0k{j}") for j in range(1, 4)]
        ssA = AA.t([128, 4, 10], F32, "ssA")

        def a_norm_a(i):
            xb = xin[i % 2]
            S.do("pool", (lambda xb=xb: lambda e: e.tensor_tensor(out=junk.ap, in0=xb.ap, in1=xb.ap, op=ALU.mult))(), reads=[xb], writes=[junk])
            S.do("dve", lambda e: e.tensor_reduce(out=ss4.ap, in_=junk.ap, axis=AX.X, op=ALU.add), reads=[junk], writes=[ss4])
            S.do("dve", lambda e: e.tensor_scalar(out=rstd4.ap, in0=ss4.ap, scalar1=1.0 / D, scalar2=EPS, op0=ALU.mult, op1=ALU.add), reads=[ss4], writes=[rstd4])

        def a_norm_b(i):
            xb = xin[i % 2]
            rsqrt_inplace(rstd4, lambda: rstd4.ap)
            ws = []
            prev = list(hbf.r) + list(hbf.w)
            for j in range(4):
                ws.append(S.do("dve", (lambda j=j, xb=xb: lambda e: e.scalar_tensor_tensor(out=hbf.ap[:, j, :], in0=xb.ap[:, j, :], scalar=rstd4.ap[:, j:j + 1], in1=g1b.ap, op0=ALU.mult, op1=ALU.mult))(),
                               reads=[xb, rstd4, g1b], extra=prev))
            hbf.w = ws
            hbf.r = []

        def a_tr(i):
            hTb = hT[i % 2]
            ws = []
            prev = list(hTb.r) + list(hTb.w)
            for c in range(8):
                pb = pbank[c % 2]
                pv = pb.ap.bitcast(BF16)

                def f(e, c=c, pv=pv):
                    last = None
                    for j in range(4):
                        last = e.transpose(pv[:, j * 128:(j + 1) * 128], hbf.ap[:, j, c * 128:(c + 1) * 128], ident.ap)
                    return last
                S.do("pe", f, reads=[hbf, ident], writes=[pb])
                if c % 2 == 0:
                    o = S.do("act", (lambda c=c, pv=pv, hTb=hTb: lambda e: e.copy(out=hTb.ap[:, c, :], in_=pv[:, 0:512]))(), reads=[pb], extra=prev)
                else:
                    o = S.do("dve", (lambda c=c, pv=pv, hTb=hTb: lambda e: e.tensor_copy(out=hTb.ap[:, c, :], in_=pv[:, 0:512]))(), reads=[pb], extra=prev)
                ws.append(o)
            hTb.w = ws
            hTb.r = []

        def a_tm1(i):
            hTb = hT[i % 2]
            vaw, vbw = [], []
            prev_va = list(va_st.r) + list(va_st.w)
            prev_vb = list(vb_st.r) + list(vb_st.w)
            ssw = []
            prev_ss = list(ssA.r) + list(ssA.w)
            for j in range(4):
                def mmtm(pb, c0, w, j=j, hTb=hTb):
                    def f(e):
                        last = None
                        for c in range(8):
                            last = e.matmul(pb.ap[:, 0:w], lhsT=hTb.ap[:, c, j * 128:(j + 1) * 128], rhs=wA.ap[:, c, c0:c0 + w], start=(c == 0), stop=(c == 7))
                        return last
                    return f
                pb = pbank[4]
                S.do("pe", mmtm(pb, 1024, 512), reads=[wA, hTb], writes=[pb])
                vaw.append(S.do("act", (lambda j=j, pb=pb: lambda e: e.copy(out=va_st.ap[:, j, :], in_=pb.ap))(), reads=[pb], extra=prev_va))
                for (nh, pbi, c0, w, t0, t1, h0) in ((8, 5, 1536, 512, t0q[j], tq[1], 0), (2, 6, 2048, 256, t0k[j], tk[1], 8)):
                    pb = pbank[pbi]
                    W = nh * 64
                    S.do("pe", mmtm(pb, c0, w), reads=[wA, hTb], writes=[pb])
                    if nh == 2:
                        vbw.append(S.do("act", (lambda j=j, pb=pb: lambda e: e.copy(out=vb_st.ap[:, j, :], in_=pb.ap[:, 128:256]))(), reads=[pb], extra=prev_vb))
                    S.do("act", (lambda pb=pb, t0=t0, W=W: lambda e: e.copy(out=t0.ap[:, 0:W], in_=pb.ap[:, 0:W]))(), reads=[pb], writes=[t0])
                    S.do("pool", (lambda t0=t0, t1=t1, W=W: lambda e: e.tensor_tensor(out=t1.ap[:, 0:W], in0=t0.ap[:, 0:W], in1=t0.ap[:, 0:W], op=ALU.mult))(), reads=[t0], writes=[t1])
                    ssw.append(S.do("dve", (lambda t1=t1, nh=nh, W=W, j=j, h0=h0: lambda e: e.tensor_reduce(out=ssA.ap[:, j, h0:h0 + nh], in_=t1.ap[:, 0:W].rearrange("p (h d) -> p h d", d=64), axis=AX.X, op=ALU.add))(),
                                    reads=[t1], extra=prev_ss))
            va_st.w = vaw
            va_st.r = []
            vb_st.w = vbw
            vb_st.r = []
            ssA.w = ssw
            ssA.r = []

        def a_qkrs(i):
            S.do("dve", lambda e: e.tensor_scalar(out=ssA.ap, in0=ssA.ap, scalar1=1.0 / 64, scalar2=EPS, op0=ALU.mult, op1=ALU.add), reads=[ssA], writes=[ssA])
            rsqrt_inplace(ssA, lambda: ssA.ap)

        def a_st3(i):
            rp = ropet[i % 2]
            for j in range(4):
                S.do("pool", (lambda j=j, rp=rp: lambda e: e.tensor_tensor(out=CqSq.ap[:, 0:64], in0=rp.ap[:, j, 0:64], in1=gq8.ap, op=ALU.mult))(), reads=[rp, gq8], writes=[CqSq])
                S.do("pool", (lambda j=j, rp=rp: lambda e: e.tensor_tensor(out=CqSq.ap[:, 64:128], in0=rp.ap[:, j, 64:128], in1=gqs8.ap, op=ALU.mult))(), reads=[rp, gqs8, CqSq], writes=[CqSq])
                S.do("pool", (lambda j=j, rp=rp: lambda e: e.tensor_tensor(out=CkSk.ap[:, 0:64], in0=rp.ap[:, j, 0:64], in1=gk.ap, op=ALU.mult))(), reads=[rp, gk], writes=[CkSk])
                S.do("pool", (lambda j=j, rp=rp: lambda e: e.tensor_tensor(out=CkSk.ap[:, 64:128], in0=rp.ap[:, j, 64:128], in1=gks.ap, op=ALU.mult))(), reads=[rp, gks, CkSk], writes=[CkSk])
                for (nh, t0, t1, t2, obf, tab, h0) in ((8, t0q[j], tq[1], tq[2], qb_bf[j], CqSq, 0), (2, t0k[j], tk[1], tk[2], kb_bf[j], CkSk, 8)):
                    W = nh * 64
                    S.do("dve", (lambda t0=t0, t1=t1, nh=nh, W=W, j=j, h0=h0: lambda e: e.tensor_tensor(out=t1.ap[:, 0:W].rearrange("p (h d) -> p h d", d=64), in0=t0.ap[:, 0:W].rearrange("p (h d) -> p h d", d=64),
                                                                                                        in1=ssA.ap[:, j, h0:h0 + nh].unsqueeze(2).to_broadcast([128, nh, 64]), op=ALU.mult))(), reads=[t0, ssA], writes=[t1])
                    S.do("pool", (lambda t1=t1, t2=t2, nh=nh, W=W, tab=tab: lambda e: e.tensor_tensor(out=t2.ap[:, 0:W].rearrange("p (h d) -> p h d", d=64), in0=t1.ap[:, 0:W].rearrange("p (h d) -> p h d", d=64),
                                                                                                        in1=tab.ap[:, 0:64].unsqueeze(1).to_broadcast([128, nh, 64]), op=ALU.mult))(), reads=[t1, tab], writes=[t2])
                    ops = []
                    prev0 = list(t0.r) + list(t0.w)
                    for xh in range(2):
                        eng = "dve" if xh == 0 else "pool"

                        def f(e, xh=xh, t0=t0, t1=t1, nh=nh, W=W, tab=tab):
                            o_ = t0.ap[:, 0:W].rearrange("p (h a x d) -> p h a x d", a=2, x=2, d=16)[:, :, :, xh, :]
                            i_ = t1.ap[:, 0:W].rearrange("p (h a x d) -> p h a x d", a=2, x=2, d=16)[:, :, :, 1 - xh, :]
                            s_ = tab.ap[:, 64:128].rearrange("p (a x d) -> p a x d", a=2, x=2, d=16)[:, :, xh, :].unsqueeze(1).to_broadcast([128, nh, 2, 16])
                            return e.tensor_tensor(out=o_, in0=i_, in1=s_, op=ALU.mult)
                        ops.append(S.do(eng, f, reads=[t1, tab], extra=prev0))
                    t0.w = ops
                    t0.r = []
                    S.do("dve", (lambda t0=t0, t2=t2, obf=obf, W=W: lambda e: e.tensor_tensor(out=obf.ap[:, 0:W], in0=t0.ap[:, 0:W], in1=t2.ap[:, 0:W], op=ALU.add))(), reads=[t0, t2], writes=[obf])

        def a_fm(i, glist):
            sn, t = tilesA[i]
            sc = scr[sn]
            hTb = hT[i % 2]
            groups = [("QA", 0, sc["QAT"], 0), ("KA", 512, sc["KAT"], 0)] + [("G", 2304 + 512 * k, sc["GT"], 512 * k) for k in range(4)]
            for gidx in glist:
                kind, col0, dst, drow0 = groups[gidx]
                st = fmst[fmi_box[0] % 4]
                fmi_box[0] += 1
                ws = []
                prev = list(st.r) + list(st.w)
                for b in range(4):
                    pb = pbank[2 + (b % 2)]
                    cc = col0 + b * 128

                    def f(e, cc=cc, pb=pb, hTb=hTb):
                        last = None
                        for c in range(8):
                            last = e.matmul(pb.ap, lhsT=wA.ap[:, c, cc:cc + 128], rhs=hTb.ap[:, c, :], start=(c == 0), stop=(c == 7))
                        return last
                    S.do("pe", f, reads=[wA, hTb], writes=[pb])
                    if kind == "QA":
                        o = S.do("act", (lambda st=st, b=b, pb=pb: lambda e: e.mul(out=st.ap[:, b, :], in_=pb.ap, mul=0.125))(), reads=[pb], extra=prev)
                    elif kind == "KA":
                        o = S.do("act", (lambda st=st, b=b, pb=pb: lambda e: e.copy(out=st.ap[:, b, :], in_=pb.ap))(), reads=[pb], extra=prev)
                    else:
                        gbi = (col0 - 2304) // 128 + b
                        o = S.do("act", (lambda st=st, b=b, pb=pb, gbi=gbi: lambda e: e.activation(out=st.ap[:, b, :], in_=pb.ap, func=AF.Sigmoid, bias=bg.ap[:, gbi:gbi + 1], scale=1.0))(), reads=[pb, bg], extra=prev)
                    ws.append(o)
                st.w = ws
                st.r = []
                S.do("sp", dma(dst[drow0:drow0 + 512, t * 512:(t + 1) * 512].rearrange("(b p) n -> p b n", p=128), st.ap), reads=[st], semkey=st.key)

        def a_qktr(i):
            sn, t = tilesA[i]
            sc = scr[sn]
            qtw, ktw = [], []
            prev_qt = list(qbT_st.r) + list(qbT_st.w)
            prev_kt = list(kbT_st.r) + list(kbT_st.w)
            for j in range(4):
                for nh, obf in ((8, qb_bf[j]), (2, kb_bf[j])):
                    ncb = nh * 64 // 128
                    tpb = pbank[7]
                    pv = tpb.ap.bitcast(BF16)

                    def ft(e, obf=obf, ncb=ncb, pv=pv):
                        last = None
                        for cbk in range(ncb):
                            last = e.transpose(pv[:, cbk * 128:(cbk + 1) * 128], obf.ap[:, cbk * 128:(cbk + 1) * 128], ident.ap)
                        return last
                    S.do("pe", ft, reads=[obf, ident], writes=[tpb])
                    if nh == 8:
                        qtw.append(S.do("dve", (lambda j=j, pv=pv: lambda e: e.tensor_copy(out=qbT_st.ap[:, :, j * 128:(j + 1) * 128], in_=pv[:, 0:512].rearrange("p (c n) -> p c n", n=128)))(), reads=[tpb], extra=prev_qt))
                    else:
                        ktw.append(S.do("dve", (lambda j=j, pv=pv: lambda e: e.tensor_copy(out=kbT_st.ap[:, j * 128:(j + 1) * 128], in_=pv[:, 0:128]))(), reads=[tpb], extra=prev_kt))
            qbT_st.w = qtw
            qbT_st.r = []
            kbT_st.w = ktw
            kbT_st.r = []
            S.do("sp", dma(sc["VA"][t * 512:(t + 1) * 512, :].rearrange("(j p) d -> p j d", p=128), va_st.ap), reads=[va_st], semkey="va_st")
            S.do("sp", dma(sc["VB"][t * 512:(t + 1) * 512, :].rearrange("(j p) d -> p j d", p=128), vb_st.ap), reads=[vb_st], semkey="vb_st")
            S.do("sp", dma(sc["QBT"][:, t * 512:(t + 1) * 512].rearrange("(c p) n -> p c n", p=128), qbT_st.ap), reads=[qbT_st], semkey="qbT_st")
            S.do("sp", dma(sc["KBT"][:, t * 512:(t + 1) * 512], kbT_st.ap), reads=[kbT_st], semkey="kbT_st")

        nTA = len(tilesA)
        a_load(0)
        a_norm_a(0)
        a_norm_b(0)
        a_tr(0)
        for i in range(nTA):
            nxt = i + 1 < nTA
            if nxt:
                a_load(i + 1)
            a_tm1(i)
            if nxt:
                a_norm_a(i + 1)
            a_fm(i, [0, 1])
            a_qkrs(i)
            a_fm(i, [2, 3])
            if nxt:
                a_norm_b(i + 1)
            a_st3(i)
            a_fm(i, [4, 5])
            a_qktr(i)
            if nxt:
                a_tr(i + 1)
        S.barrier()

        AB = Alloc(PBASE)
        NKBM = NMAX // 128
        KT = [AB.t([128, NMAX], BF16, f"KT{i}") for i in range(2)]
        VV = [AB.t([128, NKBM, 129], BF16, f"VV{i}") for i in range(2)]
        QT = [AB.t([128, 512], BF16, f"QT{i}") for i in range(3)]
        PP = [AB.t([128, 2, 512], BF16, f"PP{i}") for i in range(4)]
        XH = AB.t([128, 4 * 1152], BF16, "XH")
        XL = AB.t([128, 4 * 1152], BF16, "XL")
        accS = AB.t([128, 4, 258], F32, "accS")
        rl = AB.t([128, 8], F32, "rl")
        e0 = AB.t([128, 4, 128], F32, "e0")
        e1 = AB.t([128, 4, 128], F32, "e1")
        e2 = AB.t([128, 4, 128], F32, "e2")
        ssb = AB.t([128, 4], F32, "ssb")
        ost = [AB.t([128, 4, 128], BF16, f"ost{i}") for i in range(2)]
        bstg32 = [AB.t([128, CW], F32, f"bstg32_{i}") for i in range(2)]
        bstg16 = [AB.t([128, CW], BF16, f"bstg16_{i}") for i in range(2)]
        bg_state = {"k": 0}

        def bg_step():
            if l != 0:
                return False
            k = bg_state["k"]
            nrem = len(chunks) - NCH0
            if k >= nrem + 2:
                return False
            if k < nrem:
                conv_load(NCH0 + k, bstg32, "pool")
            if 1 <= k <= nrem:
                conv_cast(NCH0 + k - 1, bstg32, bstg16, "pool")
            if 2 <= k <= nrem + 1:
                conv_store(NCH0 + k - 2, bstg16, "pool")
            bg_state["k"] = k + 1
            return True
        S.do("sp", dma(XH.ap, Xd[0]), writes=[XH], semkey="XH")
        S.do("sp", dma(XL.ap, Xd[1]), writes=[XL], semkey="XL")
        Spair = [(pbank[0], pbank[1]), (pbank[2], pbank[3])]
        def pair_ap(i):
            return psum_all[:, 2 * i:2 * i + 2, :]
        acc_b = pbank[4:8]
        gi = 0
        ui = 0
        ji = 0
        qi = 0
        pendq = []
        for sn, N in seqs:
            sc = scr[sn]
            NKB = N // 128
            NQC = N // 512
            for grp in range(8):
                diff = grp < 4
                E = 128 if diff else 64
                kt, vv = KT[gi % 2], VV[gi % 2]
                gi += 1
                if diff:
                    h = grp
                    S.do("sp", dma(kt.ap[:, 0:N], sc["KAT"][h * 128:(h + 1) * 128, :]), writes=[kt], semkey=kt.key)
                    S.do("sp", dma(vv.ap[:, 0:NKB, 0:128], sc["VA"][:, h * 128:(h + 1) * 128].rearrange("(k p) e -> p k e", p=128)), writes=[vv], semkey=vv.key)
                    qsrc = sc["QAT"][h * 128:(h + 1) * 128, :]
                else:
                    pr = grp - 4
                    g = pr // 2
                    o1 = S.do("sp", dma(kt.ap[0:64, 0:N], sc["KBT"][g * 64:(g + 1) * 64, :]), writes=[kt], semkey=kt.key)
                    o2 = S.do("sp", dma(kt.ap[64:128, 0:N], sc["KBT"][g * 64:(g + 1) * 64, :]), extra=list(o1.deps), semkey=kt.key)
                    kt.w = [o1, o2]
                    S.do("sp", dma(vv.ap[:, 0:NKB, 0:64], sc["VB"][:, g * 64:(g + 1) * 64].rearrange("(k p) e -> p k e", p=128)), writes=[vv], semkey=vv.key)
                    qsrc = sc["QBT"][pr * 128:(pr + 1) * 128, :]
                lw = list(vv.w)
                om = S.do("pool", (lambda vv=vv, E=E, NKB=NKB: lambda e: e.memset(vv.ap[:, 0:NKB, E:E + 1], 1.0))(), extra=lw + [x for x in lw[0].deps])
                vv.w = lw + [om]
                for qc in range(NQC):
                    qt = QT[qi % 3]
                    qi += 1
                    S.do("sp", dma(qt.ap, qsrc[:, qc * 512:(qc + 1) * 512]), writes=[qt], semkey=qt.key)
                    for kb in range(NKB):
                        sp0, sp1 = Spair[ui % 2]
                        pp = PP[ui % 4]
                        near = diff and (4 * qc - 1 <= kb <= 4 * qc + 4)
                        d_off = kb * 128 - qc * 512
                        wcol = 512 - d_off

                        def fqk(e, kt=kt, qt=qt, kb=kb, sp0=sp0, sp1=sp1, near=near, wcol=wcol, grp=grp):
                            last = None
                            for c, spx in ((0, sp0), (1, sp1)):
                                last = e.matmul(spx.ap, lhsT=kt.ap[c * 64:(c + 1) * 64, kb * 128:(kb + 1) * 128], rhs=qt.ap[c * 64:(c + 1) * 64, :], start=True, stop=not near)
                            if near:
                                for c, spx in ((0, sp0), (1, sp1)):
                                    e.matmul(spx.ap, lhsT=jmat.ap, rhs=XH.ap[:, grp * 1152 + wcol:grp * 1152 + wcol + 512], start=False, stop=False)
                                    last = e.matmul(spx.ap, lhsT=jmat.ap, rhs=XL.ap[:, grp * 1152 + wcol:grp * 1152 + wcol + 512], start=False, stop=True)
                            return last
                        S.do("pe", fqk, reads=[kt, qt] + ([XH, XL, jmat] if near else []), writes=[sp0, sp1])
                        if len(pendq) >= 2:
                            pendq.pop(0)()
                        if diff and not near:
                            bcol = grp if kb < 4 * qc else 4 + grp
                            bias_ap = cb.ap[:, bcol:bcol + 1]
                        else:
                            bias_ap = None
                        pa = pair_ap(ui % 2)

                        def fex(e, pa=pa, pp=pp, bias_ap=bias_ap):
                            if bias_ap is None:
                                return e.activation(out=pp.ap, in_=pa, func=AF.Exp)
                            return e.activation(out=pp.ap, in_=pa, func=AF.Exp, bias=bias_ap, scale=1.0)
                        S.do("act", fex, reads=[sp0, sp1] + ([cb] if bias_ap is not None else []), writes=[pp])

                        def mk_pv(pp=pp, vv=vv, kb=kb, NKB=NKB, E=E, diff=diff, qc=qc, grp=grp, sc=sc, ji=ji, kt=kt, qt=qt):
                            def fpv(e):
                                last = None
                                for c in range(2):
                                    for qb in range(4):
                                        if diff:
                                            bk = acc_b[c * 2 + qb // 2]
                                            col = (qb % 2) * 129
                                        else:
                                            bk = acc_b[c]
                                            col = qb * 65
                                        first_in_bank = (qb % 2 == 0) if diff else (qb == 0)
                                        last = e.matmul(bk.ap[:, col:col + E + 1], lhsT=pp.ap[:, c, qb * 128:(qb + 1) * 128], rhs=vv.ap[:, kb, 0:E + 1],
                                                        start=(kb == 0 and first_in_bank), stop=(kb == NKB - 1), skip_group_check=True)
                                return last
                            S.do("pe", fpv, reads=[pp, vv], writes=(acc_b if diff else acc_b[0:2]) if kb == 0 else [], extra=[])
                            if kb == NKB - 1:
                                ob = ost[ji % 2]
                                if diff:
                                    def fcp(e):
                                        return e.tensor_copy(out=accS.ap, in_=psum_all[:, 4:8, 0:258])
                                    o_cp = S.do("dve", fcp, reads=[], writes=[accS], extra=[S.q["pe"][-1]])
                                    for bkx in acc_b:
                                        bkx.r.append(o_cp)
                                    av = accS.ap.rearrange("p b (i e) -> p (b i) e", e=129)
                                    S.do("dve", lambda e: e.reciprocal(out=rl.ap, in_=av[:, :, 128]), reads=[accS], writes=[rl])
                                    S.do("dve", lambda e: e.tensor_scalar(out=rl.ap[:, 4:8], in0=rl.ap[:, 4:8], scalar1=neglam.ap[:, 0:1], scalar2=None, op0=ALU.mult), reads=[rl, neglam], writes=[rl])
                                    S.do("pool", lambda e: e.tensor_tensor(out=e0.ap, in0=av[:, 0:4, 0:128], in1=rl.ap[:, 0:4].unsqueeze(2).to_broadcast([128, 4, 128]), op=ALU.mult), reads=[accS, rl], writes=[e0])
                                    S.do("dve", lambda e: e.tensor_tensor(out=e1.ap, in0=av[:, 4:8, 0:128], in1=rl.ap[:, 4:8].unsqueeze(2).to_broadcast([128, 4, 128]), op=ALU.mult), reads=[accS, rl], writes=[e1])
                                    S.do("pool", lambda e: e.tensor_tensor(out=e0.ap, in0=e0.ap, in1=e1.ap, op=ALU.add), reads=[e0, e1], writes=[e0])
                                    S.do("pool", lambda e: e.tensor_tensor(out=e2.ap, in0=e0.ap, in1=e0.ap, op=ALU.mult), reads=[e0], writes=[e2])
                                    S.do("dve", lambda e: e.tensor_reduce(out=ssb.ap, in_=e2.ap, axis=AX.X, op=ALU.add), reads=[e2], writes=[ssb])
                                    S.do("dve", lambda e: e.tensor_scalar(out=ssb.ap, in0=ssb.ap, scalar1=1.0 / 128, scalar2=SUBLN_EPS, op0=ALU.mult, op1=ALU.add), reads=[ssb], writes=[ssb])
                                    rsqrt_inplace(ssb, lambda: ssb.ap)
                                    S.do("dve", lambda e: e.tensor_tensor(out=e1.ap, in0=e0.ap, in1=ssb.ap.unsqueeze(2).to_broadcast([128, 4, 128]), op=ALU.mult), reads=[e0, ssb], writes=[e1])
                                    S.do("pool", lambda e: e.tensor_tensor(out=ob.ap, in0=e1.ap, in1=gsub.ap.unsqueeze(1).to_broadcast([128, 4, 128]), op=ALU.mult), reads=[e1, gsub], writes=[ob])
                                    S.do("sp", dma(sc["OA"][qc * 512:(qc + 1) * 512, grp * 128:(grp + 1) * 128].rearrange("(j p) e -> p j e", p=128), ob.ap), reads=[ob], semkey=ob.key)
                                else:
                                    pr = grp - 4

                                    def fcp(e):
                                        return e.tensor_copy(out=accS.ap.rearrange("p b w -> p (b w)")[:, 0:520].rearrange("p (b w) -> p b w", w=260), in_=psum_all[:, 4:6, 0:260])
                                    o_cp = S.do("dve", fcp, reads=[], writes=[accS], extra=[S.q["pe"][-1]])
                                    for bkx in acc_b[0:2]:
                                        bkx.r.append(o_cp)
                                    av = accS.ap.rearrange("p b w -> p (b w)")[:, 0:520].rearrange("p (i e) -> p i e", e=65)
                                    S.do("dve", lambda e: e.reciprocal(out=rl.ap, in_=av[:, :, 64]), reads=[accS], writes=[rl])
                                    def fo(e):
                                        o_ = ob.ap.rearrange("p j (hd d) -> p hd j d", d=64)
                                        i_ = av[:, :, 0:64].rearrange("p (hd j) d -> p hd j d", j=4)
                                        r_ = rl.ap.rearrange("p (hd j) -> p hd j", j=4).unsqueeze(3).to_broadcast([128, 2, 4, 64])
                                        return e.tensor_tensor(out=o_, in0=i_, in1=r_, op=ALU.mult)
                                    S.do("dve", fo, reads=[accS, rl], writes=[ob])
                                    S.do("sp", dma(sc["OB"][qc * 512:(qc + 1) * 512, pr * 128:(pr + 1) * 128].rearrange("(j p) e -> p j e", p=128), ob.ap), reads=[ob], semkey=ob.key)
                        pendq.append(mk_pv)
                        if kb == NKB - 1:
                            bg_step()
                        if kb == NKB - 1:
                            ji += 1
                        ui += 1
        while pendq:
            pendq.pop(0)()
        while bg_step():
            pass
        S.barrier()

        AC = Alloc(PBASE)
        wua = AC.t([128, 4, D], BF16, "wua")
        wub = AC.t([128, 4, D], BF16, "wub")
        wo = AC.t([128, 8, D], BF16, "wo")
        oat = [AC.t([128, 4, 512], BF16, f"oat{i}") for i in range(2)]
        obt = [AC.t([128, 4, 512], BF16, f"obt{i}") for i in range(2)]
        gtt = [AC.t([128, 16, 512], BF16, f"gtt{i}") for i in range(2)]
        xc = [AC.t([128, 4, D], F32, f"xc{i}") for i in range(2)]
        oT = AC.t([128, 8, 512], BF16, "oT")
        uT = AC.t([128, 8, 512], BF16, "uT")
        u1 = [AC.t([128, 512], F32, f"u1_{i}") for i in range(2)]
        u2 = [AC.t([128, 512], F32, f"u2_{i}") for i in range(2)]
        S.do("sp", dma(wua.ap, wb_ua[l * 512:(l + 1) * 512, :].rearrange("(c p) n -> p c n", p=128)), writes=[wua], semkey="wua")
        S.do("sp", dma(wub.ap, wb_ub[l * 512:(l + 1) * 512, :].rearrange("(c p) n -> p c n", p=128)), writes=[wub], semkey="wub")
        S.do("sp", dma(wo.ap, wb_o[l * D:(l + 1) * D, :].rearrange("(c p) n -> p c n", p=128)), writes=[wo], semkey="wo")
        gt = 0
        for sn, N in seqs:
            sc = scr[sn]
            xsrc = x_in[sn] if l == 0 else sc["X2"]
            for t in range(N // 512):
                oa_, ob_, g_, x_ = oat[gt % 2], obt[gt % 2], gtt[gt % 2], xc[gt % 2]
                gt += 1
                S.do("sp", dma(oa_.ap, sc["OA"][t * 512:(t + 1) * 512, :].rearrange("(j p) e -> p j e", p=128)), writes=[oa_], semkey=oa_.key)
                S.do("sp", dma(ob_.ap, sc["OB"][t * 512:(t + 1) * 512, :].rearrange("(j p) e -> p j e", p=128)), writes=[ob_], semkey=ob_.key)
                S.do("sp", dma(g_.ap, sc["GT"][:, t * 512:(t + 1) * 512].rearrange("(b p) n -> p b n", p=128)), writes=[g_], semkey=g_.key)
                S.do("sp", dma(x_.ap, xsrc[t * 512:(t + 1) * 512, :].rearrange("(j p) d -> p j d", p=128)), writes=[x_], semkey=x_.key)
                ws = []
                prev = list(oT.r) + list(oT.w)
                for cb8 in range(8):
                    srcb = oa_ if cb8 < 4 else ob_
                    cbk = cb8 % 4
                    pb = pbank[cb8 % 2]
                    pv = pb.ap.bitcast(BF16)

                    def f(e, srcb=srcb, cbk=cbk, pv=pv):
                        last = None
                        for j in range(4):
                            last = e.transpose(pv[:, j * 128:(j + 1) * 128], srcb.ap[:, j, cbk * 128:(cbk + 1) * 128], ident.ap)
                        return last
                    S.do("pe", f, reads=[srcb, ident], writes=[pb])
                    if cb8 % 2 == 0:
                        ws.append(S.do("act", (lambda cb8=cb8, pv=pv: lambda e: e.copy(out=oT.ap[:, cb8, :], in_=pv[:, 0:512]))(), reads=[pb], extra=prev))
                    else:
                        ws.append(S.do("dve", (lambda cb8=cb8, pv=pv: lambda e: e.tensor_copy(out=oT.ap[:, cb8, :], in_=pv[:, 0:512]))(), reads=[pb], extra=prev))
                oT.w = ws
                oT.r = []
                ws = []
                prev = list(uT.r) + list(uT.w)
                for fb in range(8):
                    pa_, pb_ = pbank[2 + 2 * (fb % 2)], pbank[3 + 2 * (fb % 2)]
                    t1_, t2_ = u1[fb % 2], u2[fb % 2]

                    def f(e, fb=fb, pa_=pa_, pb_=pb_):
                        last = None
                        for c in range(4):
                            last = e.matmul(pa_.ap, lhsT=wua.ap[:, c, fb * 128:(fb + 1) * 128], rhs=oT.ap[:, c, :], start=(c == 0), stop=(c == 3))
                        for c in range(4):
                            last = e.matmul(pb_.ap, lhsT=wub.ap[:, c, fb * 128:(fb + 1) * 128], rhs=oT.ap[:, 4 + c, :], start=(c == 0), stop=(c == 3))
                        return last
                    S.do("pe", f, reads=[wua, wub, oT], writes=[pa_, pb_])
                    S.do("dve", (lambda fb=fb, pa_=pa_, t1_=t1_, g_=g_: lambda e: e.tensor_tensor(out=t1_.ap, in0=pa_.ap, in1=g_.ap[:, fb, :], op=ALU.mult))(), reads=[pa_, g_], writes=[t1_])
                    S.do("dve", (lambda fb=fb, pb_=pb_, t2_=t2_, g_=g_: lambda e: e.tensor_tensor(out=t2_.ap, in0=pb_.ap, in1=g_.ap[:, 8 + fb, :], op=ALU.mult))(), reads=[pb_, g_], writes=[t2_])
                    ws.append(S.do("pool", (lambda fb=fb, t1_=t1_, t2_=t2_: lambda e: e.tensor_tensor(out=uT.ap[:, fb, :], in0=t1_.ap, in1=t2_.ap, op=ALU.add))(), reads=[t1_, t2_], extra=prev))
                uT.w = ws
                uT.r = []
                ws = []
                for j in range(4):
                    for hf in range(2):
                        pb = pbank[6 + hf]

                        def f(e, j=j, hf=hf, pb=pb):
                            last = None
                            for c in range(8):
                                last = e.matmul(pb.ap, lhsT=uT.ap[:, c, j * 128:(j + 1) * 128], rhs=wo.ap[:, c, hf * 512:(hf + 1) * 512], start=(c == 0), stop=(c == 7))
                            return last
                        S.do("pe", f, reads=[uT, wo], writes=[pb])
                        ws.append(S.do("dve", (lambda j=j, hf=hf, pb=pb, x_=x_: lambda e: e.tensor_tensor(out=x_.ap[:, j, hf * 512:(hf + 1) * 512], in0=pb.ap, in1=x_.ap[:, j, hf * 512:(hf + 1) * 512], op=ALU.add))(), reads=[pb, x_]))
                x_.w = x_.w + ws
                S.do("sp", dma(sc["X1"][t * 512:(t + 1) * 512, :].rearrange("(j p) d -> p j d", p=128), x_.ap), reads=[x_], semkey=x_.key + "s")
        S.barrier()

        AD = Alloc(PBASE)
        w1 = AD.t([128, 8, DFF], BF16, "w1")
        w2 = AD.t([128, 32, D], BF16, "w2")
        g2b = AD.t([128, D], F32, "g2b")
        gfb = AD.t([128, D], F32, "gfb")
        xd = [AD.t([128, 2, D], F32, f"xd{i}") for i in range(3)]
        junk2 = AD.t([128, 2, D], F32, "junk2")
        h2 = AD.t([128, 2, D], BF16, "h2")
        h2T = AD.t([128, 8, 256], BF16, "h2T")
        aT = AD.t([128, 32, 256], BF16, "aT")
        rr = [AD.t([128, 256], F32, f"rr{i}") for i in range(2)]
        ssd = AD.t([128, 2], F32, "ssd")
        rsd = AD.t([128, 2], F32, "rsd")
        ssf = AD.t([128, 2], F32, "ssf")
        rsf = [AD.t([128, 2], F32, f"rsf{i}") for i in range(2)]
        S.do("sp", dma(w1.ap, wb_f1[l * D:(l + 1) * D, :].rearrange("(c p) n -> p c n", p=128)), writes=[w1], semkey="w1")
        for q4 in range(4):
            o = S.do("sp", dma(w2.ap[:, q4 * 8:(q4 + 1) * 8, :], wb_f2[l * DFF + q4 * 1024:l * DFF + (q4 + 1) * 1024, :].rearrange("(c p) n -> p c n", p=128)), semkey="w2")
        w2.w = [o]
        S.do("sp", dma(g2b.ap, norm2[l].partition_broadcast(128)), writes=[g2b], semkey="g2b")
        S.do("sp", dma(gfb.ap, norm_f.partition_broadcast(128)), writes=[gfb], semkey="gfb")
        tilesD = []
        for sn, N in seqs:
            for t in range(N // 256):
                tilesD.append((sn, t))
        last_layer = (l == nlayers - 1)

        def d_load(i):
            sn, t = tilesD[i]
            x_ = xd[i % 3]
            S.do("sp", dma(x_.ap, scr[sn]["X1"][t * 256:(t + 1) * 256, :].rearrange("(j p) d -> p j d", p=128)), writes=[x_], semkey=x_.key)

        def d_norm_a(i):
            x_ = xd[i % 3]
            S.do("pool", (lambda x_=x_: lambda e: e.tensor_tensor(out=junk2.ap, in0=x_.ap, in1=x_.ap, op=ALU.mult))(), reads=[x_], writes=[junk2])
            S.do("dve", lambda e: e.tensor_reduce(out=ssd.ap, in_=junk2.ap, axis=AX.X, op=ALU.add), reads=[junk2], writes=[ssd])
            S.do("dve", lambda e: e.tensor_scalar(out=rsd.ap, in0=ssd.ap, scalar1=1.0 / D, scalar2=EPS, op0=ALU.mult, op1=ALU.add), reads=[ssd], writes=[rsd])

        def d_norm_b(i):
            x_ = xd[i % 3]
            rsqrt_inplace(rsd, lambda: rsd.ap)
            ws = []
            prev = list(h2.r) + list(h2.w)
            for j in range(2):
                ws.append(S.do("dve", (lambda j=j, x_=x_: lambda e: e.scalar_tensor_tensor(out=h2.ap[:, j, :], in0=x_.ap[:, j, :], scalar=rsd.ap[:, j:j + 1], in1=g2b.ap, op0=ALU.mult, op1=ALU.mult))(),
                               reads=[x_, rsd, g2b], extra=prev))
            h2.w = ws
            h2.r = []

        def d_tr(i):
            ws = []
            prev = list(h2T.r) + list(h2T.w)
            for c in range(8):
                pb = pbank[c % 2]
                pv = pb.ap.bitcast(BF16)

                def f(e, c=c, pv=pv):
                    last = None
                    for j in range(2):
                        last = e.transpose(pv[:, j * 128:(j + 1) * 128], h2.ap[:, j, c * 128:(c + 1) * 128], ident.ap)
                    return last
                S.do("pe", f, reads=[h2, ident], writes=[pb])
                if c % 2 == 0:
                    ws.append(S.do("act", (lambda c=c, pv=pv: lambda e: e.copy(out=h2T.ap[:, c, :], in_=pv[:, 0:256]))(), reads=[pb], extra=prev))
                else:
                    ws.append(S.do("dve", (lambda c=c, pv=pv: lambda e: e.tensor_copy(out=h2T.ap[:, c, :], in_=pv[:, 0:256]))(), reads=[pb], extra=prev))
            h2T.w = ws
            h2T.r = []

        aT_state = {"ws": [], "prev": []}

        def d_ffn1(i, fbs):
            if fbs[0] == 0:
                aT_state["ws"] = []
                aT_state["prev"] = list(aT.r) + list(aT.w)
            for fb in fbs:
                pb = pbank[2 + fb % 4]
                r_ = rr[fb % 2]

                def f(e, fb=fb, pb=pb):
                    last = None
                    for c in range(8):
                        last = e.matmul(pb.ap[:, 0:256], lhsT=w1.ap[:, c, fb * 128:(fb + 1) * 128], rhs=h2T.ap[:, c, :], start=(c == 0), stop=(c == 7))
                    return last
                S.do("pe", f, reads=[w1, h2T], writes=[pb])
                S.do("act", (lambda pb=pb, r_=r_: lambda e: e.activation(out=r_.ap, in_=pb.ap[:, 0:256], func=AF.Relu))(), reads=[pb], writes=[r_])
                aT_state["ws"].append(S.do("pool" if fb % 2 == 0 else "dve", (lambda fb=fb, r_=r_: lambda e: e.tensor_tensor(out=aT.ap[:, fb, :], in0=r_.ap, in1=r_.ap, op=ALU.mult))(), reads=[r_], extra=aT_state["prev"]))
            if fbs[-1] == 31:
                aT.w = aT_state["ws"]
                aT.r = []

        ffn2_state = {"ws": []}

        def d_ffn2(i, js, finish):
            sn, t = tilesD[i]
            x_ = xd[i % 3]
            if js[0] == 0:
                ffn2_state["ws"] = []
            ws = ffn2_state["ws"]
            for j in js:
                for hf in range(2):
                    pb = pbank[6 + hf]

                    def f(e, j=j, hf=hf, pb=pb):
                        last = None
                        for c in range(32):
                            last = e.matmul(pb.ap, lhsT=aT.ap[:, c, j * 128:(j + 1) * 128], rhs=w2.ap[:, c, hf * 512:(hf + 1) * 512], start=(c == 0), stop=(c == 31))
                        return last
                    S.do("pe", f, reads=[aT, w2], writes=[pb])
                    ws.append(S.do("dve", (lambda j=j, hf=hf, pb=pb, x_=x_: lambda e: e.tensor_tensor(out=x_.ap[:, j, hf * 512:(hf + 1) * 512], in0=pb.ap, in1=x_.ap[:, j, hf * 512:(hf + 1) * 512], op=ALU.add))(), reads=[pb, x_]))
            if not finish:
                return
            x_.w = x_.w + ws
            if (not last_layer) or debug:
                S.do("sp", dma(scr[sn]["X2"][t * 256:(t + 1) * 256, :].rearrange("(j p) d -> p j d", p=128), x_.ap), reads=[x_], semkey=x_.key + "s")
            if last_layer:
                rs_ = rsf[i % 2]
                S.do("pool", (lambda x_=x_: lambda e: e.tensor_tensor(out=junk2.ap, in0=x_.ap, in1=x_.ap, op=ALU.mult))(), reads=[x_], writes=[junk2])
                S.do("dve", lambda e: e.tensor_reduce(out=ssf.ap, in_=junk2.ap, axis=AX.X, op=ALU.add), reads=[junk2], writes=[ssf])
                S.do("dve", (lambda rs_=rs_: lambda e: e.tensor_scalar(out=rs_.ap, in0=ssf.ap, scalar1=1.0 / D, scalar2=EPS, op0=ALU.mult, op1=ALU.add))(), reads=[ssf], writes=[rs_])

        def d_final_b(i):
            sn, t = tilesD[i]
            x_ = xd[i % 3]
            rs_ = rsf[i % 2]
            rsqrt_inplace(rs_, (lambda rs_=rs_: lambda: rs_.ap)())
            ws = []
            for j in range(2):
                ws.append(S.do("dve", (lambda j=j, x_=x_, rs_=rs_: lambda e: e.scalar_tensor_tensor(out=x_.ap[:, j, :], in0=x_.ap[:, j, :], scalar=rs_.ap[:, j:j + 1], in1=gfb.ap, op0=ALU.mult, op1=ALU.mult))(),
                               reads=[x_, rs_, gfb]))
            x_.w = x_.w + ws
            S.do("sp", dma(y_out[sn][t * 256:(t + 1) * 256, :].rearrange("(j p) d -> p j d", p=128), x_.ap), reads=[x_], semkey="yo")

        nTD = len(tilesD)
        d_load(0)
        d_norm_a(0)
        d_norm_b(0)
        d_tr(0)
        for i in range(nTD):
            nxt = i + 1 < nTD
            if nxt:
                d_load(i + 1)
            d_ffn1(i, list(range(0, 16)))
            if nxt:
                d_norm_a(i + 1)
            d_ffn1(i, list(range(16, 32)))
            if nxt:
                d_norm_b(i + 1)
            if last_layer and i > 0:
                d_final_b(i - 1)
            d_ffn2(i, [0], False)
            if nxt:
                d_tr(i + 1)
            d_ffn2(i, [1], True)
        if last_layer:
            d_final_b(nTD - 1)
        S.barrier()

    keys = S.finalize()
    assert len(keys) <= 100, len(keys)
    sems = {k: nc.alloc_semaphore(name="s_" + "_".join(map(str, k))) for k in keys}
    with nc.Block() as block:
        @block.tensor
        def _(e):
            S.emit("pe", e, sems)

        @block.scalar
        def _(e):
            S.emit("act", e, sems)

        @block.vector
        def _(e):
            S.emit("dve", e, sems)

        @block.gpsimd
        def _(e):
            S.emit("pool", e, sems)

        @block.sync
        def _(e):
            S.emit("sp", e, sems)
    return nc, {e: len(S.q[e]) for e in S.ENGS}, len(keys)


def _consts():
    import jax
    import jax.numpy as jnp
    bf = ml_dtypes.bfloat16
    ident = np.eye(128, dtype=np.float32).astype(bf)
    jmat = np.eye(128, dtype=np.float32)[::-1].copy().astype(bf)
    with jax.default_device(jax.devices("cpu")[0]):
        n = 8192
        row = jnp.repeat(jnp.arange(n // 64, dtype=jnp.int32), 64).astype(jnp.float32)
        col = jnp.tile(jnp.arange(64, dtype=jnp.int32), n // 64).astype(jnp.float32)
        inv = 10000.0 ** (-jnp.arange(0, 32, 2, dtype=jnp.float32) / 32)
        ar = row[:, None] * inv[None, :]
        ac = col[:, None] * inv[None, :]
        cr, sr, cc, sc_ = (np.asarray(jnp.cos(ar)), np.asarray(jnp.sin(ar)), np.asarray(jnp.cos(ac)), np.asarray(jnp.sin(ac)))
        C = np.concatenate([cr, cr, cc, cc], axis=1)
        Sg = np.concatenate([-sr, sr, -sc_, sc_], axis=1)
        rope = np.ascontiguousarray(np.concatenate([C, Sg], axis=1).astype(np.float32))
        rel = 639 - jnp.arange(1280, dtype=jnp.int32)
        nb = 16
        max_exact = 8
        ret = (rel > 0).astype(jnp.int32) * nb
        na = jnp.abs(rel)
        nf = jnp.maximum(na, 1).astype(jnp.float32)
        large = max_exact + (jnp.log(nf / max_exact) / math.log(128 / max_exact) * (nb - max_exact)).astype(jnp.int32)
        large = jnp.minimum(large, nb - 1)
        bucket = np.asarray(ret + jnp.where(na < max_exact, na, large))
    oh = np.zeros((32, 1280), np.float32)
    oh[bucket, np.arange(1280)] = 1.0
    return ident, jmat, rope, oh


_CACHE = {}


def _run(inputs, NPR, NSA, debug=False, nlayers=NL):
    key = (NPR, NSA, debug, nlayers)
    if key not in _CACHE:
        _CACHE[key] = build(NPR, NSA, debug, nlayers)
    nc, nops, nsem = _CACHE[key]
    ident, jmat, rope, oh = _consts()
    f = lambda a: np.ascontiguousarray(np.asarray(a, dtype=np.float32))
    shared = {
        "w_in": f(inputs["w_in"]).reshape(NL * D, INC),
        "w_up_a": f(inputs["w_up_a"]).reshape(NL * 512, D),
        "w_up_b": f(inputs["w_up_b"]).reshape(NL * 512, D),
        "w_o": f(inputs["w_o"]).reshape(NL * D, D),
        "w_ff1": f(inputs["w_ff1"]).reshape(NL * D, DFF),
        "w_ff2": f(inputs["w_ff2"]).reshape(NL * DFF, D),
        "norm1": f(inputs["norm1"]), "norm2": f(inputs["norm2"]), "norm_f": f(inputs["norm_f"]),
        "b_gate": np.ascontiguousarray(f(inputs["b_gate"]).reshape(NL, 16, 128).transpose(0, 2, 1)),
        "lamv": np.ascontiguousarray(np.stack([f(inputs["lam_q1"]), f(inputs["lam_k1"]), f(inputs["lam_q2"]), f(inputs["lam_k2"])], axis=1)),
        "subln_g": f(inputs["subln_g"]), "qk_norm_q": f(inputs["qk_norm_q"]), "qk_norm_k": f(inputs["qk_norm_k"]),
        "t5_table": f(inputs["t5_table"]),
        "ident": ident, "jmat": jmat, "rope": rope, "oh": oh,
    }
    xp = f(inputs["x_prompt"])
    xs = f(inputs["x_sample"])
    in_maps = []
    for c in range(8):
        m = dict(shared)
        m["xp"] = xp[c]
        m["xs"] = xs[c]
        in_maps.append(m)
    res = run_bass_kernel_spmd(nc, in_maps, core_ids=list(range(8)))
    return res


def kernel(**inputs):
    xp = np.asarray(inputs["x_prompt"])
    xs = np.asarray(inputs["x_sample"])
    res = _run(inputs, xp.shape[1], xs.shape[1])
    yp = np.stack([np.asarray(r["yp"], dtype=np.float32) for r in res.results], axis=0)
    ys = np.stack([np.asarray(r["ys"], dtype=np.float32) for r in res.results], axis=0)
    return (yp, ys)
```
